# Optimizing a Trainium2 kernel written in Bass

```python
import math
import jax, jax.numpy as jnp
from jax import lax
import numpy as np

D_MODEL = 1024
BATCH = 4
SEQ = 8192
DEPTH = 1

ROPE_THETA = 500000.0
EPS = 1e-6
Q_BLOCK = 128
D_FF = 2816

MLA_HEADS = 8
MLA_Q_RANK = 256
MLA_KV_RANK = 128
MLA_NOPE = 64
MLA_ROPE = 32
MLA_V = 64
MLA_WIDTH = MLA_HEADS * MLA_V

DIFF_HEADS = 4
DIFF_HEAD_DIM = 64
DIFF_ROT = DIFF_HEAD_DIM // 4
DIFF_WIDTH = DIFF_HEADS * 2 * DIFF_HEAD_DIM

IN_SPLITS = (MLA_Q_RANK, MLA_KV_RANK, MLA_ROPE, DIFF_WIDTH, DIFF_WIDTH, DIFF_WIDTH, D_MODEL, D_MODEL)
IN_COLS = sum(IN_SPLITS)

kernel_name = "hybrid_mla_diffattn_gated_macaron"


def rms_norm(x, w):
    xf = x.astype(jnp.float32)
    y = xf * lax.rsqrt(jnp.mean(xf * xf, axis=-1, keepdims=True) + EPS)
    return (y * w.astype(jnp.float32)).astype(x.dtype)


def swiglu(x, w_gate, w_up, w_down):
    return (jax.nn.silu(x @ w_gate) * (x @ w_up)) @ w_down


def apply_rope(x, positions, rot_dim):
    half = rot_dim // 2
    inv_freq = jnp.power(jnp.float32(ROPE_THETA), -2.0 * jnp.arange(half, dtype=jnp.float32) / rot_dim)
    ang = positions.astype(jnp.float32)[..., None] * inv_freq
    cos = jnp.cos(ang)[:, :, None, :]
    sin = jnp.sin(ang)[:, :, None, :]
    xr = x[..., :rot_dim].astype(jnp.float32)
    x1, x2 = xr[..., :half], xr[..., half:]
    rot = jnp.concatenate([x1 * cos - x2 * sin, x2 * cos + x1 * sin], axis=-1).astype(x.dtype)
    return jnp.concatenate([rot, x[..., rot_dim:]], axis=-1)


def to_blocks(t):
    b, s, h, d = t.shape
    return t.reshape(b, s // Q_BLOCK, Q_BLOCK, h, d).transpose(1, 0, 3, 2, 4)


def from_blocks(t):
    nb, b, h, qb, d = t.shape
    return t.transpose(1, 0, 3, 2, 4).reshape(b, nb * qb, h, d)


def causal_probs(q_blk, k, blk_idx, scale):
    s = jnp.einsum('bhqd,bhkd->bhqk', q_blk, k).astype(jnp.float32) * scale
    q_pos = blk_idx * Q_BLOCK + jnp.arange(Q_BLOCK)
    k_pos = jnp.arange(k.shape[2])
    mask = k_pos[None, :] <= q_pos[:, None]
    s = jnp.where(mask, s, -jnp.inf)
    return jax.nn.softmax(s, axis=-1)


def mla_branch(c_q, c_kv, k_rope_raw, positions, q_norm, w_uq, kv_norm, w_ukv):
    b, s, _ = c_q.shape
    q = (rms_norm(c_q, q_norm) @ w_uq).reshape(b, s, MLA_HEADS, MLA_NOPE + MLA_ROPE)
    q = jnp.concatenate([q[..., :MLA_NOPE], apply_rope(q[..., MLA_NOPE:], positions, MLA_ROPE)], axis=-1)
    kv = (rms_norm(c_kv, kv_norm) @ w_ukv).reshape(b, s, MLA_HEADS, MLA_NOPE + MLA_V)
    k_nope, v = kv[..., :MLA_NOPE], kv[..., MLA_NOPE:]
    k_rope = apply_rope(k_rope_raw.reshape(b, s, 1, MLA_ROPE), positions, MLA_ROPE)
    k = jnp.concatenate([k_nope, jnp.broadcast_to(k_rope, (b, s, MLA_HEADS, MLA_ROPE))], axis=-1)
    kt = k.transpose(0, 2, 1, 3)
    vt = v.transpose(0, 2, 1, 3)
    scale = (MLA_NOPE + MLA_ROPE) ** -0.5

    def block(args):
        qb, i = args
        p = causal_probs(qb, kt, i, scale)
        return jnp.einsum('bhqk,bhkd->bhqd', p.astype(vt.dtype), vt)

    out = lax.map(block, (to_blocks(q), jnp.arange(s // Q_BLOCK)))
    return from_blocks(out).reshape(b, s, MLA_WIDTH)


def diff_branch(q, k, v, positions, lam_q1, lam_k1, lam_q2, lam_k2, subln, layer_idx):
    b, s, _ = q.shape
    hd = DIFF_HEAD_DIM
    q = apply_rope(q.reshape(b, s, 2 * DIFF_HEADS, hd), positions, DIFF_ROT).reshape(b, s, DIFF_HEADS, 2, hd)
    k = apply_rope(k.reshape(b, s, 2 * DIFF_HEADS, hd), positions, DIFF_ROT).reshape(b, s, DIFF_HEADS, 2, hd)
    v = v.reshape(b, s, DIFF_HEADS, 2 * hd)
    q1, q2 = q[..., 0, :], q[..., 1, :]
    k1t = k[..., 0, :].transpose(0, 2, 1, 3)
    k2t = k[..., 1, :].transpose(0, 2, 1, 3)
    vt = v.transpose(0, 2, 1, 3)
    lam_init = 0.8 - 0.6 * math.exp(-0.3 * (layer_idx - 1))
    f32 = jnp.float32
    lam = (jnp.exp(jnp.sum(lam_q1.astype(f32) * lam_k1.astype(f32)))
           - jnp.exp(jnp.sum(lam_q2.astype(f32) * lam_k2.astype(f32))) + lam_init)
    scale = hd ** -0.5

    def block(args):
        q1b, q2b, i = args
        a = causal_probs(q1b, k1t, i, scale) - lam * causal_probs(q2b, k2t, i, scale)
        return jnp.einsum('bhqk,bhkd->bhqd', a.astype(vt.dtype), vt)

    out = from_blocks(lax.map(block, (to_blocks(q1), to_blocks(q2), jnp.arange(s // Q_BLOCK))))
    out = rms_norm(out, subln) * (1.0 - lam_init)
    return out.reshape(b, s, DIFF_WIDTH)


def hybrid_layer(x, positions, layer_idx,
                 ffn1_norm, ffn1_w_gate, ffn1_w_up, ffn1_w_down,
                 mix_norm, w_in,
                 mla_q_norm, mla_w_uq, mla_kv_norm, mla_w_ukv,
                 diff_lam_q1, diff_lam_k1, diff_lam_q2, diff_lam_k2, diff_subln,
                 w_proj_mla, w_proj_diff, w_out,
                 ffn2_norm, ffn2_w_gate, ffn2_w_up, ffn2_w_down):
    x = x + 0.5 * swiglu(rms_norm(x, ffn1_norm), ffn1_w_gate, ffn1_w_up, ffn1_w_down)

    h = rms_norm(x, mix_norm)
    z = h @ w_in
    cuts = []
    acc = 0
    for n in IN_SPLITS[:-1]:
        acc += n
        cuts.append(acc)
    c_q, c_kv, k_rope_raw, dq, dk, dv, gate_mla, gate_diff = jnp.split(z, cuts, axis=-1)

    y_mla = mla_branch(c_q, c_kv, k_rope_raw, positions, mla_q_norm, mla_w_uq, mla_kv_norm, mla_w_ukv)
    y_diff = diff_branch(dq, dk, dv, positions, diff_lam_q1, diff_lam_k1, diff_lam_q2, diff_lam_k2,
                         diff_subln, layer_idx)

    merged = (jax.nn.sigmoid(gate_mla) * (y_mla @ w_proj_mla)
              + jax.nn.sigmoid(gate_diff) * (y_diff @ w_proj_diff))
    x = x + merged @ w_out

    x = x + 0.5 * swiglu(rms_norm(x, ffn2_norm), ffn2_w_gate, ffn2_w_up, ffn2_w_down)
    return x


def setup_inputs(seed: int = 0) -> dict:
    key = jax.random.key(seed)
    ks = jax.random.split(key, 32)
    f32 = jnp.float32

    def w(k, fan_in, fan_out):
        return jax.random.normal(k, (DEPTH, fan_in, fan_out), f32) * fan_in ** -0.5

    def gain(k, n):
        return 1.0 + 0.01 * jax.random.normal(k, (DEPTH, n), f32)

    x = jax.random.normal(ks[0], (BATCH, SEQ, D_MODEL), f32)
    offsets = jax.random.randint(ks[1], (BATCH, 1), 0, 1024, dtype=jnp.int32)
    positions = jnp.arange(SEQ, dtype=jnp.int32)[None, :] + offsets
    return {
        "x": x,
        "positions": positions,
        "ffn1_norm": gain(ks[2], D_MODEL),
        "ffn1_w_gate": w(ks[3], D_MODEL, D_FF),
        "ffn1_w_up": w(ks[4], D_MODEL, D_FF),
        "ffn1_w_down": w(ks[5], D_FF, D_MODEL),
        "mix_norm": gain(ks[6], D_MODEL),
        "w_in": w(ks[7], D_MODEL, IN_COLS),
        "mla_q_norm": gain(ks[8], MLA_Q_RANK),
        "mla_w_uq": w(ks[9], MLA_Q_RANK, MLA_HEADS * (MLA_NOPE + MLA_ROPE)),
        "mla_kv_norm": gain(ks[10], MLA_KV_RANK),
        "mla_w_ukv": w(ks[11], MLA_KV_RANK, MLA_HEADS * (MLA_NOPE + MLA_V)),
        "diff_lam_q1": 0.1 * jax.random.normal(ks[12], (DEPTH, DIFF_HEAD_DIM), f32),
        "diff_lam_k1": 0.1 * jax.random.normal(ks[13], (DEPTH, DIFF_HEAD_DIM), f32),
        "diff_lam_q2": 0.1 * jax.random.normal(ks[14], (DEPTH, DIFF_HEAD_DIM), f32),
        "diff_lam_k2": 0.1 * jax.random.normal(ks[15], (DEPTH, DIFF_HEAD_DIM), f32),
        "diff_subln": gain(ks[16], 2 * DIFF_HEAD_DIM),
        "w_proj_mla": w(ks[17], MLA_WIDTH, D_MODEL),
        "w_proj_diff": w(ks[18], DIFF_WIDTH, D_MODEL),
        "w_out": w(ks[19], D_MODEL, D_MODEL),
        "ffn2_norm": gain(ks[20], D_MODEL),
        "ffn2_w_gate": w(ks[21], D_MODEL, D_FF),
        "ffn2_w_up": w(ks[22], D_MODEL, D_FF),
        "ffn2_w_down": w(ks[23], D_FF, D_MODEL),
        "final_norm": 1.0 + 0.01 * jax.random.normal(ks[24], (D_MODEL,), f32),
    }


def reference(x, positions,
              ffn1_norm, ffn1_w_gate, ffn1_w_up, ffn1_w_down,
              mix_norm, w_in,
              mla_q_norm, mla_w_uq, mla_kv_norm, mla_w_ukv,
              diff_lam_q1, diff_lam_k1, diff_lam_q2, diff_lam_k2, diff_subln,
              w_proj_mla, w_proj_diff, w_out,
              ffn2_norm, ffn2_w_gate, ffn2_w_up, ffn2_w_down,
              final_norm):
    for l in range(DEPTH):
        x = hybrid_layer(
            x, positions, l + 1,
            ffn1_norm[l], ffn1_w_gate[l], ffn1_w_up[l], ffn1_w_down[l],
            mix_norm[l], w_in[l],
            mla_q_norm[l], mla_w_uq[l], mla_kv_norm[l], mla_w_ukv[l],
            diff_lam_q1[l], diff_lam_k1[l], diff_lam_q2[l], diff_lam_k2[l], diff_subln[l],
            w_proj_mla[l], w_proj_diff[l], w_out[l],
            ffn2_norm[l], ffn2_w_gate[l], ffn2_w_up[l], ffn2_w_down[l])
    return rms_norm(x, final_norm)
```

```python
import math
import os
import numpy as np
import concourse.bass as bass
import concourse.mybir as mybir
from concourse.bass_utils import run_bass_kernel_spmd

F32 = mybir.dt.float32
BF16 = mybir.dt.bfloat16
I32 = mybir.dt.int32
AF = mybir.ActivationFunctionType
ALU = mybir.AluOpType
AX = mybir.AxisListType

D = 1024
FF = 2816
NFF = FF // 128
T = 512
EPS = 1e-6
THETA = 500000.0
N_IN = 4000
C_CQ, C_CKV, C_KR, C_DQ, C_DK, C_DV, C_GM, C_GD = 0, 256, 384, 416, 928, 1440, 1952, 2976
LAM_INIT = 0.8 - 0.6 * math.exp(-0.3 * 0.0)
SC_M = 96.0 ** -0.5
SC_D = 64.0 ** -0.5
MAGIC = 12582912.0
OWN_A = [0, 3, 4, 7, 8, 11, 12, 15]
OWN_B = [1, 2, 5, 6, 9, 10, 13, 14]

ENGS = ("sp", "act", "dve", "pool", "pe")
KF = os.environ.get("KOPT", "1111")


def A(*a, **k):
    return (a, k)


class Reg:
    __slots__ = ("w", "r", "const", "excl")

    def __init__(self, const=False, excl=False):
        self.w = None
        self.r = []
        self.const = const
        self.excl = excl


class Sched:
    def __init__(self, nc, n_dma_sems=12):
        self.nc = nc
        self.q = {e: [] for e in ENGS}
        self.cnt = {e: 0 for e in ENGS}
        self.seen = {e: {} for e in ENGS}
        self.sem = {e: nc.alloc_semaphore("s_" + e) for e in ENGS}
        self.dma_sems = {}
        self.n_dma_sems = n_dma_sems
        self.dma_rr = {}
        self.dma_val = {}

    def _wait(self, eng, tok):
        if tok is None:
            return
        key, val = tok
        if key == eng and val > self.cnt[eng]:
            return
        if self.seen[eng].get(key, 0) >= val:
            return
        self.seen[eng][key] = val
        sem = self.sem[key] if key in self.sem else self.dma_sems[key]
        self.q[eng].append(lambda e, sem=sem, val=val: e.wait_ge(sem, val))

    def _deps(self, reads, writes, deps):
        toks = list(deps)
        for r in reads:
            if r.w is not None:
                toks.append(r.w)
            if r.excl:
                toks.extend(r.r)
        for w in writes:
            if w.w is not None:
                toks.append(w.w)
            toks.extend(w.r)
        return toks

    @staticmethod
    def _merge(toks):
        best = {}
        for t in toks:
            if t is None:
                continue
            if best.get(t[0], 0) < t[1]:
                best[t[0]] = t[1]
        return list(best.items())

    def _record(self, tok, reads, writes):
        for r in reads:
            if r.const:
                continue
            r.r.append(tok)
            if len(r.r) > 16:
                best = {}
                for k, v in r.r:
                    if best.get(k, 0) < v:
                        best[k] = v
                r.r = list(best.items())
        for w in writes:
            w.w = tok
            w.r = []

    def op(self, eng, fn, reads=(), writes=(), sig=True, deps=()):
        if isinstance(fn, tuple):
            m_, (a_, k_) = fn
            fn = lambda e, m_=m_, a_=a_, k_=k_: getattr(e, m_)(*a_, **k_)
        for t in self._merge(self._deps(reads, writes, deps)):
            self._wait(eng, t)
        if sig:
            self.cnt[eng] += 1
            sem = self.sem[eng]
            self.q[eng].append(lambda e, fn=fn, sem=sem: fn(e).then_inc(sem, 1))
            tok = (eng, self.cnt[eng])
        else:
            self.q[eng].append(lambda e, fn=fn: fn(e))
            tok = (eng, self.cnt[eng] + 1)
        self._record(tok, reads, writes)
        return tok

    def dma(self, eng, out, in_, reads=(), writes=(), deps=()):
        toks = self._deps(reads, writes, deps)
        rr = self.dma_rr.get(eng, 0)
        self.dma_rr[eng] = (rr + 1) % (6 if (eng == "pool" and KF[3] == "1") else self.n_dma_sems)
        key = "dma_%s_%d" % (eng, rr)
        if key not in self.dma_sems:
            self.dma_sems[key] = self.nc.alloc_semaphore(key)
            self.dma_val[key] = 0
        if self.dma_val[key] > 0:
            toks.append((key, self.dma_val[key]))
        for t in toks:
            self._wait(eng, t)
        self.dma_val[key] += 16
        sem = self.dma_sems[key]
        self.q[eng].append(lambda e, out=out, in_=in_, sem=sem: e.dma_start(out=out, in_=in_).then_inc(sem, 16))
        tok = (key, self.dma_val[key])
        self._record(tok, reads, writes)
        return tok

    def coll(self, kind, op, groups, in_ap, out_ap, reads=(), writes=()):
        eng = "pool"
        toks = self._deps(reads, writes, ())
        i = self.dma_rr.get("cc", 0)
        self.dma_rr["cc"] = i + 1
        key = "cc_%d" % (i % 8)
        if key not in self.dma_sems:
            self.dma_sems[key] = self.nc.alloc_semaphore(key)
            self.dma_val[key] = 0
        if self.dma_val[key] > 0:
            toks.append((key, self.dma_val[key]))
        for t in self._merge(toks):
            self._wait(eng, t)
        self.dma_val[key] += 1
        sem = self.dma_sems[key]
        self.q[eng].append(lambda e: e.collective_compute(kind, op, replica_groups=groups, ins=[in_ap], outs=[out_ap]).then_inc(sem, 1))
        tok = (key, self.dma_val[key])
        self._record(tok, reads, writes)
        return tok

    def barrier(self):
        toks = [(e, self.cnt[e]) for e in ENGS if self.cnt[e] > 0]
        toks += [(k, v) for k, v in self.dma_val.items() if v > 0]
        for e in ENGS:
            for t in toks:
                self._wait(e, t)

    def emit(self):
        q = self.q
        with self.nc.Block() as block:
            @block.sync
            def _(e):
                for f in q["sp"]:
                    f(e)

            @block.scalar
            def _(e):
                for f in q["act"]:
                    f(e)

            @block.vector
            def _(e):
                for f in q["dve"]:
                    f(e)

            @block.gpsimd
            def _(e):
                for f in q["pool"]:
                    f(e)

            @block.tensor
            def _(e):
                for f in q["pe"]:
                    f(e)


class SB:
    def __init__(self, nc):
        self.nc = nc
        self.off = 16512
        self.top = 229344
        self.n = 0

    def alloc(self, shape, dt):
        n = 1
        for s in shape[1:]:
            n *= s
        nb = n * (2 if dt == BF16 else 4)
        nb = (nb + 31) // 32 * 32
        t = self.nc.alloc_sbuf_tensor_at("t%d" % self.n, list(shape), dt, offset=self.off)
        self.n += 1
        self.off += nb
        assert self.off <= self.top, "SBUF overflow %d" % self.off
        return t


def build(S_LEN, n_cores=8):
    NT = S_LEN // T
    NO = NT // 2
    own = list(range(NO))
    nc = bass.Bass("TRN2", target_bir_lowering=False)

    def din(name, shape, dt=F32):
        return nc.dram_tensor(name, list(shape), dt, kind="ExternalInput").ap()

    def dscr(name, shape, dt):
        return nc.dram_tensor(name, list(shape), dt, kind="Internal").ap()

    xT = din("xT", [D, S_LEN])
    posr = din("posr", [128, S_LEN], I32)
    gv = din("gv", [128, 36])
    lamr = din("lamr", [128, 256])
    fsc = din("fsc", [128, 2])
    cmat = din("cmat", [128, 3, 128])
    flag = din("flag", [128, 1])
    w1g = din("w1g", [D, FF]); w1u = din("w1u", [D, FF]); w1d = din("w1d", [FF, D])
    w2g = din("w2g", [D, FF]); w2u = din("w2u", [D, FF]); w2d = din("w2d", [FF, D])
    win = din("win", [D, N_IN])
    wuq = din("wuq", [256, 768]); wukv = din("wukv", [128, 1024])
    wpm = din("wpm", [512, D]); wpd = din("wpd", [512, D]); wout = din("wout", [D, D])
    outT = nc.dram_tensor("outT", [D, NO * T], F32, kind="ExternalOutput").ap()

    x1o = dscr("x1o", [NO, D, T], F32)
    x1s = dscr("x1s", [NO, D, T], F32)
    groups = [[2 * i, 2 * i + 1] for i in range(n_cores // 2)]
    Rxo = [Reg() for _ in range(NO)]; Rxsum = [Reg() for _ in range(NO)]
    dkT = dscr("dkT", [4, 128, S_LEN], BF16)
    dvS = dscr("dvS", [S_LEN, 512], BF16)
    qTs = dscr("qTs", [8, 96, NO * T], BF16)
    dqT = dscr("dqT", [4, 128, NO * T], BF16)
    ymT = dscr("ymT", [512, NO * T], BF16)
    ydT = dscr("ydT", [512, NO * T], BF16)
    x2T = dscr("x2T", [D, NO * T], F32)

    S = Sched(nc)
    sb = SB(nc)
    psall = nc.alloc_psum_tensor("psall", [128, 8 * T], F32)
    ps = [psall[:, i * T:(i + 1) * T] for i in range(8)]

    def pspair(a):
        return psall[:, a * T:(a + 2) * T].rearrange("p (b t) -> p b t", t=T)
    Rps = [Reg(excl=True) for _ in range(8)]

    ones_bf = sb.alloc([128, 128], BF16)
    ones_f = sb.alloc([128, 64], F32)
    cm = sb.alloc([128, 3, 128], BF16)
    gvt = sb.alloc([128, 36], F32)
    fst = sb.alloc([128, 2], F32)
    lamt = sb.alloc([128, 256], F32)
    lamw = sb.alloc([128, 8], F32)
    flagt = sb.alloc([128, 1], F32)
    Rc = Reg(const=True)
    S.op("pool", ("memset", A(ones_bf[:], 1.0)), writes=[Rc])
    S.op("pool", ("memset", A(ones_f[:], 1.0)), writes=[Rc])
    S.dma("pool", cm[:], cmat, writes=[Rc])
    S.dma("sp", gvt[:], gv, writes=[Rc])
    S.dma("sp", fst[:], fsc, writes=[Rc])
    S.dma("sp", lamt[:], lamr, writes=[Rc])
    S.dma("sp", flagt[:], flag, writes=[Rc])
    Rl = Reg()
    S.op("dve", ("tensor_tensor", A(out=lamt[:, 0:64], in0=lamt[:, 0:64], in1=lamt[:, 64:128], op=ALU.mult)), reads=[Rc], writes=[Rl])
    S.op("dve", ("tensor_tensor", A(out=lamt[:, 128:192], in0=lamt[:, 128:192], in1=lamt[:, 192:256], op=ALU.mult)), reads=[Rc], writes=[Rl])
    S.op("dve", ("reduce_sum", A(out=lamw[:, 0:1], in_=lamt[:, 0:64], axis=AX.X)), reads=[Rl], writes=[Rl])
    S.op("dve", ("reduce_sum", A(out=lamw[:, 1:2], in_=lamt[:, 128:192], axis=AX.X)), reads=[Rl], writes=[Rl])
    S.op("act", ("activation", A(out=lamw[:, 2:4], in_=lamw[:, 0:2], func=AF.Exp)), reads=[Rl], writes=[Rl])
    S.op("dve", ("tensor_tensor", A(out=lamw[:, 4:5], in0=lamw[:, 3:4], in1=lamw[:, 2:3], op=ALU.subtract)), reads=[Rl], writes=[Rl])
    S.op("dve", ("tensor_scalar", A(out=lamw[:, 4:5], in0=lamw[:, 4:5], scalar1=-LAM_INIT, scalar2=None, op0=ALU.add)), reads=[Rl], writes=[Rl])
    S.op("dve", ("tensor_scalar", A(out=lamw[:, 5:6], in0=gvt[:, 35:36], scalar1=1.0 - LAM_INIT, scalar2=None, op0=ALU.mult)), reads=[Rl, Rc], writes=[Rl])
    S.op("dve", ("tensor_scalar", A(out=lamw[:, 6:7], in0=flagt[:, 0:1], scalar1=-1.0, scalar2=30000.0, op0=ALU.add, op1=ALU.mult)), reads=[Rc], writes=[Rl])
    S.barrier()
    G_FFN1, G_MIX, G_FFN2, G_FIN, G_Q, G_KV = 0, 8, 16, 24, 32, 34
    neglam = lamw[:, 4:5]
    sgain = lamw[:, 5:6]
    fbias = lamw[:, 6:7]
    base_off = sb.off

    def mm(out, lhsT, rhs, start, stop, reads, writes, sig):
        return S.op("pe", ("matmul", A(out, lhsT, rhs, start=start, stop=stop)), reads=reads, writes=writes, sig=sig)

    def rmsnorm(src, Rsrc, C, nfeat, gcol, sqbuf, Rsq, out, Rout, pbank, rstd, Rrstd):
        snap = ([Rsq.w] if Rsq.w is not None else []) + list(Rsq.r)
        Rsqc = [Reg() for _ in range(C)]
        for c in range(C):
            if c % 2 == 0:
                S.op("act", ("activation", A(out=sqbuf(c), in_=src(c), func=AF.Square)), reads=[Rsrc], writes=[Rsqc[c]], deps=snap)
            else:
                S.op("dve", ("tensor_tensor", A(out=sqbuf(c), in0=src(c), in1=src(c), op=ALU.mult)), reads=[Rsrc], writes=[Rsqc[c]], deps=snap)
        for c in range(C):
            mm(ps[pbank][:], ones_bf[:, :], sqbuf(c), c == 0, c == C - 1, [Rsqc[c], Rsq, Rc], [Rps[pbank]], c == C - 1)
        S.op("act", ("activation", A(out=rstd[:], in_=ps[pbank][:], func=AF.Ln, bias=EPS, scale=1.0 / nfeat)), reads=[Rps[pbank]], writes=[Rrstd])
        S.op("act", ("activation", A(out=rstd[:], in_=rstd[:], func=AF.Exp, scale=-0.5)), reads=[Rrstd], writes=[Rrstd])
        for c in range(C):
            S.op("dve", ("scalar_tensor_tensor", A(out=out(c), in0=src(c), scalar=gvt[:, gcol + c:gcol + c + 1], in1=rstd[:], op0=ALU.mult, op1=ALU.mult)),
                 reads=[Rsrc, Rrstd, Rc], writes=[Rout])

    def load_w(dst, src_ap, R, pieces, axis_len, dim):
        step = axis_len // pieces
        for i in range(pieces):
            sl = slice(i * step, (i + 1) * step)
            if dim == 3:
                S.dma("pool", dst[:, :, sl], src_ap[:, :, sl], writes=[R])
            else:
                S.dma("pool", dst[:, sl], src_ap[:, sl], writes=[R])

    def ffn_phase(x_in, n_tiles, wg, wu, wd, gpre, x_out, out_tiles, final_norm, after_store=None):
        sb.off = base_off
        Wg = sb.alloc([128, 8, FF], BF16); Wu = sb.alloc([128, 8, FF], BF16); Wd = sb.alloc([128, NFF, D], BF16)
        xb = [sb.alloc([128, 8, T], F32) for _ in range(2)]
        hT = sb.alloc([128, 8, T], BF16)
        aT = sb.alloc([128, NFF, T], BF16)
        rstd = sb.alloc([128, T], F32)
        sil = [sb.alloc([128, T], BF16) for _ in range(2)]
        Rx = [Reg(), Reg()]; Rh = Reg(); Ra = [Reg() for _ in range(NFF)]; Rr = Reg(); Rs = [Reg(), Reg()]

        def load_x(t):
            S.dma("sp", xb[t % 2][:], x_in(t).rearrange("(c p) t -> p c t", p=128), writes=[Rx[t % 2]])

        load_x(0)
        NPC = 4
        FPC = 6
        RWg = [Reg(True) for _ in range(NPC)]; RWu = [Reg(True) for _ in range(NPC)]; RWd = [Reg(True) for _ in range(2)]
        wg3 = wg.rearrange("(c p) f -> p c f", p=128); wu3 = wu.rearrange("(c p) f -> p c f", p=128); wd3 = wd.rearrange("(c p) f -> p c f", p=128)
        for pc in range(NPC):
            sl = slice(pc * FPC * 128, min((pc + 1) * FPC * 128, FF))
            S.dma("pool", Wg[:, :, sl], wg3[:, :, sl], writes=[RWg[pc]])
            S.dma("pool", Wu[:, :, sl], wu3[:, :, sl], writes=[RWu[pc]])
        for d2 in range(2):
            sl = slice(d2 * 512, (d2 + 1) * 512)
            S.dma("pool", Wd[:, :, sl], wd3[:, :, sl], writes=[RWd[d2]])

        def norm_pre(t):
            x = xb[t % 2]
            rmsnorm(lambda c: x[:, c, :], Rx[t % 2], 8, D, gpre, lambda c: hT[:, c, :], Rh, lambda c: hT[:, c, :], Rh, 0, rstd, Rr)

        if KF[0] == "1":
            norm_pre(0)
        for t in range(n_tiles):
            x = xb[t % 2]; Rxc = Rx[t % 2]
            if t + 1 < n_tiles:
                load_x(t + 1)
            if KF[0] != "1":
                norm_pre(t)
            for f in range(NFF):
                pg, pu = 1 + f % 2, 3 + f % 2
                for k in range(8):
                    mm(ps[pg][:], Wg[:, k, f * 128:(f + 1) * 128], hT[:, k, :], k == 0, k == 7, [RWg[f // FPC], Rh], [Rps[pg]], k == 7)
                for k in range(8):
                    mm(ps[pu][:], Wu[:, k, f * 128:(f + 1) * 128], hT[:, k, :], k == 0, k == 7, [RWu[f // FPC], Rh], [Rps[pu]], k == 7)
                S.op("act", ("activation", A(out=sil[f % 2][:], in_=ps[pg][:], func=AF.Silu)), reads=[Rps[pg]], writes=[Rs[f % 2]])
                S.op("dve", ("tensor_tensor", A(out=aT[:, f, :], in0=ps[pu][:], in1=sil[f % 2][:], op=ALU.mult)), reads=[Rps[pu], Rs[f % 2]], writes=[Ra[f]])
            for d in range(8):
                po = 5 + d % 2
                for f in range(NFF):
                    mm(ps[po][:], Wd[:, f, d * 128:(d + 1) * 128], aT[:, f, :], f == 0, f == NFF - 1, [RWd[d // 4], Ra[f]], [Rps[po]], f == NFF - 1)
                S.op("dve", ("scalar_tensor_tensor", A(out=x[:, d, :], in0=ps[po][:], scalar=0.5, in1=x[:, d, :], op0=ALU.mult, op1=ALU.add)),
                     reads=[Rps[po]], writes=[Rxc])
                if KF[0] == "1" and d == 1 and t + 1 < n_tiles:
                    norm_pre(t + 1)
            if final_norm:
                rmsnorm(lambda c: x[:, c, :], Rxc, 8, D, G_FIN, lambda c: aT[:, c, :], Ra[0], lambda c: x[:, c, :], Rxc, 0, rstd, Rr)
            S.dma("pool", x_out(t).rearrange("(c p) t -> p c t", p=128), x[:], reads=[Rxc], writes=out_tiles(t))
            if after_store is not None:
                after_store(t)
        S.barrier()


    def exchange(t):
        S.coll("AllReduce", ALU.add, groups, x1o[t], x1s[t], reads=[Rxo[t]], writes=[Rxsum[t]])

    ffn_phase(lambda t: xT[:, t * T:(t + 1) * T], NO, w1g, w1u, w1d, G_FFN1, lambda t: x1o[t], lambda t: [Rxo[t]], False, after_store=exchange)

    sb.off = base_off
    kvnT = sb.alloc([128, S_LEN], BF16)
    KT2 = sb.alloc([128, 2, S_LEN], BF16)
    bc_off = sb.off
    Rkvn = Reg(); Rkt = [Reg(), Reg()]; Rkr = Reg(); Rkz = Reg()
    S.op("pool", ("memset", A(KT2[96:128, :, :], 0.0)), writes=[Rkz])
    Win = sb.alloc([128, 8, C_GM], BF16)
    Wuq = sb.alloc([128, 2, 768], BF16)
    RWin, RWuq = Reg(True), Reg(True)
    load_w(Win, win.rearrange("(c p) f -> p c f", p=128)[:, :, 0:C_GM], RWin, 2, C_GM, 3)
    load_w(Wuq, wuq.rearrange("(c p) f -> p c f", p=128), RWuq, 1, 768, 3)
    xb = [sb.alloc([128, 8, T], F32) for _ in range(2)]
    posi = [sb.alloc([128, T], I32) for _ in range(2)]
    h2 = sb.alloc([128, 8, T], BF16)
    rstd = sb.alloc([128, T], F32)
    posf = sb.alloc([128, T], F32)
    trgb = [sb.alloc([128, 4, T], F32) for _ in range(2)]
    ta = sb.alloc([128, T], F32); tb = sb.alloc([128, T], F32)
    ckv = sb.alloc([128, 2, T], F32)
    sqb = sb.alloc([128, 2, T], BF16)
    cqn = sb.alloc([128, 2, T], BF16)
    xs = [sb.alloc([128, T], BF16) for _ in range(2)]
    t1 = [sb.alloc([128, T], F32) for _ in range(2)]
    t2 = [sb.alloc([128, T], F32) for _ in range(2)]
    dk_st = sb.alloc([128, 4, T], BF16)
    stg_off = sb.off
    xo_full = sb.alloc([128, 8, T], F32)
    q_st = nc.alloc_sbuf_tensor_at("q_st_alias", [96, 8, T], BF16, offset=stg_off)
    dq_st = nc.alloc_sbuf_tensor_at("dq_st_alias", [128, 4, T], BF16, offset=stg_off + 8 * T * 2)
    dv_st = sb.alloc([128, 4, 512], BF16)
    Rx = [Reg(), Reg()]; Rpi = [Reg(), Reg()]; Rh2 = Reg(); Rr = Reg(); Rpf = Reg(); Rtrgb = [Reg(), Reg()]; Rta = Reg(); Rtb = Reg()
    Rckv = Reg(); Rsqb = Reg(); Rcqn = Reg(); Rxs = [Reg(), Reg()]; Rt1 = [Reg(), Reg()]; Rt2 = [Reg(), Reg()]
    Rdk = Reg(); Rdq = Reg(); Rq = Reg(); Rdv = Reg()
    rope_i = [0]
    cur = {}
    h2b = [h2, sb.alloc([128, 8, T], BF16)]
    Rh2b = [Rh2, Reg()]
    pend_rope = []

    def flush_rope():
        while pend_rope:
            pend_rope.pop(0)()

    def rope(pbank, R, tab, outs):
        i = rope_i[0] % 2
        rope_i[0] += 1
        pw = 4 + i
        S.op("act", ("activation", A(out=xs[i][0:R, :], in_=ps[pbank][0:R, :], func=AF.Identity)), reads=[Rps[pbank]], writes=[Rxs[i]])
        r0 = min(o[2] for o in outs); r1 = max(o[3] for o in outs)
        S.op("dve", ("tensor_tensor", A(out=t1[i][r0:r1, :], in0=ps[pbank][r0:r1, :], in1=cur['trg'][r0:r1, 2 * tab, :], op=ALU.mult)), reads=[Rps[pbank], cur['Rtrg']], writes=[Rt1[i]])

        trg_, Rtrg_ = cur['trg'], cur['Rtrg']

        def stage_b():
            mm(ps[pw][0:R, :], cm[0:R, 1 + tab, 0:R], xs[i][0:R, :], True, True, [Rxs[i], Rc], [Rps[pw]], True)
            S.op("dve", ("tensor_tensor", A(out=t2[i][r0:r1, :], in0=ps[pw][r0:r1, :], in1=trg_[r0:r1, 2 * tab + 1, :], op=ALU.mult)), reads=[Rps[pw], Rtrg_], writes=[Rt2[i]])
            for (apf, Ro, a, b) in outs:
                S.op("pool", ("tensor_tensor", A(out=apf, in0=t1[i][a:b, :], in1=t2[i][a:b, :], op=ALU.add)), reads=[Rt1[i], Rt2[i]], writes=[Ro])
        if KF[1] == "1":
            pend_rope.append(stage_b)
        else:
            stage_b()

    Rxoh = Reg()
    pend_sub = []

    def load_b(t):
        if t < NO:
            S.dma("sp", xb[t % 2][:], x1o[t].rearrange("(c p) t -> p c t", p=128), writes=[Rx[t % 2]])
        else:
            j = t - NO
            S.dma("sp", xb[t % 2][:], x1s[j].rearrange("(c p) t -> p c t", p=128), reads=[Rxsum[j]], writes=[Rx[t % 2]])
            S.dma("sp", xo_full[:], x1o[j].rearrange("(c p) t -> p c t", p=128), writes=[Rxoh, Rq, Rdq])

            def sub(t=t):
                S.op("dve", ("tensor_tensor", A(out=xb[t % 2][:], in0=xb[t % 2][:], in1=xo_full[:], op=ALU.subtract)), reads=[Rxoh], writes=[Rx[t % 2]])
            pend_sub.append(sub)
        S.dma("sp", posi[t % 2][:], posr[:, t * T:(t + 1) * T], writes=[Rpi[t % 2]])

    load_b(0)
    pp_i = [0]

    def proj(cols, M, srcs, W, RW, nk):
        pb = 1 + pp_i[0] % 3
        pp_i[0] += 1
        for k in range(nk):
            mm(ps[pb][0:M, :], W[:, k, cols], srcs[0](k), k == 0, k == nk - 1, [RW, srcs[1]], [Rps[pb]], k == nk - 1)
        flush_rope()
        return pb

    def norm_b(t):
        x = xb[t % 2]
        hh = h2b[t % 2]
        rmsnorm(lambda c: x[:, c, :], Rx[t % 2], 8, D, G_MIX, lambda c: hh[:, c, :], Rh2b[t % 2], lambda c: hh[:, c, :], Rh2b[t % 2], 0, rstd, Rr)

    def trig_b(t):
        trg, Rtrg = trgb[t % 2], Rtrgb[t % 2]
        S.op("dve", ("tensor_copy", A(out=posf[:], in_=posi[t % 2][:])), reads=[Rpi[t % 2]], writes=[Rpf])
        for tab in range(2):
            for cs in range(2):
                S.op("dve", ("tensor_scalar", A(out=ta[:], in0=posf[:], scalar1=fst[:, tab:tab + 1], scalar2=(0.25 if cs == 0 else 0.0), op0=ALU.mult, op1=ALU.add)),
                     reads=[Rpf, Rc], writes=[Rta])
                S.op("dve", ("tensor_scalar", A(out=tb[:], in0=ta[:], scalar1=MAGIC, scalar2=None, op0=ALU.add)), reads=[Rta], writes=[Rtb])
                S.op("dve", ("scalar_tensor_tensor", A(out=ta[:], in0=tb[:], scalar=MAGIC, in1=ta[:], op0=ALU.subtract, op1=ALU.subtract)), reads=[Rtb, Rta], writes=[Rta])
                S.op("act", ("activation", A(out=trg[:, 2 * tab + cs, :], in_=ta[:], func=AF.Sin, scale=-2.0 * math.pi * 0.999999)), reads=[Rta], writes=[Rtrg])

    norm_b(0)
    trig_b(0)
    for t in range(NT):
        if t + 1 < NT:
            load_b(t + 1)
        cols_t = slice(t * T, (t + 1) * T)
        hh = h2b[t % 2]; Rhh = Rh2b[t % 2]
        cur['trg'] = trgb[t % 2]; cur['Rtrg'] = Rtrgb[t % 2]
        hsrc = (lambda k: hh[:, k, :], Rhh)
        is_own = t in own
        pb = proj(slice(C_CKV, C_CKV + 128), 128, hsrc, Win, RWin, 8)
        S.op("act", ("activation", A(out=ckv[:, 0, :], in_=ps[pb][:], func=AF.Identity)), reads=[Rps[pb]], writes=[Rckv])
        rmsnorm(lambda c: ckv[:, c, :], Rckv, 1, 128, G_KV, lambda c: sqb[:, c, :], Rsqb, lambda c: kvnT[:, cols_t], Rkvn, 7, rstd, Rr)
        if is_own:
            qi = own.index(t)
            cols_q = slice(qi * T, (qi + 1) * T)
            for c in range(2):
                pb = proj(slice(C_CQ + 128 * c, C_CQ + 128 * (c + 1)), 128, hsrc, Win, RWin, 8)
                S.op("act", ("activation", A(out=ckv[:, c, :], in_=ps[pb][:], func=AF.Identity)), reads=[Rps[pb]], writes=[Rckv])
            rmsnorm(lambda c: ckv[:, c, :], Rckv, 2, 256, G_Q, lambda c: sqb[:, c, :], Rsqb, lambda c: cqn[:, c, :], Rcqn, 7, rstd, Rr)
        pb = proj(slice(C_KR - 64, C_KR + 32), 96, hsrc, Win, RWin, 8)
        rope(pb, 96, 0, [(KT2[64:96, 0, cols_t], Rkr, 64, 96), (KT2[64:96, 1, cols_t], Rkr, 64, 96)])
        for j in range(4):
            pb = proj(slice(C_DK + 128 * j, C_DK + 128 * (j + 1)), 128, hsrc, Win, RWin, 8)
            rope(pb, 128, 1, [(dk_st[:, j, :], Rdk, 0, 128)])
        if t + 1 < NT:
            while pend_sub:
                pend_sub.pop(0)()
            norm_b(t + 1)
            trig_b(t + 1)
        if is_own:
            for j in range(4):
                pb = proj(slice(C_DQ + 128 * j, C_DQ + 128 * (j + 1)), 128, hsrc, Win, RWin, 8)
                rope(pb, 128, 1, [(dq_st[:, j, :], Rdq, 0, 128)])
            for h in range(8):
                pb = proj(slice(96 * h, 96 * (h + 1)), 96, (lambda k: cqn[:, k, :], Rcqn), Wuq, RWuq, 2)
                rope(pb, 96, 0, [(q_st[0:96, h, :], Rq, 0, 96)])
        for sub in range(4):
            pv = 6
            for k in range(8):
                mm(ps[pv][:], hh[:, k, sub * 128:(sub + 1) * 128], Win[:, k, C_DV:C_DV + 512], k == 0, k == 7, [Rhh, RWin], [Rps[pv]], k == 7)
            if sub == 0:
                flush_rope()
            S.op("act", ("activation", A(out=dv_st[:, sub, :], in_=ps[pv][:], func=AF.Identity)), reads=[Rps[pv]], writes=[Rdv])
        S.dma("pool", dvS[t * T:(t + 1) * T, :].rearrange("(s p) c -> p s c", p=128), dv_st[:], reads=[Rdv])
        flush_rope()
        S.dma("pool", dkT[:, :, cols_t].rearrange("j p t -> p j t"), dk_st[:], reads=[Rdk])
        if is_own:
            S.dma("pool", dqT[:, :, cols_q].rearrange("j p t -> p j t"), dq_st[:], reads=[Rdq])
            S.dma("pool", qTs[:, :, cols_q].rearrange("h r t -> r h t"), q_st[:], reads=[Rq])
    S.barrier()

    sb.off = bc_off
    WkK = sb.alloc([128, 8, 64], BF16); WkV = sb.alloc([128, 8, 64], BF16)
    RWk = Reg(True)
    wukv4 = wukv.rearrange("k (h two d) -> k h two d", two=2, d=64)
    S.dma("pool", WkK[:], wukv4[:, :, 0, :], writes=[RWk])
    S.dma("pool", WkV[:], wukv4[:, :, 1, :], writes=[RWk])
    NKT = S_LEN // 128
    c_off = sb.off
    Vh = [sb.alloc([128, NKT, 128], BF16) for _ in range(2)]
    QT = [sb.alloc([128, NO * T], BF16) for _ in range(2)]
    PT = [sb.alloc([128, T], BF16) for _ in range(4)]
    rc = sb.alloc([128, T], F32)
    bcs = sb.alloc([64, T], F32)
    yst = [sb.alloc([64, T], BF16) for _ in range(2)]
    RV = [Reg(), Reg()]; RQ = [Reg(), Reg()]; RPT = [Reg() for _ in range(4)]; Rrc = Reg(); Rbcs = Reg(); Ryst = [Reg(), Reg()]
    RQz = Reg()
    for b_ in range(2):
        S.op("pool", ("memset", A(Vh[b_][:], 1.0)), writes=[RV[b_]])
        S.op("pool", ("memset", A(QT[b_][96:128, :], 0.0)), writes=[RQz])

    def key_tiles(i):
        L = []
        for j in range(i):
            for s_ in range(4):
                L.append((j * 4 + s_, 0, 0))
            for s_ in range(4):
                L.append(((NO + j) * 4 + s_, 0, 0))
        for s_ in range(4):
            L.append((i * 4 + s_, 1, s_))
        for s_ in range(4):
            L.append(((NO + i) * 4 + s_, 2, 0))
        return L

    blocks = []
    for h in range(8):
        for qi in range(NO):
            L = key_tiles(qi)
            for n_, (kt, kind, sub) in enumerate(L):
                blocks.append((h, qi, kt, kind, sub, n_ == 0, n_ == len(L) - 1))
    LA = 2
    pending = []
    fin_i = [0]

    def mla_prologue_steps(h):
        b = h % 2
        steps = []

        def kstep(g):
            mm(ps[7][0:64, :], WkK[:, h, :], kvnT[:, g * T:(g + 1) * T], True, True, [Rkvn, RWk], [Rps[7]], True)
            S.op("dve", ("tensor_copy", A(out=KT2[0:64, b, g * T:(g + 1) * T], in_=ps[7][0:64, :])), reads=[Rps[7]], writes=[Rkt[b]])

        def vstep(g8):
            for j in range(8):
                kt = g8 * 8 + j
                mm(ps[7][:, j * 64:(j + 1) * 64], kvnT[:, kt * 128:(kt + 1) * 128], WkV[:, h, :], True, True, [Rkvn, RWk], [Rps[7]], j == 7)
            S.op("dve", ("tensor_copy", A(out=Vh[b][:, g8 * 8:(g8 + 1) * 8, 0:64], in_=ps[7][:].rearrange("p (j d) -> p j d", d=64))), reads=[Rps[7]], writes=[RV[b]])

        steps.append(lambda: S.dma("sp", QT[b][0:96, :], qTs[h], writes=[RQ[b]]))
        for g in range(S_LEN // T):
            steps.append(lambda g=g: kstep(g))
        for g8 in range(NKT // 8):
            steps.append(lambda g8=g8: vstep(g8))
        return steps

    pro_steps = []

    def mla_head_prologue(h):
        while pro_steps:
            pro_steps.pop(0)()
        if h == 0:
            for st_ in mla_prologue_steps(0):
                st_()
        if h + 1 < 8:
            pro_steps.extend(mla_prologue_steps(h + 1))

    units = []
    for h in range(8):
        for qi in range(NO):
            L = key_tiles(qi)
            n_ = 0
            while n_ < len(L):
                kt, kind, sub = L[n_]
                if kind != 1 and n_ + 1 < len(L) and L[n_ + 1][1] == kind:
                    blks = [L[n_], L[n_ + 1]]
                    n_ += 2
                else:
                    blks = [L[n_]]
                    n_ += 1
                units.append((h, qi, kind, blks, blks[0] is L[0], blks[-1] is L[-1]))
    PT2 = [sb.alloc([128, 2, T], BF16) for _ in range(3)]
    RPT2 = [Reg() for _ in range(3)]

    def mla_S(u):
        h, qi, kind, blks, first, last = units[u]
        if qi == 0 and first:
            mla_head_prologue(h)
        b = h % 2
        a = (u % 2) * 2
        pb = u % 3
        sub = blks[0][2]
        q0 = 128 * sub
        w = T - q0
        nb = len(blks)
        for bi, (kt, _, _) in enumerate(blks):
            mm(ps[a + bi][:, 0:w], KT2[:, b, kt * 128:(kt + 1) * 128], QT[b][:, qi * T + q0:(qi + 1) * T], True, True,
               [Rkt[b], Rkr, Rkz, RQ[b], RQz], [Rps[a + bi]], True)
        kw = dict(func=AF.Exp, scale=SC_M)
        rd = [Rps[a + bi] for bi in range(nb)]
        if kind == 2:
            kw["bias"] = fbias
            rd.append(Rl)
        S.op("act", ("activation", A(out=PT2[pb][:, 0:nb, 0:w], in_=pspair(a)[:, 0:nb, 0:w], **kw)), reads=rd, writes=[RPT2[pb]])
        if kind == 1:
            S.op("dve", ("tensor_tensor", A(out=PT2[pb][:, 0, 0:128], in0=PT2[pb][:, 0, 0:128], in1=cm[:, 0, :], op=ALU.mult)), reads=[RPT2[pb], Rc], writes=[RPT2[pb]])

    def mla_PV(u):
        h, qi, kind, blks, first, last = units[u]
        pb = u % 3
        sub = blks[0][2]
        q0 = 128 * sub
        w = T - q0
        ob = 4 + fin_i[0] % 2
        nb = len(blks)
        for bi, (kt, _, _) in enumerate(blks):
            st = first and bi == 0
            sp_ = last and bi == nb - 1
            mm(ps[ob][:, q0:T], Vh[h % 2][:, kt, :], PT2[pb][:, bi, 0:w], st, sp_, [RV[h % 2], RPT2[pb]], [Rps[ob]], sp_ or bi == nb - 1)
        if last:
            f = fin_i[0] % 2
            fin_i[0] += 1
            S.op("act", ("activation", A(out=rc[64:65, :], in_=ps[ob][64:65, :], func=AF.Ln)), reads=[Rps[ob]], writes=[Rrc])
            S.op("act", ("activation", A(out=rc[64:65, :], in_=rc[64:65, :], func=AF.Exp, scale=-1.0)), reads=[Rrc], writes=[Rrc])

            def fin_pe(ob=ob, f=f, h=h, qi=qi):
                mm(ps[6][0:64, :], ones_f[64:65, 0:64], rc[64:65, :], True, True, [Rrc, Rc], [Rps[6]], True)
                S.op("dve", ("tensor_copy", A(out=bcs[:], in_=ps[6][0:64, :])), reads=[Rps[6]], writes=[Rbcs])
                S.op("dve", ("tensor_tensor", A(out=yst[f][:], in0=ps[ob][0:64, :], in1=bcs[:], op=ALU.mult)), reads=[Rps[ob], Rbcs], writes=[Ryst[f]])
                S.dma("pool", ymT[h * 64:(h + 1) * 64, qi * T:(qi + 1) * T], yst[f][:], reads=[Ryst[f]])
            pending.append([2, fin_pe])

    def run_pending(force=False):
        for p in list(pending):
            p[0] -= 1
            if p[0] <= 0 or force:
                p[1]()
                pending.remove(p)

    n = len(units)
    for u in range(n + 1):
        if u < n:
            mla_S(u)
        if u - 1 >= 0:
            mla_PV(u - 1)
        run_pending()
        if pro_steps and u % 2 == 1:
            pro_steps.pop(0)()
    run_pending(force=True)
    S.barrier()

    sb.off = c_off
    KTd = [sb.alloc([128, S_LEN], BF16) for _ in range(2)]
    Vd = [sb.alloc([128, NKT, 128], BF16) for _ in range(2)]
    QTd = [sb.alloc([128, NO * T], BF16) for _ in range(2)]
    P12 = [sb.alloc([128, 2, T], BF16) for _ in range(3)]
    r1 = sb.alloc([128, T], F32); r2 = sb.alloc([128, T], F32)
    u1 = sb.alloc([128, T], F32); u2 = sb.alloc([128, T], F32)
    dd = sb.alloc([128, T], F32); sqd = sb.alloc([128, T], BF16); rsd = sb.alloc([128, T], F32)
    ysd = [sb.alloc([128, T], BF16) for _ in range(2)]
    RKd = [Reg(), Reg()]; RVd = [Reg(), Reg()]; RQd = [Reg(), Reg()]
    RP12 = [Reg() for _ in range(3)]
    Rr1, Rr2, Ru1, Ru2, Rdd, Rsqd, Rrsd = [Reg() for _ in range(7)]
    Rysd = [Reg(), Reg()]
    dblocks = []
    for g in range(4):
        for qi in range(NO):
            L = key_tiles(qi)
            for n_, (kt, kind, sub) in enumerate(L):
                dblocks.append((g, qi, kt, kind, sub, n_ == 0, n_ == len(L) - 1))
    dfin = [0]

    def diff_prologue(g):
        b = g % 2
        S.dma("sp", KTd[b][:], dkT[g], writes=[RKd[b]])
        S.dma("sp", Vd[b][:], dvS[:, g * 128:(g + 1) * 128].rearrange("(k p) d -> p k d", p=128), writes=[RVd[b]])
        S.dma("sp", QTd[b][:], dqT[g], writes=[RQd[b]])

    def diff_S(i):
        g, qi, kt, kind, sub, first, last = dblocks[i]
        if qi == 0 and first and g == 0:
            diff_prologue(0)
        b = g % 2
        q0 = 128 * sub
        w = T - q0
        a = (i % 2) * 2
        pb = i % 3
        mm(ps[a][:, 0:w], KTd[b][0:64, kt * 128:(kt + 1) * 128], QTd[b][0:64, qi * T + q0:(qi + 1) * T], True, True, [RKd[b], RQd[b]], [Rps[a]], True)
        mm(ps[a + 1][:, 0:w], KTd[b][64:128, kt * 128:(kt + 1) * 128], QTd[b][64:128, qi * T + q0:(qi + 1) * T], True, True, [RKd[b], RQd[b]], [Rps[a + 1]], True)
        kw = dict(func=AF.Exp, scale=SC_D)
        rd = [Rps[a], Rps[a + 1]]
        if kind == 2:
            kw["bias"] = fbias
            rd.append(Rl)
        S.op("act", ("activation", A(out=P12[pb][:, :, 0:w], in_=pspair(a)[:, :, 0:w], **kw)), reads=rd, writes=[RP12[pb]])
        if kind == 1:
            for k_ in range(2):
                S.op("dve", ("tensor_tensor", A(out=P12[pb][:, k_, 0:128], in0=P12[pb][:, k_, 0:128], in1=cm[:, 0, :], op=ALU.mult)), reads=[RP12[pb], Rc], writes=[RP12[pb]])

    def diff_PV(i):
        g, qi, kt, kind, sub, first, last = dblocks[i]
        b = g % 2
        q0 = 128 * sub
        w = T - q0
        pb = i % 3
        st, sp_ = first, last
        mm(ps[4][:, q0:T], Vd[b][:, kt, :], P12[pb][:, 0, 0:w], st, sp_, [RVd[b], RP12[pb]], [Rps[4]], sp_)
        mm(ps[6][:, q0:T], ones_bf[:, :], P12[pb][:, 0, 0:w], st, sp_, [Rc, RP12[pb]], [Rps[6]], sp_)
        mm(ps[5][:, q0:T], Vd[b][:, kt, :], P12[pb][:, 1, 0:w], st, sp_, [RVd[b], RP12[pb]], [Rps[5]], sp_)
        mm(ps[7][:, q0:T], ones_bf[:, :], P12[pb][:, 1, 0:w], st, sp_, [Rc, RP12[pb]], [Rps[7]], True)
        if sp_:
            f = dfin[0] % 2
            dfin[0] += 1
            S.op("act", ("activation", A(out=r1[:], in_=ps[6][:], func=AF.Ln)), reads=[Rps[6]], writes=[Rr1])
            S.op("act", ("activation", A(out=r2[:], in_=ps[7][:], func=AF.Ln)), reads=[Rps[7]], writes=[Rr2])
            S.op("act", ("activation", A(out=r1[:], in_=r1[:], func=AF.Exp, scale=-1.0)), reads=[Rr1], writes=[Rr1])
            S.op("act", ("activation", A(out=r2[:], in_=r2[:], func=AF.Exp, scale=-1.0)), reads=[Rr2], writes=[Rr2])
            S.op("dve", ("tensor_tensor", A(out=u1[:], in0=ps[4][:], in1=r1[:], op=ALU.mult)), reads=[Rps[4], Rr1], writes=[Ru1])
            S.op("dve", ("tensor_tensor", A(out=u2[:], in0=ps[5][:], in1=r2[:], op=ALU.mult)), reads=[Rps[5], Rr2], writes=[Ru2])
            S.op("dve", ("scalar_tensor_tensor", A(out=dd[:], in0=u2[:], scalar=neglam, in1=u1[:], op0=ALU.mult, op1=ALU.add)), reads=[Ru1, Ru2, Rl], writes=[Rdd])
            S.op("pool", ("tensor_tensor", A(out=sqd[:], in0=dd[:], in1=dd[:], op=ALU.mult)), reads=[Rdd], writes=[Rsqd])

            def fin_pe(f=f, g=g, qi=qi):
                mm(ps[6][:], ones_bf[:, :], sqd[:], True, True, [Rsqd, Rc], [Rps[6]], True)
                S.op("act", ("activation", A(out=rsd[:], in_=ps[6][:], func=AF.Ln, bias=EPS, scale=1.0 / 128.0)), reads=[Rps[6]], writes=[Rrsd])
                S.op("act", ("activation", A(out=rsd[:], in_=rsd[:], func=AF.Exp, scale=-0.5)), reads=[Rrsd], writes=[Rrsd])
                S.op("dve", ("scalar_tensor_tensor", A(out=ysd[f][:], in0=dd[:], scalar=sgain, in1=rsd[:], op0=ALU.mult, op1=ALU.mult)), reads=[Rdd, Rrsd, Rl], writes=[Rysd[f]])
                S.dma("pool", ydT[g * 128:(g + 1) * 128, qi * T:(qi + 1) * T], ysd[f][:], reads=[Rysd[f]])
            fin_pe()

    n = len(dblocks)
    for i in range(n + 1):
        if i < n:
            diff_S(i)
        if i - 1 >= 0:
            diff_PV(i - 1)
        if i < n and dblocks[i][1] == 0 and dblocks[i][5] and dblocks[i][0] + 1 < 4:
            diff_prologue(dblocks[i][0] + 1)
    S.barrier()

    sb.off = base_off
    WgM = sb.alloc([128, 8, D], BF16); WgD = sb.alloc([128, 8, D], BF16)
    Wpm = sb.alloc([128, 4, D], BF16); Wpd = sb.alloc([128, 4, D], BF16); Wo = sb.alloc([128, 8, D], BF16)
    RW1 = Reg(True)
    winr = win.rearrange("(c p) f -> p c f", p=128)
    load_w(WgM, winr[:, :, C_GM:C_GM + D], RW1, 1, D, 3)
    load_w(WgD, winr[:, :, C_GD:C_GD + D], RW1, 1, D, 3)
    load_w(Wpm, wpm.rearrange("(c p) f -> p c f", p=128), RW1, 1, D, 3)
    load_w(Wpd, wpd.rearrange("(c p) f -> p c f", p=128), RW1, 1, D, 3)
    load_w(Wo, wout.rearrange("(c p) f -> p c f", p=128), RW1, 1, D, 3)
    xb = [sb.alloc([128, 8, T], F32) for _ in range(2)]
    ym = [sb.alloc([128, 4, T], BF16) for _ in range(2)]
    yd = [sb.alloc([128, 4, T], BF16) for _ in range(2)]
    h2 = sb.alloc([128, 8, T], BF16)
    rstd = sb.alloc([128, T], F32)
    sg1 = sb.alloc([128, T], F32); sg2 = sb.alloc([128, T], F32)
    m1 = sb.alloc([128, T], F32); m2 = sb.alloc([128, T], F32)
    mg = sb.alloc([128, 8, T], BF16)
    Rx = [Reg(), Reg()]; Rym = [Reg(), Reg()]; Ryd = [Reg(), Reg()]; Rh2 = Reg(); Rr = Reg()
    Rsg1, Rsg2, Rm1, Rm2 = Reg(), Reg(), Reg(), Reg()
    Rmg = [Reg() for _ in range(8)]

    def load_d(qi):
        t = own[qi]
        S.dma("sp", xb[qi % 2][:], x1o[qi].rearrange("(c p) t -> p c t", p=128), writes=[Rx[qi % 2]])
        S.dma("sp", ym[qi % 2][:], ymT[:, qi * T:(qi + 1) * T].rearrange("(c p) t -> p c t", p=128), writes=[Rym[qi % 2]])
        S.dma("sp", yd[qi % 2][:], ydT[:, qi * T:(qi + 1) * T].rearrange("(c p) t -> p c t", p=128), writes=[Ryd[qi % 2]])

    load_d(0)
    h2d = [h2, sb.alloc([128, 8, T], BF16)]
    Rh2d = [Rh2, Reg()]
    sgs = [sg1, sg2, sb.alloc([128, T], F32), sb.alloc([128, T], F32)]
    Rsgs = [Rsg1, Rsg2, Reg(), Reg()]
    mms = [m1, m2, sb.alloc([128, T], F32), sb.alloc([128, T], F32)]
    Rmms = [Rm1, Rm2, Reg(), Reg()]
    bank_i = [0]

    def nbank():
        b_ = 1 + bank_i[0] % 6
        bank_i[0] += 1
        return b_

    def norm_d(qi):
        x_ = xb[qi % 2]
        hh_ = h2d[qi % 2]
        rmsnorm(lambda c: x_[:, c, :], Rx[qi % 2], 8, D, G_MIX, lambda c: hh_[:, c, :], Rh2d[qi % 2], lambda c: hh_[:, c, :], Rh2d[qi % 2], 0, rstd, Rr)

    norm_d(0)
    for qi in range(NO):
        x = xb[qi % 2]; Rxc = Rx[qi % 2]
        hh = h2d[qi % 2]; Rhh = Rh2d[qi % 2]
        if qi + 1 < NO:
            load_d(qi + 1)
        for d in range(8):
            dsl = slice(d * 128, (d + 1) * 128)
            o_ = (d % 2) * 2
            bg1, bp1, bg2, bp2 = (nbank(), nbank(), nbank(), nbank()) if KF[2] == "1" else (1, 3, 2, 4)
            for k in range(8):
                mm(ps[bg1][:], WgM[:, k, dsl], hh[:, k, :], k == 0, k == 7, [RW1, Rhh], [Rps[bg1]], k == 7)
            S.op("act", ("activation", A(out=sgs[o_][:], in_=ps[bg1][:], func=AF.Sigmoid)), reads=[Rps[bg1]], writes=[Rsgs[o_]])
            for k in range(4):
                mm(ps[bp1][:], Wpm[:, k, dsl], ym[qi % 2][:, k, :], k == 0, k == 3, [RW1, Rym[qi % 2]], [Rps[bp1]], k == 3)
            S.op("dve", ("tensor_tensor", A(out=mms[o_][:], in0=ps[bp1][:], in1=sgs[o_][:], op=ALU.mult)), reads=[Rps[bp1], Rsgs[o_]], writes=[Rmms[o_]])
            for k in range(8):
                mm(ps[bg2][:], WgD[:, k, dsl], hh[:, k, :], k == 0, k == 7, [RW1, Rhh], [Rps[bg2]], k == 7)
            S.op("act", ("activation", A(out=sgs[o_ + 1][:], in_=ps[bg2][:], func=AF.Sigmoid)), reads=[Rps[bg2]], writes=[Rsgs[o_ + 1]])
            for k in range(4):
                mm(ps[bp2][:], Wpd[:, k, dsl], yd[qi % 2][:, k, :], k == 0, k == 3, [RW1, Ryd[qi % 2]], [Rps[bp2]], k == 3)
            S.op("dve", ("tensor_tensor", A(out=mms[o_ + 1][:], in0=ps[bp2][:], in1=sgs[o_ + 1][:], op=ALU.mult)), reads=[Rps[bp2], Rsgs[o_ + 1]], writes=[Rmms[o_ + 1]])
            S.op("pool", ("tensor_tensor", A(out=mg[:, d, :], in0=mms[o_][:], in1=mms[o_ + 1][:], op=ALU.add)), reads=[Rmms[o_], Rmms[o_ + 1]], writes=[Rmg[d]])
        if qi + 1 < NO:
            norm_d(qi + 1)
        for d in range(8):
            po = nbank() if KF[2] == "1" else 5 + d % 2
            for k in range(8):
                mm(ps[po][:], Wo[:, k, d * 128:(d + 1) * 128], mg[:, k, :], k == 0, k == 7, [RW1, Rmg[k]], [Rps[po]], k == 7)
            S.op("dve", ("tensor_tensor", A(out=x[:, d, :], in0=ps[po][:], in1=x[:, d, :], op=ALU.add)), reads=[Rps[po]], writes=[Rxc])
        S.dma("pool", x2T[:, qi * T:(qi + 1) * T].rearrange("(c p) t -> p c t", p=128), x[:], reads=[Rxc])
    S.barrier()

    ffn_phase(lambda t: x2T[:, t * T:(t + 1) * T], NO, w2g, w2u, w2d, G_FFN2, lambda t: outT[:, t * T:(t + 1) * T], lambda t: [], True)
    S.emit()
    return nc


def const_tables():
    tri = np.zeros((128, 128), np.float32)
    for p in range(128):
        tri[p, p:] = 1.0
    pmM = np.zeros((128, 128), np.float32)
    for j in range(16):
        pmM[64 + j + 16, 64 + j] = -1.0
        pmM[64 + j, 64 + j + 16] = 1.0
    pmD = np.zeros((128, 128), np.float32)
    for b in (0, 64):
        for j in range(8):
            pmD[b + j + 8, b + j] = -1.0
            pmD[b + j, b + j + 8] = 1.0
    cmat = np.stack([tri, pmM, pmD], axis=1)
    fsc = np.zeros((128, 2), np.float64)
    for j in range(32):
        fsc[64 + j, 0] = THETA ** (-(j % 16) / 16.0) / (2 * math.pi)
    for b in (0, 64):
        for j in range(16):
            fsc[b + j, 1] = THETA ** (-(j % 8) / 8.0) / (2 * math.pi)
    return np.ascontiguousarray(cmat), fsc.astype(np.float32)


def col8(v, n):
    return np.ascontiguousarray(np.asarray(v, np.float32).reshape(n, 128).T)


def prep_shared(inp):
    f = lambda k: np.ascontiguousarray(np.asarray(inp[k], np.float32)[0])
    cmat, fsc = const_tables()
    gv = np.zeros((128, 36), np.float32)
    gv[:, 0:8] = col8(inp["ffn1_norm"][0], 8)
    gv[:, 8:16] = col8(inp["mix_norm"][0], 8)
    gv[:, 16:24] = col8(inp["ffn2_norm"][0], 8)
    gv[:, 24:32] = col8(inp["final_norm"], 8)
    gv[:, 32:34] = col8(inp["mla_q_norm"][0], 2)
    gv[:, 34:35] = col8(inp["mla_kv_norm"][0], 1)
    gv[:, 35:36] = col8(inp["diff_subln"][0], 1)
    lam = np.concatenate([np.asarray(inp[k], np.float32)[0] for k in ("diff_lam_q1", "diff_lam_k1", "diff_lam_q2", "diff_lam_k2")])
    lamr = np.ascontiguousarray(np.broadcast_to(lam[None, :], (128, 256)))
    return {
        "gv": gv, "lamr": lamr, "fsc": fsc, "cmat": cmat,
        "w1g": f("ffn1_w_gate"), "w1u": f("ffn1_w_up"), "w1d": f("ffn1_w_down"),
        "w2g": f("ffn2_w_gate"), "w2u": f("ffn2_w_up"), "w2d": f("ffn2_w_down"),
        "win": f("w_in"), "wuq": f("mla_w_uq"), "wukv": f("mla_w_ukv"),
        "wpm": f("w_proj_mla"), "wpd": f("w_proj_diff"), "wout": f("w_out"),
    }


def core_perm(half, NT):
    NO = NT // 2
    own = [2 * i + half for i in range(NO)]
    oth = [2 * i + 1 - half for i in range(NO)]
    return own, oth


def make_in_maps(inputs, n_cores):
    x = np.asarray(inputs["x"], np.float32)
    pos = np.asarray(inputs["positions"], np.int32)
    B, S_LEN, _ = x.shape
    NT = S_LEN // T
    shared = prep_shared(inputs)
    in_maps, owns = [], []
    for core in range(n_cores):
        b, half = core // 2, core % 2
        own, oth = core_perm(half, NT)
        idx = np.concatenate([np.arange(c * T, (c + 1) * T) for c in own + oth])
        m = dict(shared)
        m["xT"] = np.ascontiguousarray(x[b][idx].T)
        m["posr"] = np.ascontiguousarray(np.broadcast_to(pos[b][idx][None, :], (128, S_LEN)))
        m["flag"] = np.full((128, 1), float(half), np.float32)
        in_maps.append(m)
        owns.append(own)
    return in_maps, owns


def gather_out(results, owns, B, S_LEN):
    out = np.empty((B, S_LEN, D), np.float32)
    for core, r in enumerate(results):
        b = core // 2
        oT = np.asarray(r["outT"])
        for qi, c in enumerate(owns[core]):
            out[b, c * T:(c + 1) * T, :] = oT[:, qi * T:(qi + 1) * T].T
    return out


def kernel(**inputs):
    x = np.asarray(inputs["x"])
    B, S_LEN, _ = x.shape
    in_maps, owns = make_in_maps(inputs, 2 * B)
    nc = build(S_LEN, 2 * B)
    res = run_bass_kernel_spmd(nc, in_maps, core_ids=list(range(2 * B)))
    return gather_out(res.results, owns, B, S_LEN)
```

```python
import math
import os
import numpy as np
import concourse.bass as bass
import concourse.mybir as mybir
from concourse.bass_utils import run_bass_kernel_spmd

F32 = mybir.dt.float32
BF16 = mybir.dt.bfloat16
I32 = mybir.dt.int32
AF = mybir.ActivationFunctionType
ALU = mybir.AluOpType
AX = mybir.AxisListType

D = 1024
FF = 2816
NFF = FF // 128
T = 512
EPS = 1e-6
THETA = 500000.0
N_IN = 4000
C_CQ, C_CKV, C_KR, C_DQ, C_DK, C_DV, C_GM, C_GD = 0, 256, 384, 416, 928, 1440, 1952, 2976
LAM_INIT = 0.8 - 0.6 * math.exp(-0.3 * 0.0)
SC_M = 96.0 ** -0.5
SC_D = 64.0 ** -0.5
MAGIC = 12582912.0
OWN_A = [0, 3, 4, 7, 8, 11, 12, 15]
OWN_B = [1, 2, 5, 6, 9, 10, 13, 14]

ENGS = ("sp", "act", "dve", "pool", "pe")
KF = os.environ.get("KOPT", "1111")


def A(*a, **k):
    return (a, k)


class Reg:
    __slots__ = ("w", "r", "const", "excl")

    def __init__(self, const=False, excl=False):
        self.w = None
        self.r = []
        self.const = const
        self.excl = excl


class Sched:
    def __init__(self, nc, n_dma_sems=12):
        self.nc = nc
        self.q = {e: [] for e in ENGS}
        self.cnt = {e: 0 for e in ENGS}
        self.seen = {e: {} for e in ENGS}
        self.sem = {e: nc.alloc_semaphore("s_" + e) for e in ENGS}
        self.dma_sems = {}
        self.n_dma_sems = n_dma_sems
        self.dma_rr = {}
        self.dma_val = {}

    def _wait(self, eng, tok):
        if tok is None:
            return
        key, val = tok
        if key == eng and val > self.cnt[eng]:
            return
        if self.seen[eng].get(key, 0) >= val:
            return
        self.seen[eng][key] = val
        sem = self.sem[key] if key in self.sem else self.dma_sems[key]
        self.q[eng].append(lambda e, sem=sem, val=val: e.wait_ge(sem, val))

    def _deps(self, reads, writes, deps):
        toks = list(deps)
        for r in reads:
            if r.w is not None:
                toks.append(r.w)
            if r.excl:
                toks.extend(r.r)
        for w in writes:
            if w.w is not None:
                toks.append(w.w)
            toks.extend(w.r)
        return toks

    @staticmethod
    def _merge(toks):
        best = {}
        for t in toks:
            if t is None:
                continue
            if best.get(t[0], 0) < t[1]:
                best[t[0]] = t[1]
        return list(best.items())

    def _record(self, tok, reads, writes):
        for r in reads:
            if r.const:
                continue
            r.r.append(tok)
            if len(r.r) > 16:
                best = {}
                for k, v in r.r:
                    if best.get(k, 0) < v:
                        best[k] = v
                r.r = list(best.items())
        for w in writes:
            w.w = tok
            w.r = []

    def op(self, eng, fn, reads=(), writes=(), sig=True, deps=()):
        if isinstance(fn, tuple):
            m_, (a_, k_) = fn
            fn = lambda e, m_=m_, a_=a_, k_=k_: getattr(e, m_)(*a_, **k_)
        for t in self._merge(self._deps(reads, writes, deps)):
            self._wait(eng, t)
        if sig:
            self.cnt[eng] += 1
            sem = self.sem[eng]
            self.q[eng].append(lambda e, fn=fn, sem=sem: fn(e).then_inc(sem, 1))
            tok = (eng, self.cnt[eng])
        else:
            self.q[eng].append(lambda e, fn=fn: fn(e))
            tok = (eng, self.cnt[eng] + 1)
        self._record(tok, reads, writes)
        return tok

    def dma(self, eng, out, in_, reads=(), writes=(), deps=()):
        toks = self._deps(reads, writes, deps)
        rr = self.dma_rr.get(eng, 0)
        self.dma_rr[eng] = (rr + 1) % (6 if (eng == "pool" and KF[3] == "1") else self.n_dma_sems)
        key = "dma_%s_%d" % (eng, rr)
        if key not in self.dma_sems:
            self.dma_sems[key] = self.nc.alloc_semaphore(key)
            self.dma_val[key] = 0
        if self.dma_val[key] > 0:
            toks.append((key, self.dma_val[key]))
        for t in toks:
            self._wait(eng, t)
        self.dma_val[key] += 16
        sem = self.dma_sems[key]
        self.q[eng].append(lambda e, out=out, in_=in_, sem=sem: e.dma_start(out=out, in_=in_).then_inc(sem, 16))
        tok = (key, self.dma_val[key])
        self._record(tok, reads, writes)
        return tok

    def coll(self, kind, op, groups, in_ap, out_ap, reads=(), writes=()):
        eng = "pool"
        toks = self._deps(reads, writes, ())
        i = self.dma_rr.get("cc", 0)
        self.dma_rr["cc"] = i + 1
        key = "cc_%d" % (i % 8)
        if key not in self.dma_sems:
            self.dma_sems[key] = self.nc.alloc_semaphore(key)
            self.dma_val[key] = 0
        if self.dma_val[key] > 0:
            toks.append((key, self.dma_val[key]))
        for t in self._merge(toks):
            self._wait(eng, t)
        self.dma_val[key] += 1
        sem = self.dma_sems[key]
        self.q[eng].append(lambda e: e.collective_compute(kind, op, replica_groups=groups, ins=[in_ap], outs=[out_ap]).then_inc(sem, 1))
        tok = (key, self.dma_val[key])
        self._record(tok, reads, writes)
        return tok

    def barrier(self):
        toks = [(e, self.cnt[e]) for e in ENGS if self.cnt[e] > 0]
        toks += [(k, v) for k, v in self.dma_val.items() if v > 0]
        for e in ENGS:
            for t in toks:
                self._wait(e, t)

    def emit(self):
        q = self.q
        with self.nc.Block() as block:
            @block.sync
            def _(e):
                for f in q["sp"]:
                    f(e)

            @block.scalar
            def _(e):
                for f in q["act"]:
                    f(e)

            @block.vector
            def _(e):
                for f in q["dve"]:
                    f(e)

            @block.gpsimd
            def _(e):
                for f in q["pool"]:
                    f(e)

            @block.tensor
            def _(e):
                for f in q["pe"]:
                    f(e)


class SB:
    def __init__(self, nc):
        self.nc = nc
        self.off = 16512
        self.top = 229344
        self.n = 0

    def alloc(self, shape, dt):
        n = 1
        for s in shape[1:]:
            n *= s
        nb = n * (2 if dt == BF16 else 4)
        nb = (nb + 31) // 32 * 32
        t = self.nc.alloc_sbuf_tensor_at("t%d" % self.n, list(shape), dt, offset=self.off)
        self.n += 1
        self.off += nb
        assert self.off <= self.top, "SBUF overflow %d" % self.off
        return t


def build(S_LEN, n_cores=8):
    NT = S_LEN // T
    NO = NT // 2
    own = list(range(NO))
    nc = bass.Bass("TRN2", target_bir_lowering=False)

    def din(name, shape, dt=F32):
        return nc.dram_tensor(name, list(shape), dt, kind="ExternalInput").ap()

    def dscr(name, shape, dt):
        return nc.dram_tensor(name, list(shape), dt, kind="Internal").ap()

    xT = din("xT", [D, S_LEN])
    posr = din("posr", [128, S_LEN], I32)
    gv = din("gv", [128, 36])
    lamr = din("lamr", [128, 256])
    fsc = din("fsc", [128, 2])
    cmat = din("cmat", [128, 3, 128])
    flag = din("flag", [128, 1])
    w1g = din("w1g", [D, FF]); w1u = din("w1u", [D, FF]); w1d = din("w1d", [FF, D])
    w2g = din("w2g", [D, FF]); w2u = din("w2u", [D, FF]); w2d = din("w2d", [FF, D])
    win = din("win", [D, N_IN])
    wuq = din("wuq", [256, 768]); wukv = din("wukv", [128, 1024])
    wpm = din("wpm", [512, D]); wpd = din("wpd", [512, D]); wout = din("wout", [D, D])
    outT = nc.dram_tensor("outT", [D, NO * T], F32, kind="ExternalOutput").ap()

    x1o = dscr("x1o", [NO, D, T], F32)
    x1s = dscr("x1s", [NO, D, T], F32)
    groups = [[2 * i, 2 * i + 1] for i in range(n_cores // 2)]
    Rxo = [Reg() for _ in range(NO)]; Rxsum = [Reg() for _ in range(NO)]
    dkT = dscr("dkT", [4, 128, S_LEN], BF16)
    dvS = dscr("dvS", [S_LEN, 512], BF16)
    qTs = dscr("qTs", [8, 96, NO * T], BF16)
    dqT = dscr("dqT", [4, 128, NO * T], BF16)
    ymT = dscr("ymT", [512, NO * T], BF16)
    ydT = dscr("ydT", [512, NO * T], BF16)
    x2T = dscr("x2T", [D, NO * T], F32)

    S = Sched(nc)
    sb = SB(nc)
    psall = nc.alloc_psum_tensor("psall", [128, 8 * T], F32)
    ps = [psall[:, i * T:(i + 1) * T] for i in range(8)]

    def pspair(a):
        return psall[:, a * T:(a + 2) * T].rearrange("p (b t) -> p b t", t=T)
    Rps = [Reg(excl=True) for _ in range(8)]

    ones_bf = sb.alloc([128, 128], BF16)
    ones_f = sb.alloc([128, 64], F32)
    cm = sb.alloc([128, 3, 128], BF16)
    gvt = sb.alloc([128, 36], F32)
    fst = sb.alloc([128, 2], F32)
    lamt = sb.alloc([128, 256], F32)
    lamw = sb.alloc([128, 8], F32)
    flagt = sb.alloc([128, 1], F32)
    Rc = Reg(const=True)
    S.op("pool", ("memset", A(ones_bf[:], 1.0)), writes=[Rc])
    S.op("pool", ("memset", A(ones_f[:], 1.0)), writes=[Rc])
    S.dma("pool", cm[:], cmat, writes=[Rc])
    S.dma("sp", gvt[:], gv, writes=[Rc])
    S.dma("sp", fst[:], fsc, writes=[Rc])
    S.dma("sp", lamt[:], lamr, writes=[Rc])
    S.dma("sp", flagt[:], flag, writes=[Rc])
    Rl = Reg()
    S.op("dve", ("tensor_tensor", A(out=lamt[:, 0:64], in0=lamt[:, 0:64], in1=lamt[:, 64:128], op=ALU.mult)), reads=[Rc], writes=[Rl])
    S.op("dve", ("tensor_tensor", A(out=lamt[:, 128:192], in0=lamt[:, 128:192], in1=lamt[:, 192:256], op=ALU.mult)), reads=[Rc], writes=[Rl])
    S.op("dve", ("reduce_sum", A(out=lamw[:, 0:1], in_=lamt[:, 0:64], axis=AX.X)), reads=[Rl], writes=[Rl])
    S.op("dve", ("reduce_sum", A(out=lamw[:, 1:2], in_=lamt[:, 128:192], axis=AX.X)), reads=[Rl], writes=[Rl])
    S.op("act", ("activation", A(out=lamw[:, 2:4], in_=lamw[:, 0:2], func=AF.Exp)), reads=[Rl], writes=[Rl])
    S.op("dve", ("tensor_tensor", A(out=lamw[:, 4:5], in0=lamw[:, 3:4], in1=lamw[:, 2:3], op=ALU.subtract)), reads=[Rl], writes=[Rl])
    S.op("dve", ("tensor_scalar", A(out=lamw[:, 4:5], in0=lamw[:, 4:5], scalar1=-LAM_INIT, scalar2=None, op0=ALU.add)), reads=[Rl], writes=[Rl])
    S.op("dve", ("tensor_scalar", A(out=lamw[:, 5:6], in0=gvt[:, 35:36], scalar1=1.0 - LAM_INIT, scalar2=None, op0=ALU.mult)), reads=[Rl, Rc], writes=[Rl])
    S.op("dve", ("tensor_scalar", A(out=lamw[:, 6:7], in0=flagt[:, 0:1], scalar1=-1.0, scalar2=30000.0, op0=ALU.add, op1=ALU.mult)), reads=[Rc], writes=[Rl])
    S.barrier()
    G_FFN1, G_MIX, G_FFN2, G_FIN, G_Q, G_KV = 0, 8, 16, 24, 32, 34
    neglam = lamw[:, 4:5]
    sgain = lamw[:, 5:6]
    fbias = lamw[:, 6:7]
    base_off = sb.off

    def mm(out, lhsT, rhs, start, stop, reads, writes, sig):
        return S.op("pe", ("matmul", A(out, lhsT, rhs, start=start, stop=stop)), reads=reads, writes=writes, sig=sig)

    def rmsnorm(src, Rsrc, C, nfeat, gcol, sqbuf, Rsq, out, Rout, pbank, rstd, Rrstd):
        snap = ([Rsq.w] if Rsq.w is not None else []) + list(Rsq.r)
        Rsqc = [Reg() for _ in range(C)]
        for c in range(C):
            if c % 2 == 0:
                S.op("act", ("activation", A(out=sqbuf(c), in_=src(c), func=AF.Square)), reads=[Rsrc], writes=[Rsqc[c]], deps=snap)
            else:
                S.op("dve", ("tensor_tensor", A(out=sqbuf(c), in0=src(c), in1=src(c), op=ALU.mult)), reads=[Rsrc], writes=[Rsqc[c]], deps=snap)
        for c in range(C):
            mm(ps[pbank][:], ones_bf[:, :], sqbuf(c), c == 0, c == C - 1, [Rsqc[c], Rsq, Rc], [Rps[pbank]], c == C - 1)
        S.op("act", ("activation", A(out=rstd[:], in_=ps[pbank][:], func=AF.Ln, bias=EPS, scale=1.0 / nfeat)), reads=[Rps[pbank]], writes=[Rrstd])
        S.op("act", ("activation", A(out=rstd[:], in_=rstd[:], func=AF.Exp, scale=-0.5)), reads=[Rrstd], writes=[Rrstd])
        for c in range(C):
            S.op("dve", ("scalar_tensor_tensor", A(out=out(c), in0=src(c), scalar=gvt[:, gcol + c:gcol + c + 1], in1=rstd[:], op0=ALU.mult, op1=ALU.mult)),
                 reads=[Rsrc, Rrstd, Rc], writes=[Rout])

    def load_w(dst, src_ap, R, pieces, axis_len, dim):
        step = axis_len // pieces
        for i in range(pieces):
            sl = slice(i * step, (i + 1) * step)
            if dim == 3:
                S.dma("pool", dst[:, :, sl], src_ap[:, :, sl], writes=[R])
            else:
                S.dma("pool", dst[:, sl], src_ap[:, sl], writes=[R])

    def ffn_phase(x_in, n_tiles, wg, wu, wd, gpre, x_out, out_tiles, final_norm, after_store=None):
        sb.off = base_off
        Wg = sb.alloc([128, 8, FF], BF16); Wu = sb.alloc([128, 8, FF], BF16); Wd = sb.alloc([128, NFF, D], BF16)
        xb = [sb.alloc([128, 8, T], F32) for _ in range(2)]
        hT = sb.alloc([128, 8, T], BF16)
        aT = sb.alloc([128, NFF, T], BF16)
        rstd = sb.alloc([128, T], F32)
        sil = [sb.alloc([128, T], BF16) for _ in range(2)]
        Rx = [Reg(), Reg()]; Rh = Reg(); Ra = [Reg() for _ in range(NFF)]; Rr = Reg(); Rs = [Reg(), Reg()]

        def load_x(t):
            S.dma("sp", xb[t % 2][:], x_in(t).rearrange("(c p) t -> p c t", p=128), writes=[Rx[t % 2]])

        load_x(0)
        NPC = 4
        FPC = 6
        RWg = [Reg(True) for _ in range(NPC)]; RWu = [Reg(True) for _ in range(NPC)]; RWd = [Reg(True) for _ in range(2)]
        wg3 = wg.rearrange("(c p) f -> p c f", p=128); wu3 = wu.rearrange("(c p) f -> p c f", p=128); wd3 = wd.rearrange("(c p) f -> p c f", p=128)
        for pc in range(NPC):
            sl = slice(pc * FPC * 128, min((pc + 1) * FPC * 128, FF))
            S.dma("pool", Wg[:, :, sl], wg3[:, :, sl], writes=[RWg[pc]])
            S.dma("pool", Wu[:, :, sl], wu3[:, :, sl], writes=[RWu[pc]])
        for d2 in range(2):
            sl = slice(d2 * 512, (d2 + 1) * 512)
            S.dma("pool", Wd[:, :, sl], wd3[:, :, sl], writes=[RWd[d2]])

        def norm_pre(t):
            x = xb[t % 2]
            rmsnorm(lambda c: x[:, c, :], Rx[t % 2], 8, D, gpre, lambda c: hT[:, c, :], Rh, lambda c: hT[:, c, :], Rh, 0, rstd, Rr)

        if KF[0] == "1":
            norm_pre(0)
        for t in range(n_tiles):
            x = xb[t % 2]; Rxc = Rx[t % 2]
            if t + 1 < n_tiles:
                load_x(t + 1)
            if KF[0] != "1":
                norm_pre(t)
            for f in range(NFF):
                pg, pu = 1 + f % 2, 3 + f % 2
                for k in range(8):
                    mm(ps[pg][:], Wg[:, k, f * 128:(f + 1) * 128], hT[:, k, :], k == 0, k == 7, [RWg[f // FPC], Rh], [Rps[pg]], k == 7)
                for k in range(8):
                    mm(ps[pu][:], Wu[:, k, f * 128:(f + 1) * 128], hT[:, k, :], k == 0, k == 7, [RWu[f // FPC], Rh], [Rps[pu]], k == 7)
                S.op("act", ("activation", A(out=sil[f % 2][:], in_=ps[pg][:], func=AF.Silu)), reads=[Rps[pg]], writes=[Rs[f % 2]])
                S.op("dve", ("tensor_tensor", A(out=aT[:, f, :], in0=ps[pu][:], in1=sil[f % 2][:], op=ALU.mult)), reads=[Rps[pu], Rs[f % 2]], writes=[Ra[f]])
            for d in range(8):
                po = 5 + d % 2
                for f in range(NFF):
                    mm(ps[po][:], Wd[:, f, d * 128:(d + 1) * 128], aT[:, f, :], f == 0, f == NFF - 1, [RWd[d // 4], Ra[f]], [Rps[po]], f == NFF - 1)
                S.op("dve", ("scalar_tensor_tensor", A(out=x[:, d, :], in0=ps[po][:], scalar=0.5, in1=x[:, d, :], op0=ALU.mult, op1=ALU.add)),
                     reads=[Rps[po]], writes=[Rxc])
                if KF[0] == "1" and d == 1 and t + 1 < n_tiles:
                    norm_pre(t + 1)
            if final_norm:
                rmsnorm(lambda c: x[:, c, :], Rxc, 8, D, G_FIN, lambda c: aT[:, c, :], Ra[0], lambda c: x[:, c, :], Rxc, 0, rstd, Rr)
            S.dma("pool", x_out(t).rearrange("(c p) t -> p c t", p=128), x[:], reads=[Rxc], writes=out_tiles(t))
            if after_store is not None:
                after_store(t)
        S.barrier()


    def exchange(t):
        S.coll("AllReduce", ALU.add, groups, x1o[t], x1s[t], reads=[Rxo[t]], writes=[Rxsum[t]])

    ffn_phase(lambda t: xT[:, t * T:(t + 1) * T], NO, w1g, w1u, w1d, G_FFN1, lambda t: x1o[t], lambda t: [Rxo[t]], False, after_store=exchange)

    sb.off = base_off
    kvnT = sb.alloc([128, S_LEN], BF16)
    KT2 = sb.alloc([128, 2, S_LEN], BF16)
    bc_off = sb.off
    Rkvn = Reg(); Rkt = [Reg(), Reg()]; Rkr = Reg(); Rkz = Reg()
    S.op("pool", ("memset", A(KT2[96:128, :, :], 0.0)), writes=[Rkz])
    Win = sb.alloc([128, 8, C_GM], BF16)
    Wuq = sb.alloc([128, 2, 768], BF16)
    RWin, RWuq = Reg(True), Reg(True)
    load_w(Win, win.rearrange("(c p) f -> p c f", p=128)[:, :, 0:C_GM], RWin, 2, C_GM, 3)
    load_w(Wuq, wuq.rearrange("(c p) f -> p c f", p=128), RWuq, 1, 768, 3)
    xb = [sb.alloc([128, 8, T], F32) for _ in range(2)]
    posi = [sb.alloc([128, T], I32) for _ in range(2)]
    h2 = sb.alloc([128, 8, T], BF16)
    rstd = sb.alloc([128, T], F32)
    posf = sb.alloc([128, T], F32)
    trgb = [sb.alloc([128, 4, T], F32) for _ in range(2)]
    ta = sb.alloc([128, T], F32); tb = sb.alloc([128, T], F32)
    ckv = sb.alloc([128, 2, T], F32)
    sqb = sb.alloc([128, 2, T], BF16)
    cqn = sb.alloc([128, 2, T], BF16)
    xs = [sb.alloc([128, T], BF16) for _ in range(2)]
    t1 = [sb.alloc([128, T], F32) for _ in range(2)]
    t2 = [sb.alloc([128, T], F32) for _ in range(2)]
    dk_st = sb.alloc([128, 4, T], BF16)
    stg_off = sb.off
    xo_full = sb.alloc([128, 8, T], F32)
    q_st = nc.alloc_sbuf_tensor_at("q_st_alias", [96, 8, T], BF16, offset=stg_off)
    dq_st = nc.alloc_sbuf_tensor_at("dq_st_alias", [128, 4, T], BF16, offset=stg_off + 8 * T * 2)
    dv_st = sb.alloc([128, 4, 512], BF16)
    Rx = [Reg(), Reg()]; Rpi = [Reg(), Reg()]; Rh2 = Reg(); Rr = Reg(); Rpf = Reg(); Rtrgb = [Reg(), Reg()]; Rta = Reg(); Rtb = Reg()
    Rckv = Reg(); Rsqb = Reg(); Rcqn = Reg(); Rxs = [Reg(), Reg()]; Rt1 = [Reg(), Reg()]; Rt2 = [Reg(), Reg()]
    Rdk = Reg(); Rdq = Reg(); Rq = Reg(); Rdv = Reg()
    rope_i = [0]
    cur = {}
    h2b = [h2, sb.alloc([128, 8, T], BF16)]
    Rh2b = [Rh2, Reg()]
    pend_rope = []

    def flush_rope():
        while pend_rope:
            pend_rope.pop(0)()

    def rope(pbank, R, tab, outs):
        i = rope_i[0] % 2
        rope_i[0] += 1
        pw = 4 + i
        S.op("act", ("activation", A(out=xs[i][0:R, :], in_=ps[pbank][0:R, :], func=AF.Identity)), reads=[Rps[pbank]], writes=[Rxs[i]])
        r0 = min(o[2] for o in outs); r1 = max(o[3] for o in outs)
        S.op("dve", ("tensor_tensor", A(out=t1[i][r0:r1, :], in0=ps[pbank][r0:r1, :], in1=cur['trg'][r0:r1, 2 * tab, :], op=ALU.mult)), reads=[Rps[pbank], cur['Rtrg']], writes=[Rt1[i]])

        trg_, Rtrg_ = cur['trg'], cur['Rtrg']

        def stage_b():
            mm(ps[pw][0:R, :], cm[0:R, 1 + tab, 0:R], xs[i][0:R, :], True, True, [Rxs[i], Rc], [Rps[pw]], True)
            S.op("dve", ("tensor_tensor", A(out=t2[i][r0:r1, :], in0=ps[pw][r0:r1, :], in1=trg_[r0:r1, 2 * tab + 1, :], op=ALU.mult)), reads=[Rps[pw], Rtrg_], writes=[Rt2[i]])
            for (apf, Ro, a, b) in outs:
                S.op("pool", ("tensor_tensor", A(out=apf, in0=t1[i][a:b, :], in1=t2[i][a:b, :], op=ALU.add)), reads=[Rt1[i], Rt2[i]], writes=[Ro])
        if KF[1] == "1":
            pend_rope.append(stage_b)
        else:
            stage_b()

    Rxoh = Reg()
    pend_sub = []

    def load_b(t):
        if t < NO:
            S.dma("sp", xb[t % 2][:], x1o[t].rearrange("(c p) t -> p c t", p=128), writes=[Rx[t % 2]])
        else:
            j = t - NO
            S.dma("sp", xb[t % 2][:], x1s[j].rearrange("(c p) t -> p c t", p=128), reads=[Rxsum[j]], writes=[Rx[t % 2]])
            S.dma("sp", xo_full[:], x1o[j].rearrange("(c p) t -> p c t", p=128), writes=[Rxoh, Rq, Rdq])

            def sub(t=t):
                S.op("dve", ("tensor_tensor", A(out=xb[t % 2][:], in0=xb[t % 2][:], in1=xo_full[:], op=ALU.subtract)), reads=[Rxoh], writes=[Rx[t % 2]])
            pend_sub.append(sub)
        S.dma("sp", posi[t % 2][:], posr[:, t * T:(t + 1) * T], writes=[Rpi[t % 2]])

    load_b(0)
    pp_i = [0]

    def proj(cols, M, srcs, W, RW, nk):
        pb = 1 + pp_i[0] % 3
        pp_i[0] += 1
        for k in range(nk):
            mm(ps[pb][0:M, :], W[:, k, cols], srcs[0](k), k == 0, k == nk - 1, [RW, srcs[1]], [Rps[pb]], k == nk - 1)
        flush_rope()
        return pb

    def norm_b(t):
        x = xb[t % 2]
        hh = h2b[t % 2]
        rmsnorm(lambda c: x[:, c, :], Rx[t % 2], 8, D, G_MIX, lambda c: hh[:, c, :], Rh2b[t % 2], lambda c: hh[:, c, :], Rh2b[t % 2], 0, rstd, Rr)

    def trig_b(t):
        trg, Rtrg = trgb[t % 2], Rtrgb[t % 2]
        S.op("dve", ("tensor_copy", A(out=posf[:], in_=posi[t % 2][:])), reads=[Rpi[t % 2]], writes=[Rpf])
        for tab in range(2):
            for cs in range(2):
                S.op("dve", ("tensor_scalar", A(out=ta[:], in0=posf[:], scalar1=fst[:, tab:tab + 1], scalar2=(0.25 if cs == 0 else 0.0), op0=ALU.mult, op1=ALU.add)),
                     reads=[Rpf, Rc], writes=[Rta])
                S.op("dve", ("tensor_scalar", A(out=tb[:], in0=ta[:], scalar1=MAGIC, scalar2=None, op0=ALU.add)), reads=[Rta], writes=[Rtb])
                S.op("dve", ("scalar_tensor_tensor", A(out=ta[:], in0=tb[:], scalar=MAGIC, in1=ta[:], op0=ALU.subtract, op1=ALU.subtract)), reads=[Rtb, Rta], writes=[Rta])
                S.op("act", ("activation", A(out=trg[:, 2 * tab + cs, :], in_=ta[:], func=AF.Sin, scale=-2.0 * math.pi * 0.999999)), reads=[Rta], writes=[Rtrg])

    norm_b(0)
    trig_b(0)
    for t in range(NT):
        if t + 1 < NT:
            load_b(t + 1)
        cols_t = slice(t * T, (t + 1) * T)
        hh = h2b[t % 2]; Rhh = Rh2b[t % 2]
        cur['trg'] = trgb[t % 2]; cur['Rtrg'] = Rtrgb[t % 2]
        hsrc = (lambda k: hh[:, k, :], Rhh)
        is_own = t in own
        pb = proj(slice(C_CKV, C_CKV + 128), 128, hsrc, Win, RWin, 8)
        S.op("act", ("activation", A(out=ckv[:, 0, :], in_=ps[pb][:], func=AF.Identity)), reads=[Rps[pb]], writes=[Rckv])
        rmsnorm(lambda c: ckv[:, c, :], Rckv, 1, 128, G_KV, lambda c: sqb[:, c, :], Rsqb, lambda c: kvnT[:, cols_t], Rkvn, 7, rstd, Rr)
        if is_own:
            qi = own.index(t)
            cols_q = slice(qi * T, (qi + 1) * T)
            for c in range(2):
                pb = proj(slice(C_CQ + 128 * c, C_CQ + 128 * (c + 1)), 128, hsrc, Win, RWin, 8)
                S.op("act", ("activation", A(out=ckv[:, c, :], in_=ps[pb][:], func=AF.Identity)), reads=[Rps[pb]], writes=[Rckv])
            rmsnorm(lambda c: ckv[:, c, :], Rckv, 2, 256, G_Q, lambda c: sqb[:, c, :], Rsqb, lambda c: cqn[:, c, :], Rcqn, 7, rstd, Rr)
        pb = proj(slice(C_KR - 64, C_KR + 32), 96, hsrc, Win, RWin, 8)
        rope(pb, 96, 0, [(KT2[64:96, 0, cols_t], Rkr, 64, 96), (KT2[64:96, 1, cols_t], Rkr, 64, 96)])
        for j in range(4):
            pb = proj(slice(C_DK + 128 * j, C_DK + 128 * (j + 1)), 128, hsrc, Win, RWin, 8)
            rope(pb, 128, 1, [(dk_st[:, j, :], Rdk, 0, 128)])
        if t + 1 < NT:
            while pend_sub:
                pend_sub.pop(0)()
            norm_b(t + 1)
            trig_b(t + 1)
        if is_own:
            for j in range(4):
                pb = proj(slice(C_DQ + 128 * j, C_DQ + 128 * (j + 1)), 128, hsrc, Win, RWin, 8)
                rope(pb, 128, 1, [(dq_st[:, j, :], Rdq, 0, 128)])
            for h in range(8):
                pb = proj(slice(96 * h, 96 * (h + 1)), 96, (lambda k: cqn[:, k, :], Rcqn), Wuq, RWuq, 2)
                rope(pb, 96, 0, [(q_st[0:96, h, :], Rq, 0, 96)])
        for sub in range(4):
            pv = 6
            for k in range(8):
                mm(ps[pv][:], hh[:, k, sub * 128:(sub + 1) * 128], Win[:, k, C_DV:C_DV + 512], k == 0, k == 7, [Rhh, RWin], [Rps[pv]], k == 7)
            if sub == 0:
                flush_rope()
            S.op("act", ("activation", A(out=dv_st[:, sub, :], in_=ps[pv][:], func=AF.Identity)), reads=[Rps[pv]], writes=[Rdv])
        S.dma("pool", dvS[t * T:(t + 1) * T, :].rearrange("(s p) c -> p s c", p=128), dv_st[:], reads=[Rdv])
        flush_rope()
        S.dma("pool", dkT[:, :, cols_t].rearrange("j p t -> p j t"), dk_st[:], reads=[Rdk])
        if is_own:
            S.dma("pool", dqT[:, :, cols_q].rearrange("j p t -> p j t"), dq_st[:], reads=[Rdq])
            S.dma("pool", qTs[:, :, cols_q].rearrange("h r t -> r h t"), q_st[:], reads=[Rq])
    S.barrier()

    sb.off = bc_off
    WkK = sb.alloc([128, 8, 64], BF16); WkV = sb.alloc([128, 8, 64], BF16)
    RWk = Reg(True)
    wukv4 = wukv.rearrange("k (h two d) -> k h two d", two=2, d=64)
    S.dma("pool", WkK[:], wukv4[:, :, 0, :], writes=[RWk])
    S.dma("pool", WkV[:], wukv4[:, :, 1, :], writes=[RWk])
    NKT = S_LEN // 128
    c_off = sb.off
    Vh = [sb.alloc([128, NKT, 128], BF16) for _ in range(2)]
    QT = [sb.alloc([128, NO * T], BF16) for _ in range(2)]
    PT = [sb.alloc([128, T], BF16) for _ in range(4)]
    rc = sb.alloc([128, T], F32)
    bcs = sb.alloc([64, T], F32)
    yst = [sb.alloc([64, T], BF16) for _ in range(2)]
    RV = [Reg(), Reg()]; RQ = [Reg(), Reg()]; RPT = [Reg() for _ in range(4)]; Rrc = Reg(); Rbcs = Reg(); Ryst = [Reg(), Reg()]
    RQz = Reg()
    for b_ in range(2):
        S.op("pool", ("memset", A(Vh[b_][:], 1.0)), writes=[RV[b_]])
        S.op("pool", ("memset", A(QT[b_][96:128, :], 0.0)), writes=[RQz])

    def key_tiles(i):
        L = []
        for j in range(i):
            for s_ in range(4):
                L.append((j * 4 + s_, 0, 0))
            for s_ in range(4):
                L.append(((NO + j) * 4 + s_, 0, 0))
        for s_ in range(4):
            L.append((i * 4 + s_, 1, s_))
        for s_ in range(4):
            L.append(((NO + i) * 4 + s_, 2, 0))
        return L

    blocks = []
    for h in range(8):
        for qi in range(NO):
            L = key_tiles(qi)
            for n_, (kt, kind, sub) in enumerate(L):
                blocks.append((h, qi, kt, kind, sub, n_ == 0, n_ == len(L) - 1))
    LA = 2
    pending = []
    fin_i = [0]

    def mla_prologue_steps(h):
        b = h % 2
        steps = []

        def kstep(g):
            mm(ps[7][0:64, :], WkK[:, h, :], kvnT[:, g * T:(g + 1) * T], True, True, [Rkvn, RWk], [Rps[7]], True)
            S.op("dve", ("tensor_copy", A(out=KT2[0:64, b, g * T:(g + 1) * T], in_=ps[7][0:64, :])), reads=[Rps[7]], writes=[Rkt[b]])

        def vstep(g8):
            for j in range(8):
                kt = g8 * 8 + j
                mm(ps[7][:, j * 64:(j + 1) * 64], kvnT[:, kt * 128:(kt + 1) * 128], WkV[:, h, :], True, True, [Rkvn, RWk], [Rps[7]], j == 7)
            S.op("dve", ("tensor_copy", A(out=Vh[b][:, g8 * 8:(g8 + 1) * 8, 0:64], in_=ps[7][:].rearrange("p (j d) -> p j d", d=64))), reads=[Rps[7]], writes=[RV[b]])

        steps.append(lambda: S.dma("sp", QT[b][0:96, :], qTs[h], writes=[RQ[b]]))
        for g in range(S_LEN // T):
            steps.append(lambda g=g: kstep(g))
        for g8 in range(NKT // 8):
            steps.append(lambda g8=g8: vstep(g8))
        return steps

    pro_steps = []

    def mla_head_prologue(h):
        while pro_steps:
            pro_steps.pop(0)()
        if h == 0:
            for st_ in mla_prologue_steps(0):
                st_()
        if h + 1 < 8:
            pro_steps.extend(mla_prologue_steps(h + 1))

    units = []
    for h in range(8):
        for qi in range(NO):
            L = key_tiles(qi)
            n_ = 0
            while n_ < len(L):
                kt, kind, sub = L[n_]
                if kind != 1 and n_ + 1 < len(L) and L[n_ + 1][1] == kind:
                    blks = [L[n_], L[n_ + 1]]
                    n_ += 2
                else:
                    blks = [L[n_]]
                    n_ += 1
                units.append((h, qi, kind, blks, blks[0] is L[0], blks[-1] is L[-1]))
    PT2 = [sb.alloc([128, 2, T], BF16) for _ in range(3)]
    RPT2 = [Reg() for _ in range(3)]

    def mla_S(u):
        h, qi, kind, blks, first, last = units[u]
        if qi == 0 and first:
            mla_head_prologue(h)
        b = h % 2
        a = (u % 2) * 2
        pb = u % 3
        sub = blks[0][2]
        q0 = 128 * sub
        w = T - q0
        nb = len(blks)
        for bi, (kt, _, _) in enumerate(blks):
            mm(ps[a + bi][:, 0:w], KT2[:, b, kt * 128:(kt + 1) * 128], QT[b][:, qi * T + q0:(qi + 1) * T], True, True,
               [Rkt[b], Rkr, Rkz, RQ[b], RQz], [Rps[a + bi]], True)
        kw = dict(func=AF.Exp, scale=SC_M)
        rd = [Rps[a + bi] for bi in range(nb)]
        if kind == 2:
            kw["bias"] = fbias
            rd.append(Rl)
        S.op("act", ("activation", A(out=PT2[pb][:, 0:nb, 0:w], in_=pspair(a)[:, 0:nb, 0:w], **kw)), reads=rd, writes=[RPT2[pb]])
        if kind == 1:
            S.op("dve", ("tensor_tensor", A(out=PT2[pb][:, 0, 0:128], in0=PT2[pb][:, 0, 0:128], in1=cm[:, 0, :], op=ALU.mult)), reads=[RPT2[pb], Rc], writes=[RPT2[pb]])

    def mla_PV(u):
        h, qi, kind, blks, first, last = units[u]
        pb = u % 3
        sub = blks[0][2]
        q0 = 128 * sub
        w = T - q0
        ob = 4 + fin_i[0] % 2
        nb = len(blks)
        for bi, (kt, _, _) in enumerate(blks):
            st = first and bi == 0
            sp_ = last and bi == nb - 1
            mm(ps[ob][:, q0:T], Vh[h % 2][:, kt, :], PT2[pb][:, bi, 0:w], st, sp_, [RV[h % 2], RPT2[pb]], [Rps[ob]], sp_ or bi == nb - 1)
        if last:
            f = fin_i[0] % 2
            fin_i[0] += 1
            S.op("act", ("activation", A(out=rc[64:65, :], in_=ps[ob][64:65, :], func=AF.Ln)), reads=[Rps[ob]], writes=[Rrc])
            S.op("act", ("activation", A(out=rc[64:65, :], in_=rc[64:65, :], func=AF.Exp, scale=-1.0)), reads=[Rrc], writes=[Rrc])

            def fin_pe(ob=ob, f=f, h=h, qi=qi):
                mm(ps[6][0:64, :], ones_f[64:65, 0:64], rc[64:65, :], True, True, [Rrc, Rc], [Rps[6]], True)
                S.op("dve", ("tensor_copy", A(out=bcs[:], in_=ps[6][0:64, :])), reads=[Rps[6]], writes=[Rbcs])
                S.op("dve", ("tensor_tensor", A(out=yst[f][:], in0=ps[ob][0:64, :], in1=bcs[:], op=ALU.mult)), reads=[Rps[ob], Rbcs], writes=[Ryst[f]])
                S.dma("pool", ymT[h * 64:(h + 1) * 64, qi * T:(qi + 1) * T], yst[f][:], reads=[Ryst[f]])
            pending.append([2, fin_pe])

    def run_pending(force=False):
        for p in list(pending):
            p[0] -= 1
            if p[0] <= 0 or force:
                p[1]()
                pending.remove(p)

    n = len(units)
    for u in range(n + 1):
        if u < n:
            mla_S(u)
        if u - 1 >= 0:
            mla_PV(u - 1)
        run_pending()
        if pro_steps and u % 2 == 1:
            pro_steps.pop(0)()
    run_pending(force=True)
    S.barrier()

    sb.off = c_off
    KTd = [sb.alloc([128, S_LEN], BF16) for _ in range(2)]
    Vd = [sb.alloc([128, NKT, 128], BF16) for _ in range(2)]
    QTd = [sb.alloc([128, NO * T], BF16) for _ in range(2)]
    P12 = [sb.alloc([128, 2, T], BF16) for _ in range(3)]
    r1 = sb.alloc([128, T], F32); r2 = sb.alloc([128, T], F32)
    u1 = sb.alloc([128, T], F32); u2 = sb.alloc([128, T], F32)
    dd = sb.alloc([128, T], F32); sqd = sb.alloc([128, T], BF16); rsd = sb.alloc([128, T], F32)
    ysd = [sb.alloc([128, T], BF16) for _ in range(2)]
    RKd = [Reg(), Reg()]; RVd = [Reg(), Reg()]; RQd = [Reg(), Reg()]
    RP12 = [Reg() for _ in range(3)]
    Rr1, Rr2, Ru1, Ru2, Rdd, Rsqd, Rrsd = [Reg() for _ in range(7)]
    Rysd = [Reg(), Reg()]
    dblocks = []
    for g in range(4):
        for qi in range(NO):
            L = key_tiles(qi)
            for n_, (kt, kind, sub) in enumerate(L):
                dblocks.append((g, qi, kt, kind, sub, n_ == 0, n_ == len(L) - 1))
    dfin = [0]

    def diff_prologue(g):
        b = g % 2
        S.dma("sp", KTd[b][:], dkT[g], writes=[RKd[b]])
        S.dma("sp", Vd[b][:], dvS[:, g * 128:(g + 1) * 128].rearrange("(k p) d -> p k d", p=128), writes=[RVd[b]])
        S.dma("sp", QTd[b][:], dqT[g], writes=[RQd[b]])

    def diff_S(i):
        g, qi, kt, kind, sub, first, last = dblocks[i]
        if qi == 0 and first and g == 0:
            diff_prologue(0)
        b = g % 2
        q0 = 128 * sub
        w = T - q0
        a = (i % 2) * 2
        pb = i % 3
        mm(ps[a][:, 0:w], KTd[b][0:64, kt * 128:(kt + 1) * 128], QTd[b][0:64, qi * T + q0:(qi + 1) * T], True, True, [RKd[b], RQd[b]], [Rps[a]], True)
        mm(ps[a + 1][:, 0:w], KTd[b][64:128, kt * 128:(kt + 1) * 128], QTd[b][64:128, qi * T + q0:(qi + 1) * T], True, True, [RKd[b], RQd[b]], [Rps[a + 1]], True)
        kw = dict(func=AF.Exp, scale=SC_D)
        rd = [Rps[a], Rps[a + 1]]
        if kind == 2:
            kw["bias"] = fbias
            rd.append(Rl)
        S.op("act", ("activation", A(out=P12[pb][:, :, 0:w], in_=pspair(a)[:, :, 0:w], **kw)), reads=rd, writes=[RP12[pb]])
        if kind == 1:
            for k_ in range(2):
                S.op("dve", ("tensor_tensor", A(out=P12[pb][:, k_, 0:128], in0=P12[pb][:, k_, 0:128], in1=cm[:, 0, :], op=ALU.mult)), reads=[RP12[pb], Rc], writes=[RP12[pb]])

    def diff_PV(i):
        g, qi, kt, kind, sub, first, last = dblocks[i]
        b = g % 2
        q0 = 128 * sub
        w = T - q0
        pb = i % 3
        st, sp_ = first, last
        mm(ps[4][:, q0:T], Vd[b][:, kt, :], P12[pb][:, 0, 0:w], st, sp_, [RVd[b], RP12[pb]], [Rps[4]], sp_)
        mm(ps[6][:, q0:T], ones_bf[:, :], P12[pb][:, 0, 0:w], st, sp_, [Rc, RP12[pb]], [Rps[6]], sp_)
        mm(ps[5][:, q0:T], Vd[b][:, kt, :], P12[pb][:, 1, 0:w], st, sp_, [RVd[b], RP12[pb]], [Rps[5]], sp_)
        mm(ps[7][:, q0:T], ones_bf[:, :], P12[pb][:, 1, 0:w], st, sp_, [Rc, RP12[pb]], [Rps[7]], True)
        if sp_:
            f = dfin[0] % 2
            dfin[0] += 1
            S.op("dve", ("tensor_copy", A(out=u1[:], in_=ps[4][:])), reads=[Rps[4]], writes=[Ru1])
            S.op("dve", ("tensor_copy", A(out=u2[:], in_=ps[5][:])), reads=[Rps[5]], writes=[Ru2])
            S.op("act", ("activation", A(out=r1[:], in_=ps[6][:], func=AF.Ln)), reads=[Rps[6]], writes=[Rr1])
            S.op("act", ("activation", A(out=r2[:], in_=ps[7][:], func=AF.Ln)), reads=[Rps[7]], writes=[Rr2])
            S.op("act", ("activation", A(out=r1[:], in_=r1[:], func=AF.Exp, scale=-1.0)), reads=[Rr1], writes=[Rr1])
            S.op("act", ("activation", A(out=r2[:], in_=r2[:], func=AF.Exp, scale=-1.0)), reads=[Rr2], writes=[Rr2])
            S.op("dve", ("tensor_tensor", A(out=u1[:], in0=u1[:], in1=r1[:], op=ALU.mult)), reads=[Rr1], writes=[Ru1])
            S.op("dve", ("tensor_tensor", A(out=u2[:], in0=u2[:], in1=r2[:], op=ALU.mult)), reads=[Rr2], writes=[Ru2])
            S.op("dve", ("scalar_tensor_tensor", A(out=dd[:], in0=u2[:], scalar=neglam, in1=u1[:], op0=ALU.mult, op1=ALU.add)), reads=[Ru1, Ru2, Rl], writes=[Rdd])
            S.op("pool", ("tensor_tensor", A(out=sqd[:], in0=dd[:], in1=dd[:], op=ALU.mult)), reads=[Rdd], writes=[Rsqd])

            def fin_pe(f=f, g=g, qi=qi):
                mm(ps[6][:], ones_bf[:, :], sqd[:], True, True, [Rsqd, Rc], [Rps[6]], True)
                S.op("act", ("activation", A(out=rsd[:], in_=ps[6][:], func=AF.Ln, bias=EPS, scale=1.0 / 128.0)), reads=[Rps[6]], writes=[Rrsd])
                S.op("act", ("activation", A(out=rsd[:], in_=rsd[:], func=AF.Exp, scale=-0.5)), reads=[Rrsd], writes=[Rrsd])
                S.op("dve", ("scalar_tensor_tensor", A(out=ysd[f][:], in0=dd[:], scalar=sgain, in1=rsd[:], op0=ALU.mult, op1=ALU.mult)), reads=[Rdd, Rrsd, Rl], writes=[Rysd[f]])
                S.dma("pool", ydT[g * 128:(g + 1) * 128, qi * T:(qi + 1) * T], ysd[f][:], reads=[Rysd[f]])
            fin_pe()

    n = len(dblocks)
    for i in range(n + 1):
        if i < n:
            diff_S(i)
        if i - 1 >= 0:
            diff_PV(i - 1)
        if i < n and dblocks[i][1] == 0 and dblocks[i][5] and dblocks[i][0] + 1 < 4:
            diff_prologue(dblocks[i][0] + 1)
    S.barrier()

    sb.off = base_off
    WgM = sb.alloc([128, 8, D], BF16); WgD = sb.alloc([128, 8, D], BF16)
    Wpm = sb.alloc([128, 4, D], BF16); Wpd = sb.alloc([128, 4, D], BF16); Wo = sb.alloc([128, 8, D], BF16)
    RW1 = Reg(True)
    winr = win.rearrange("(c p) f -> p c f", p=128)
    load_w(WgM, winr[:, :, C_GM:C_GM + D], RW1, 1, D, 3)
    load_w(WgD, winr[:, :, C_GD:C_GD + D], RW1, 1, D, 3)
    load_w(Wpm, wpm.rearrange("(c p) f -> p c f", p=128), RW1, 1, D, 3)
    load_w(Wpd, wpd.rearrange("(c p) f -> p c f", p=128), RW1, 1, D, 3)
    load_w(Wo, wout.rearrange("(c p) f -> p c f", p=128), RW1, 1, D, 3)
    xb = [sb.alloc([128, 8, T], F32) for _ in range(2)]
    ym = [sb.alloc([128, 4, T], BF16) for _ in range(2)]
    yd = [sb.alloc([128, 4, T], BF16) for _ in range(2)]
    h2 = sb.alloc([128, 8, T], BF16)
    rstd = sb.alloc([128, T], F32)
    sg1 = sb.alloc([128, T], F32); sg2 = sb.alloc([128, T], F32)
    m1 = sb.alloc([128, T], F32); m2 = sb.alloc([128, T], F32)
    mg = sb.alloc([128, 8, T], BF16)
    Rx = [Reg(), Reg()]; Rym = [Reg(), Reg()]; Ryd = [Reg(), Reg()]; Rh2 = Reg(); Rr = Reg()
    Rsg1, Rsg2, Rm1, Rm2 = Reg(), Reg(), Reg(), Reg()
    Rmg = [Reg() for _ in range(8)]

    def load_d(qi):
        t = own[qi]
        S.dma("sp", xb[qi % 2][:], x1o[qi].rearrange("(c p) t -> p c t", p=128), writes=[Rx[qi % 2]])
        S.dma("sp", ym[qi % 2][:], ymT[:, qi * T:(qi + 1) * T].rearrange("(c p) t -> p c t", p=128), writes=[Rym[qi % 2]])
        S.dma("sp", yd[qi % 2][:], ydT[:, qi * T:(qi + 1) * T].rearrange("(c p) t -> p c t", p=128), writes=[Ryd[qi % 2]])

    load_d(0)
    h2d = [h2, sb.alloc([128, 8, T], BF16)]
    Rh2d = [Rh2, Reg()]
    sgs = [sg1, sg2, sb.alloc([128, T], F32), sb.alloc([128, T], F32)]
    Rsgs = [Rsg1, Rsg2, Reg(), Reg()]
    mms = [m1, m2, sb.alloc([128, T], F32), sb.alloc([128, T], F32)]
    Rmms = [Rm1, Rm2, Reg(), Reg()]
    bank_i = [0]

    def nbank():
        b_ = 1 + bank_i[0] % 6
        bank_i[0] += 1
        return b_

    def norm_d(qi):
        x_ = xb[qi % 2]
        hh_ = h2d[qi % 2]
        rmsnorm(lambda c: x_[:, c, :], Rx[qi % 2], 8, D, G_MIX, lambda c: hh_[:, c, :], Rh2d[qi % 2], lambda c: hh_[:, c, :], Rh2d[qi % 2], 0, rstd, Rr)

    norm_d(0)
    for qi in range(NO):
        x = xb[qi % 2]; Rxc = Rx[qi % 2]
        hh = h2d[qi % 2]; Rhh = Rh2d[qi % 2]
        if qi + 1 < NO:
            load_d(qi + 1)
        for d in range(8):
            dsl = slice(d * 128, (d + 1) * 128)
            o_ = (d % 2) * 2
            bg1, bp1, bg2, bp2 = (nbank(), nbank(), nbank(), nbank()) if KF[2] == "1" else (1, 3, 2, 4)
            for k in range(8):
                mm(ps[bg1][:], WgM[:, k, dsl], hh[:, k, :], k == 0, k == 7, [RW1, Rhh], [Rps[bg1]], k == 7)
            S.op("act", ("activation", A(out=sgs[o_][:], in_=ps[bg1][:], func=AF.Sigmoid)), reads=[Rps[bg1]], writes=[Rsgs[o_]])
            for k in range(4):
                mm(ps[bp1][:], Wpm[:, k, dsl], ym[qi % 2][:, k, :], k == 0, k == 3, [RW1, Rym[qi % 2]], [Rps[bp1]], k == 3)
            S.op("dve", ("tensor_tensor", A(out=mms[o_][:], in0=ps[bp1][:], in1=sgs[o_][:], op=ALU.mult)), reads=[Rps[bp1], Rsgs[o_]], writes=[Rmms[o_]])
            for k in range(8):
                mm(ps[bg2][:], WgD[:, k, dsl], hh[:, k, :], k == 0, k == 7, [RW1, Rhh], [Rps[bg2]], k == 7)
            S.op("act", ("activation", A(out=sgs[o_ + 1][:], in_=ps[bg2][:], func=AF.Sigmoid)), reads=[Rps[bg2]], writes=[Rsgs[o_ + 1]])
            for k in range(4):
                mm(ps[bp2][:], Wpd[:, k, dsl], yd[qi % 2][:, k, :], k == 0, k == 3, [RW1, Ryd[qi % 2]], [Rps[bp2]], k == 3)
            S.op("dve", ("tensor_tensor", A(out=mms[o_ + 1][:], in0=ps[bp2][:], in1=sgs[o_ + 1][:], op=ALU.mult)), reads=[Rps[bp2], Rsgs[o_ + 1]], writes=[Rmms[o_ + 1]])
            S.op("pool", ("tensor_tensor", A(out=mg[:, d, :], in0=mms[o_][:], in1=mms[o_ + 1][:], op=ALU.add)), reads=[Rmms[o_], Rmms[o_ + 1]], writes=[Rmg[d]])
        if qi + 1 < NO:
            norm_d(qi + 1)
        for d in range(8):
            po = nbank() if KF[2] == "1" else 5 + d % 2
            for k in range(8):
                mm(ps[po][:], Wo[:, k, d * 128:(d + 1) * 128], mg[:, k, :], k == 0, k == 7, [RW1, Rmg[k]], [Rps[po]], k == 7)
            S.op("dve", ("tensor_tensor", A(out=x[:, d, :], in0=ps[po][:], in1=x[:, d, :], op=ALU.add)), reads=[Rps[po]], writes=[Rxc])
        S.dma("pool", x2T[:, qi * T:(qi + 1) * T].rearrange("(c p) t -> p c t", p=128), x[:], reads=[Rxc])
    S.barrier()

    ffn_phase(lambda t: x2T[:, t * T:(t + 1) * T], NO, w2g, w2u, w2d, G_FFN2, lambda t: outT[:, t * T:(t + 1) * T], lambda t: [], True)
    S.emit()
    return nc


def const_tables():
    tri = np.zeros((128, 128), np.float32)
    for p in range(128):
        tri[p, p:] = 1.0
    pmM = np.zeros((128, 128), np.float32)
    for j in range(16):
        pmM[64 + j + 16, 64 + j] = -1.0
        pmM[64 + j, 64 + j + 16] = 1.0
    pmD = np.zeros((128, 128), np.float32)
    for b in (0, 64):
        for j in range(8):
            pmD[b + j + 8, b + j] = -1.0
            pmD[b + j, b + j + 8] = 1.0
    cmat = np.stack([tri, pmM, pmD], axis=1)
    fsc = np.zeros((128, 2), np.float64)
    for j in range(32):
        fsc[64 + j, 0] = THETA ** (-(j % 16) / 16.0) / (2 * math.pi)
    for b in (0, 64):
        for j in range(16):
            fsc[b + j, 1] = THETA ** (-(j % 8) / 8.0) / (2 * math.pi)
    return np.ascontiguousarray(cmat), fsc.astype(np.float32)


def col8(v, n):
    return np.ascontiguousarray(np.asarray(v, np.float32).reshape(n, 128).T)


def prep_shared(inp):
    f = lambda k: np.ascontiguousarray(np.asarray(inp[k], np.float32)[0])
    cmat, fsc = const_tables()
    gv = np.zeros((128, 36), np.float32)
    gv[:, 0:8] = col8(inp["ffn1_norm"][0], 8)
    gv[:, 8:16] = col8(inp["mix_norm"][0], 8)
    gv[:, 16:24] = col8(inp["ffn2_norm"][0], 8)
    gv[:, 24:32] = col8(inp["final_norm"], 8)
    gv[:, 32:34] = col8(inp["mla_q_norm"][0], 2)
    gv[:, 34:35] = col8(inp["mla_kv_norm"][0], 1)
    gv[:, 35:36] = col8(inp["diff_subln"][0], 1)
    lam = np.concatenate([np.asarray(inp[k], np.float32)[0] for k in ("diff_lam_q1", "diff_lam_k1", "diff_lam_q2", "diff_lam_k2")])
    lamr = np.ascontiguousarray(np.broadcast_to(lam[None, :], (128, 256)))
    return {
        "gv": gv, "lamr": lamr, "fsc": fsc, "cmat": cmat,
        "w1g": f("ffn1_w_gate"), "w1u": f("ffn1_w_up"), "w1d": f("ffn1_w_down"),
        "w2g": f("ffn2_w_gate"), "w2u": f("ffn2_w_up"), "w2d": f("ffn2_w_down"),
        "win": f("w_in"), "wuq": f("mla_w_uq"), "wukv": f("mla_w_ukv"),
        "wpm": f("w_proj_mla"), "wpd": f("w_proj_diff"), "wout": f("w_out"),
    }


def core_perm(half, NT):
    NO = NT // 2
    own = [2 * i + half for i in range(NO)]
    oth = [2 * i + 1 - half for i in range(NO)]
    return own, oth


def make_in_maps(inputs, n_cores):
    x = np.asarray(inputs["x"], np.float32)
    pos = np.asarray(inputs["positions"], np.int32)
    B, S_LEN, _ = x.shape
    NT = S_LEN // T
    shared = prep_shared(inputs)
    in_maps, owns = [], []
    for core in range(n_cores):
        b, half = core // 2, core % 2
        own, oth = core_perm(half, NT)
        idx = np.concatenate([np.arange(c * T, (c + 1) * T) for c in own + oth])
        m = dict(shared)
        m["xT"] = np.ascontiguousarray(x[b][idx].T)
        m["posr"] = np.ascontiguousarray(np.broadcast_to(pos[b][idx][None, :], (128, S_LEN)))
        m["flag"] = np.full((128, 1), float(half), np.float32)
        in_maps.append(m)
        owns.append(own)
    return in_maps, owns


def gather_out(results, owns, B, S_LEN):
    out = np.empty((B, S_LEN, D), np.float32)
    for core, r in enumerate(results):
        b = core // 2
        oT = np.asarray(r["outT"])
        for qi, c in enumerate(owns[core]):
            out[b, c * T:(c + 1) * T, :] = oT[:, qi * T:(qi + 1) * T].T
    return out


def kernel(**inputs):
    x = np.asarray(inputs["x"])
    B, S_LEN, _ = x.shape
    in_maps, owns = make_in_maps(inputs, 2 * B)
    nc = build(S_LEN, 2 * B)
    res = run_bass_kernel_spmd(nc, in_maps, core_ids=list(range(2 * B)))
    return gather_out(res.results, owns, B, S_LEN)
```

```python
import math
import os
import numpy as np
import concourse.bass as bass
import concourse.mybir as mybir
from concourse.bass_utils import run_bass_kernel_spmd

F32 = mybir.dt.float32
BF16 = mybir.dt.bfloat16
I32 = mybir.dt.int32
AF = mybir.ActivationFunctionType
ALU = mybir.AluOpType
AX = mybir.AxisListType

D = 1024
FF = 2816
NFF = FF // 128
T = 512
EPS = 1e-6
THETA = 500000.0
N_IN = 4000
C_CQ, C_CKV, C_KR, C_DQ, C_DK, C_DV, C_GM, C_GD = 0, 256, 384, 416, 928, 1440, 1952, 2976
LAM_INIT = 0.8 - 0.6 * math.exp(-0.3 * 0.0)
SC_M = 96.0 ** -0.5
SC_D = 64.0 ** -0.5
MAGIC = 12582912.0
OWN_A = [0, 3, 4, 7, 8, 11, 12, 15]
OWN_B = [1, 2, 5, 6, 9, 10, 13, 14]

ENGS = ("sp", "act", "dve", "pool", "pe")
KF = os.environ.get("KOPT", "1111")


def A(*a, **k):
    return (a, k)


class Reg:
    __slots__ = ("w", "r", "const", "excl")

    def __init__(self, const=False, excl=False):
        self.w = None
        self.r = []
        self.const = const
        self.excl = excl


class Sched:
    def __init__(self, nc, n_dma_sems=12):
        self.nc = nc
        self.q = {e: [] for e in ENGS}
        self.cnt = {e: 0 for e in ENGS}
        self.seen = {e: {} for e in ENGS}
        self.sem = {e: nc.alloc_semaphore("s_" + e) for e in ENGS}
        self.dma_sems = {}
        self.n_dma_sems = n_dma_sems
        self.dma_rr = {}
        self.dma_val = {}

    def _wait(self, eng, tok):
        if tok is None:
            return
        key, val = tok
        if key == eng and val > self.cnt[eng]:
            return
        if self.seen[eng].get(key, 0) >= val:
            return
        self.seen[eng][key] = val
        sem = self.sem[key] if key in self.sem else self.dma_sems[key]
        self.q[eng].append(lambda e, sem=sem, val=val: e.wait_ge(sem, val))

    def _deps(self, reads, writes, deps):
        toks = list(deps)
        for r in reads:
            if r.w is not None:
                toks.append(r.w)
            if r.excl:
                toks.extend(r.r)
        for w in writes:
            if w.w is not None:
                toks.append(w.w)
            toks.extend(w.r)
        return toks

    @staticmethod
    def _merge(toks):
        best = {}
        for t in toks:
            if t is None:
                continue
            if best.get(t[0], 0) < t[1]:
                best[t[0]] = t[1]
        return list(best.items())

    def _record(self, tok, reads, writes):
        for r in reads:
            if r.const:
                continue
            r.r.append(tok)
            if len(r.r) > 16:
                best = {}
                for k, v in r.r:
                    if best.get(k, 0) < v:
                        best[k] = v
                r.r = list(best.items())
        for w in writes:
            w.w = tok
            w.r = []

    def _need(self, eng, tok):
        key, val = tok
        if key == eng and val > self.cnt[eng]:
            return None
        if self.seen[eng].get(key, 0) >= val:
            return None
        self.seen[eng][key] = val
        sem = self.sem[key] if key in self.sem else self.dma_sems[key]
        return (sem, val)

    def op(self, eng, fn, reads=(), writes=(), sig=True, deps=()):
        if isinstance(fn, tuple):
            m_, (a_, k_) = fn
            fn = lambda e, m_=m_, a_=a_, k_=k_: getattr(e, m_)(*a_, **k_)
        needs = []
        for t in self._merge(self._deps(reads, writes, deps)):
            n_ = self._need(eng, t)
            if n_ is not None:
                needs.append(n_)
        for (sem_, val_) in needs[:-1]:
            self.q[eng].append(lambda e, sem_=sem_, val_=val_: e.wait_ge(sem_, val_))
        att = needs[-1] if needs else None
        if sig:
            self.cnt[eng] += 1
            sem = self.sem[eng]
            if att is not None:
                self.q[eng].append(lambda e, fn=fn, sem=sem, att=att: fn(e)._wait_ge(att[0], att[1]).then_inc(sem, 1))
            else:
                self.q[eng].append(lambda e, fn=fn, sem=sem: fn(e).then_inc(sem, 1))
            tok = (eng, self.cnt[eng])
        else:
            if att is not None:
                self.q[eng].append(lambda e, fn=fn, att=att: fn(e)._wait_ge(att[0], att[1]))
            else:
                self.q[eng].append(lambda e, fn=fn: fn(e))
            tok = (eng, self.cnt[eng] + 1)
        self._record(tok, reads, writes)
        return tok

    def dma(self, eng, out, in_, reads=(), writes=(), deps=()):
        toks = self._deps(reads, writes, deps)
        rr = self.dma_rr.get(eng, 0)
        self.dma_rr[eng] = (rr + 1) % (6 if (eng == "pool" and KF[3] == "1") else self.n_dma_sems)
        key = "dma_%s_%d" % (eng, rr)
        if key not in self.dma_sems:
            self.dma_sems[key] = self.nc.alloc_semaphore(key)
            self.dma_val[key] = 0
        if self.dma_val[key] > 0:
            toks.append((key, self.dma_val[key]))
        for t in toks:
            self._wait(eng, t)
        self.dma_val[key] += 16
        sem = self.dma_sems[key]
        self.q[eng].append(lambda e, out=out, in_=in_, sem=sem: e.dma_start(out=out, in_=in_).then_inc(sem, 16))
        tok = (key, self.dma_val[key])
        self._record(tok, reads, writes)
        return tok

    def coll(self, kind, op, groups, in_ap, out_ap, reads=(), writes=()):
        eng = "pool"
        toks = self._deps(reads, writes, ())
        i = self.dma_rr.get("cc", 0)
        self.dma_rr["cc"] = i + 1
        key = "cc_%d" % (i % 8)
        if key not in self.dma_sems:
            self.dma_sems[key] = self.nc.alloc_semaphore(key)
            self.dma_val[key] = 0
        if self.dma_val[key] > 0:
            toks.append((key, self.dma_val[key]))
        for t in self._merge(toks):
            self._wait(eng, t)
        self.dma_val[key] += 1
        sem = self.dma_sems[key]
        self.q[eng].append(lambda e: e.collective_compute(kind, op, replica_groups=groups, ins=[in_ap], outs=[out_ap]).then_inc(sem, 1))
        tok = (key, self.dma_val[key])
        self._record(tok, reads, writes)
        return tok

    def barrier(self):
        toks = [(e, self.cnt[e]) for e in ENGS if self.cnt[e] > 0]
        toks += [(k, v) for k, v in self.dma_val.items() if v > 0]
        for e in ENGS:
            for t in toks:
                self._wait(e, t)

    def emit(self):
        q = self.q
        with self.nc.Block() as block:
            @block.sync
            def _(e):
                for f in q["sp"]:
                    f(e)

            @block.scalar
            def _(e):
                for f in q["act"]:
                    f(e)

            @block.vector
            def _(e):
                for f in q["dve"]:
                    f(e)

            @block.gpsimd
            def _(e):
                for f in q["pool"]:
                    f(e)

            @block.tensor
            def _(e):
                for f in q["pe"]:
                    f(e)


class SB:
    def __init__(self, nc):
        self.nc = nc
        self.off = 16512
        self.top = 229344
        self.n = 0

    def alloc(self, shape, dt):
        n = 1
        for s in shape[1:]:
            n *= s
        nb = n * (2 if dt == BF16 else 4)
        nb = (nb + 31) // 32 * 32
        t = self.nc.alloc_sbuf_tensor_at("t%d" % self.n, list(shape), dt, offset=self.off)
        self.n += 1
        self.off += nb
        assert self.off <= self.top, "SBUF overflow %d" % self.off
        return t


def build(S_LEN, n_cores=8):
    NT = S_LEN // T
    NO = NT // 2
    own = list(range(NO))
    nc = bass.Bass("TRN2", target_bir_lowering=False)

    def din(name, shape, dt=F32):
        return nc.dram_tensor(name, list(shape), dt, kind="ExternalInput").ap()

    def dscr(name, shape, dt):
        return nc.dram_tensor(name, list(shape), dt, kind="Internal").ap()

    xT = din("xT", [D, S_LEN])
    posr = din("posr", [128, S_LEN], I32)
    gv = din("gv", [128, 36])
    lamr = din("lamr", [128, 256])
    fsc = din("fsc", [128, 2])
    cmat = din("cmat", [128, 3, 128])
    flag = din("flag", [128, 1])
    w1g = din("w1g", [D, FF]); w1u = din("w1u", [D, FF]); w1d = din("w1d", [FF, D])
    w2g = din("w2g", [D, FF]); w2u = din("w2u", [D, FF]); w2d = din("w2d", [FF, D])
    win = din("win", [D, N_IN])
    wuq = din("wuq", [256, 768]); wukv = din("wukv", [128, 1024])
    wpm = din("wpm", [512, D]); wpd = din("wpd", [512, D]); wout = din("wout", [D, D])
    outT = nc.dram_tensor("outT", [D, NO * T], F32, kind="ExternalOutput").ap()

    x1o = dscr("x1o", [NO, D, T], F32)
    x1s = dscr("x1s", [NO, D, T], F32)
    groups = [[2 * i, 2 * i + 1] for i in range(n_cores // 2)]
    Rxo = [Reg() for _ in range(NO)]; Rxsum = [Reg() for _ in range(NO)]
    dkT = dscr("dkT", [4, 128, S_LEN], BF16)
    dvS = dscr("dvS", [S_LEN, 512], BF16)
    qTs = dscr("qTs", [8, 96, NO * T], BF16)
    dqT = dscr("dqT", [4, 128, NO * T], BF16)
    ymT = dscr("ymT", [512, NO * T], BF16)
    ydT = dscr("ydT", [512, NO * T], BF16)
    x2T = dscr("x2T", [D, NO * T], F32)

    S = Sched(nc)
    sb = SB(nc)
    psall = nc.alloc_psum_tensor("psall", [128, 8 * T], F32)
    ps = [psall[:, i * T:(i + 1) * T] for i in range(8)]

    def pspair(a):
        return psall[:, a * T:(a + 2) * T].rearrange("p (b t) -> p b t", t=T)
    Rps = [Reg(excl=True) for _ in range(8)]

    ones_bf = sb.alloc([128, 128], BF16)
    ones_f = sb.alloc([128, 64], F32)
    cm = sb.alloc([128, 3, 128], BF16)
    gvt = sb.alloc([128, 36], F32)
    fst = sb.alloc([128, 2], F32)
    lamt = sb.alloc([128, 256], F32)
    lamw = sb.alloc([128, 8], F32)
    flagt = sb.alloc([128, 1], F32)
    Rc = Reg(const=True)
    S.op("pool", ("memset", A(ones_bf[:], 1.0)), writes=[Rc])
    S.op("pool", ("memset", A(ones_f[:], 1.0)), writes=[Rc])
    S.dma("pool", cm[:], cmat, writes=[Rc])
    S.dma("sp", gvt[:], gv, writes=[Rc])
    S.dma("sp", fst[:], fsc, writes=[Rc])
    S.dma("sp", lamt[:], lamr, writes=[Rc])
    S.dma("sp", flagt[:], flag, writes=[Rc])
    Rl = Reg()
    S.op("dve", ("tensor_tensor", A(out=lamt[:, 0:64], in0=lamt[:, 0:64], in1=lamt[:, 64:128], op=ALU.mult)), reads=[Rc], writes=[Rl])
    S.op("dve", ("tensor_tensor", A(out=lamt[:, 128:192], in0=lamt[:, 128:192], in1=lamt[:, 192:256], op=ALU.mult)), reads=[Rc], writes=[Rl])
    S.op("dve", ("reduce_sum", A(out=lamw[:, 0:1], in_=lamt[:, 0:64], axis=AX.X)), reads=[Rl], writes=[Rl])
    S.op("dve", ("reduce_sum", A(out=lamw[:, 1:2], in_=lamt[:, 128:192], axis=AX.X)), reads=[Rl], writes=[Rl])
    S.op("act", ("activation", A(out=lamw[:, 2:4], in_=lamw[:, 0:2], func=AF.Exp)), reads=[Rl], writes=[Rl])
    S.op("dve", ("tensor_tensor", A(out=lamw[:, 4:5], in0=lamw[:, 3:4], in1=lamw[:, 2:3], op=ALU.subtract)), reads=[Rl], writes=[Rl])
    S.op("dve", ("tensor_scalar", A(out=lamw[:, 4:5], in0=lamw[:, 4:5], scalar1=-LAM_INIT, scalar2=None, op0=ALU.add)), reads=[Rl], writes=[Rl])
    S.op("dve", ("tensor_scalar", A(out=lamw[:, 5:6], in0=gvt[:, 35:36], scalar1=1.0 - LAM_INIT, scalar2=None, op0=ALU.mult)), reads=[Rl, Rc], writes=[Rl])
    S.op("dve", ("tensor_scalar", A(out=lamw[:, 6:7], in0=flagt[:, 0:1], scalar1=-1.0, scalar2=30000.0, op0=ALU.add, op1=ALU.mult)), reads=[Rc], writes=[Rl])
    S.barrier()
    G_FFN1, G_MIX, G_FFN2, G_FIN, G_Q, G_KV = 0, 8, 16, 24, 32, 34
    neglam = lamw[:, 4:5]
    sgain = lamw[:, 5:6]
    fbias = lamw[:, 6:7]
    base_off = sb.off

    def mm(out, lhsT, rhs, start, stop, reads, writes, sig):
        return S.op("pe", ("matmul", A(out, lhsT, rhs, start=start, stop=stop)), reads=reads, writes=writes, sig=sig)

    def rmsnorm(src, Rsrc, C, nfeat, gcol, sqbuf, Rsq, out, Rout, pbank, rstd, Rrstd):
        snap = ([Rsq.w] if Rsq.w is not None else []) + list(Rsq.r)
        Rsqc = [Reg() for _ in range(C)]
        for c in range(C):
            if c % 2 == 0:
                S.op("act", ("activation", A(out=sqbuf(c), in_=src(c), func=AF.Square)), reads=[Rsrc], writes=[Rsqc[c]], deps=snap)
            else:
                S.op("dve", ("tensor_tensor", A(out=sqbuf(c), in0=src(c), in1=src(c), op=ALU.mult)), reads=[Rsrc], writes=[Rsqc[c]], deps=snap)
        for c in range(C):
            mm(ps[pbank][:], ones_bf[:, :], sqbuf(c), c == 0, c == C - 1, [Rsqc[c], Rsq, Rc], [Rps[pbank]], c == C - 1)
        S.op("act", ("activation", A(out=rstd[:], in_=ps[pbank][:], func=AF.Ln, bias=EPS, scale=1.0 / nfeat)), reads=[Rps[pbank]], writes=[Rrstd])
        S.op("act", ("activation", A(out=rstd[:], in_=rstd[:], func=AF.Exp, scale=-0.5)), reads=[Rrstd], writes=[Rrstd])
        for c in range(C):
            S.op("dve", ("scalar_tensor_tensor", A(out=out(c), in0=src(c), scalar=gvt[:, gcol + c:gcol + c + 1], in1=rstd[:], op0=ALU.mult, op1=ALU.mult)),
                 reads=[Rsrc, Rrstd, Rc], writes=[Rout])

    def load_w(dst, src_ap, R, pieces, axis_len, dim):
        step = axis_len // pieces
        for i in range(pieces):
            sl = slice(i * step, (i + 1) * step)
            if dim == 3:
                S.dma("pool", dst[:, :, sl], src_ap[:, :, sl], writes=[R])
            else:
                S.dma("pool", dst[:, sl], src_ap[:, sl], writes=[R])

    def ffn_phase(x_in, n_tiles, wg, wu, wd, gpre, x_out, out_tiles, final_norm, after_store=None):
        sb.off = base_off
        Wg = sb.alloc([128, 8, FF], BF16); Wu = sb.alloc([128, 8, FF], BF16); Wd = sb.alloc([128, NFF, D], BF16)
        xb = [sb.alloc([128, 8, T], F32) for _ in range(2)]
        hT = sb.alloc([128, 8, T], BF16)
        aT = sb.alloc([128, NFF, T], BF16)
        rstd = sb.alloc([128, T], F32)
        sil = [sb.alloc([128, T], BF16) for _ in range(2)]
        Rx = [Reg(), Reg()]; Rh = Reg(); Ra = [Reg() for _ in range(NFF)]; Rr = Reg(); Rs = [Reg(), Reg()]

        def load_x(t):
            S.dma("sp", xb[t % 2][:], x_in(t).rearrange("(c p) t -> p c t", p=128), writes=[Rx[t % 2]])

        load_x(0)
        NPC = 4
        FPC = 6
        RWg = [Reg(True) for _ in range(NPC)]; RWu = [Reg(True) for _ in range(NPC)]; RWd = [Reg(True) for _ in range(2)]
        wg3 = wg.rearrange("(c p) f -> p c f", p=128); wu3 = wu.rearrange("(c p) f -> p c f", p=128); wd3 = wd.rearrange("(c p) f -> p c f", p=128)
        for pc in range(NPC):
            sl = slice(pc * FPC * 128, min((pc + 1) * FPC * 128, FF))
            S.dma("pool", Wg[:, :, sl], wg3[:, :, sl], writes=[RWg[pc]])
            S.dma("pool", Wu[:, :, sl], wu3[:, :, sl], writes=[RWu[pc]])
        for d2 in range(2):
            sl = slice(d2 * 512, (d2 + 1) * 512)
            S.dma("pool", Wd[:, :, sl], wd3[:, :, sl], writes=[RWd[d2]])

        def norm_pre(t):
            x = xb[t % 2]
            rmsnorm(lambda c: x[:, c, :], Rx[t % 2], 8, D, gpre, lambda c: hT[:, c, :], Rh, lambda c: hT[:, c, :], Rh, 0, rstd, Rr)

        if KF[0] == "1":
            norm_pre(0)
        for t in range(n_tiles):
            x = xb[t % 2]; Rxc = Rx[t % 2]
            if t + 1 < n_tiles:
                load_x(t + 1)
            if KF[0] != "1":
                norm_pre(t)
            for f in range(NFF):
                pg, pu = 1 + f % 2, 3 + f % 2
                for k in range(8):
                    mm(ps[pg][:], Wg[:, k, f * 128:(f + 1) * 128], hT[:, k, :], k == 0, k == 7, [RWg[f // FPC], Rh], [Rps[pg]], k == 7)
                for k in range(8):
                    mm(ps[pu][:], Wu[:, k, f * 128:(f + 1) * 128], hT[:, k, :], k == 0, k == 7, [RWu[f // FPC], Rh], [Rps[pu]], k == 7)
                S.op("act", ("activation", A(out=sil[f % 2][:], in_=ps[pg][:], func=AF.Silu)), reads=[Rps[pg]], writes=[Rs[f % 2]])
                S.op("dve", ("tensor_tensor", A(out=aT[:, f, :], in0=ps[pu][:], in1=sil[f % 2][:], op=ALU.mult)), reads=[Rps[pu], Rs[f % 2]], writes=[Ra[f]])
            for d in range(8):
                po = 5 + d % 2
                for f in range(NFF):
                    mm(ps[po][:], Wd[:, f, d * 128:(d + 1) * 128], aT[:, f, :], f == 0, f == NFF - 1, [RWd[d // 4], Ra[f]], [Rps[po]], f == NFF - 1)
                S.op("dve", ("scalar_tensor_tensor", A(out=x[:, d, :], in0=ps[po][:], scalar=0.5, in1=x[:, d, :], op0=ALU.mult, op1=ALU.add)),
                     reads=[Rps[po]], writes=[Rxc])
                if KF[0] == "1" and d == 1 and t + 1 < n_tiles:
                    norm_pre(t + 1)
            if final_norm:
                rmsnorm(lambda c: x[:, c, :], Rxc, 8, D, G_FIN, lambda c: aT[:, c, :], Ra[0], lambda c: x[:, c, :], Rxc, 0, rstd, Rr)
            S.dma("pool", x_out(t).rearrange("(c p) t -> p c t", p=128), x[:], reads=[Rxc], writes=out_tiles(t))
            if after_store is not None:
                after_store(t)
        S.barrier()


    def exchange(t):
        S.coll("AllReduce", ALU.add, groups, x1o[t], x1s[t], reads=[Rxo[t]], writes=[Rxsum[t]])

    ffn_phase(lambda t: xT[:, t * T:(t + 1) * T], NO, w1g, w1u, w1d, G_FFN1, lambda t: x1o[t], lambda t: [Rxo[t]], False, after_store=exchange)

    sb.off = base_off
    kvnT = sb.alloc([128, S_LEN], BF16)
    KT2 = sb.alloc([128, 2, S_LEN], BF16)
    bc_off = sb.off
    Rkvn = Reg(); Rkt = [Reg(), Reg()]; Rkr = Reg(); Rkz = Reg()
    S.op("pool", ("memset", A(KT2[96:128, :, :], 0.0)), writes=[Rkz])
    Win = sb.alloc([128, 8, C_GM], BF16)
    Wuq = sb.alloc([128, 2, 768], BF16)
    RWin, RWuq = Reg(True), Reg(True)
    load_w(Win, win.rearrange("(c p) f -> p c f", p=128)[:, :, 0:C_GM], RWin, 2, C_GM, 3)
    load_w(Wuq, wuq.rearrange("(c p) f -> p c f", p=128), RWuq, 1, 768, 3)
    xb = [sb.alloc([128, 8, T], F32) for _ in range(2)]
    posi = [sb.alloc([128, T], I32) for _ in range(2)]
    h2 = sb.alloc([128, 8, T], BF16)
    rstd = sb.alloc([128, T], F32)
    posf = sb.alloc([128, T], F32)
    trgb = [sb.alloc([128, 4, T], F32) for _ in range(2)]
    ta = sb.alloc([128, T], F32); tb = sb.alloc([128, T], F32)
    ckv = sb.alloc([128, 2, T], F32)
    sqb = sb.alloc([128, 2, T], BF16)
    cqn = sb.alloc([128, 2, T], BF16)
    xs = [sb.alloc([128, T], BF16) for _ in range(2)]
    t1 = [sb.alloc([128, T], F32) for _ in range(2)]
    t2 = [sb.alloc([128, T], F32) for _ in range(2)]
    dk_st = sb.alloc([128, 4, T], BF16)
    stg_off = sb.off
    xo_full = sb.alloc([128, 8, T], F32)
    q_st = nc.alloc_sbuf_tensor_at("q_st_alias", [96, 8, T], BF16, offset=stg_off)
    dq_st = nc.alloc_sbuf_tensor_at("dq_st_alias", [128, 4, T], BF16, offset=stg_off + 8 * T * 2)
    dv_st = sb.alloc([128, 4, 512], BF16)
    Rx = [Reg(), Reg()]; Rpi = [Reg(), Reg()]; Rh2 = Reg(); Rr = Reg(); Rpf = Reg(); Rtrgb = [Reg(), Reg()]; Rta = Reg(); Rtb = Reg()
    Rckv = Reg(); Rsqb = Reg(); Rcqn = Reg(); Rxs = [Reg(), Reg()]; Rt1 = [Reg(), Reg()]; Rt2 = [Reg(), Reg()]
    Rdk = Reg(); Rdq = Reg(); Rq = Reg(); Rdv = Reg()
    rope_i = [0]
    cur = {}
    h2b = [h2, sb.alloc([128, 8, T], BF16)]
    Rh2b = [Rh2, Reg()]
    pend_rope = []

    def flush_rope():
        while pend_rope:
            pend_rope.pop(0)()

    def rope(pbank, R, tab, outs):
        i = rope_i[0] % 2
        rope_i[0] += 1
        pw = 4 + i
        S.op("act", ("activation", A(out=xs[i][0:R, :], in_=ps[pbank][0:R, :], func=AF.Identity)), reads=[Rps[pbank]], writes=[Rxs[i]])
        r0 = min(o[2] for o in outs); r1 = max(o[3] for o in outs)
        S.op("dve", ("tensor_tensor", A(out=t1[i][r0:r1, :], in0=ps[pbank][r0:r1, :], in1=cur['trg'][r0:r1, 2 * tab, :], op=ALU.mult)), reads=[Rps[pbank], cur['Rtrg']], writes=[Rt1[i]])

        trg_, Rtrg_ = cur['trg'], cur['Rtrg']

        def stage_b():
            mm(ps[pw][0:R, :], cm[0:R, 1 + tab, 0:R], xs[i][0:R, :], True, True, [Rxs[i], Rc], [Rps[pw]], True)
            S.op("dve", ("tensor_tensor", A(out=t2[i][r0:r1, :], in0=ps[pw][r0:r1, :], in1=trg_[r0:r1, 2 * tab + 1, :], op=ALU.mult)), reads=[Rps[pw], Rtrg_], writes=[Rt2[i]])
            for (apf, Ro, a, b) in outs:
                S.op("pool", ("tensor_tensor", A(out=apf, in0=t1[i][a:b, :], in1=t2[i][a:b, :], op=ALU.add)), reads=[Rt1[i], Rt2[i]], writes=[Ro])
        if KF[1] == "1":
            pend_rope.append(stage_b)
        else:
            stage_b()

    Rxoh = Reg()
    pend_sub = []

    def load_b(t):
        if t < NO:
            S.dma("sp", xb[t % 2][:], x1o[t].rearrange("(c p) t -> p c t", p=128), writes=[Rx[t % 2]])
        else:
            j = t - NO
            S.dma("sp", xb[t % 2][:], x1s[j].rearrange("(c p) t -> p c t", p=128), reads=[Rxsum[j]], writes=[Rx[t % 2]])
            S.dma("sp", xo_full[:], x1o[j].rearrange("(c p) t -> p c t", p=128), writes=[Rxoh, Rq, Rdq])

            def sub(t=t):
                S.op("dve", ("tensor_tensor", A(out=xb[t % 2][:], in0=xb[t % 2][:], in1=xo_full[:], op=ALU.subtract)), reads=[Rxoh], writes=[Rx[t % 2]])
            pend_sub.append(sub)
        S.dma("sp", posi[t % 2][:], posr[:, t * T:(t + 1) * T], writes=[Rpi[t % 2]])

    load_b(0)
    pp_i = [0]

    def proj(cols, M, srcs, W, RW, nk):
        pb = 1 + pp_i[0] % 3
        pp_i[0] += 1
        for k in range(nk):
            mm(ps[pb][0:M, :], W[:, k, cols], srcs[0](k), k == 0, k == nk - 1, [RW, srcs[1]], [Rps[pb]], k == nk - 1)
        flush_rope()
        return pb

    def norm_b(t):
        x = xb[t % 2]
        hh = h2b[t % 2]
        rmsnorm(lambda c: x[:, c, :], Rx[t % 2], 8, D, G_MIX, lambda c: hh[:, c, :], Rh2b[t % 2], lambda c: hh[:, c, :], Rh2b[t % 2], 0, rstd, Rr)

    def trig_b(t):
        trg, Rtrg = trgb[t % 2], Rtrgb[t % 2]
        S.op("dve", ("tensor_copy", A(out=posf[:], in_=posi[t % 2][:])), reads=[Rpi[t % 2]], writes=[Rpf])
        for tab in range(2):
            for cs in range(2):
                S.op("dve", ("tensor_scalar", A(out=ta[:], in0=posf[:], scalar1=fst[:, tab:tab + 1], scalar2=(0.25 if cs == 0 else 0.0), op0=ALU.mult, op1=ALU.add)),
                     reads=[Rpf, Rc], writes=[Rta])
                S.op("dve", ("tensor_scalar", A(out=tb[:], in0=ta[:], scalar1=MAGIC, scalar2=None, op0=ALU.add)), reads=[Rta], writes=[Rtb])
                S.op("dve", ("scalar_tensor_tensor", A(out=ta[:], in0=tb[:], scalar=MAGIC, in1=ta[:], op0=ALU.subtract, op1=ALU.subtract)), reads=[Rtb, Rta], writes=[Rta])
                S.op("act", ("activation", A(out=trg[:, 2 * tab + cs, :], in_=ta[:], func=AF.Sin, scale=-2.0 * math.pi * 0.999999)), reads=[Rta], writes=[Rtrg])

    norm_b(0)
    trig_b(0)
    for t in range(NT):
        if t + 1 < NT:
            load_b(t + 1)
        cols_t = slice(t * T, (t + 1) * T)
        hh = h2b[t % 2]; Rhh = Rh2b[t % 2]
        cur['trg'] = trgb[t % 2]; cur['Rtrg'] = Rtrgb[t % 2]
        hsrc = (lambda k: hh[:, k, :], Rhh)
        is_own = t in own
        pb = proj(slice(C_CKV, C_CKV + 128), 128, hsrc, Win, RWin, 8)
        S.op("act", ("activation", A(out=ckv[:, 0, :], in_=ps[pb][:], func=AF.Identity)), reads=[Rps[pb]], writes=[Rckv])
        rmsnorm(lambda c: ckv[:, c, :], Rckv, 1, 128, G_KV, lambda c: sqb[:, c, :], Rsqb, lambda c: kvnT[:, cols_t], Rkvn, 7, rstd, Rr)
        if is_own:
            qi = own.index(t)
            cols_q = slice(qi * T, (qi + 1) * T)
            for c in range(2):
                pb = proj(slice(C_CQ + 128 * c, C_CQ + 128 * (c + 1)), 128, hsrc, Win, RWin, 8)
                S.op("act", ("activation", A(out=ckv[:, c, :], in_=ps[pb][:], func=AF.Identity)), reads=[Rps[pb]], writes=[Rckv])
            rmsnorm(lambda c: ckv[:, c, :], Rckv, 2, 256, G_Q, lambda c: sqb[:, c, :], Rsqb, lambda c: cqn[:, c, :], Rcqn, 7, rstd, Rr)
        pb = proj(slice(C_KR - 64, C_KR + 32), 96, hsrc, Win, RWin, 8)
        rope(pb, 96, 0, [(KT2[64:96, 0, cols_t], Rkr, 64, 96), (KT2[64:96, 1, cols_t], Rkr, 64, 96)])
        for j in range(4):
            pb = proj(slice(C_DK + 128 * j, C_DK + 128 * (j + 1)), 128, hsrc, Win, RWin, 8)
            rope(pb, 128, 1, [(dk_st[:, j, :], Rdk, 0, 128)])
        if t + 1 < NT:
            while pend_sub:
                pend_sub.pop(0)()
            norm_b(t + 1)
            trig_b(t + 1)
        if is_own:
            for j in range(4):
                pb = proj(slice(C_DQ + 128 * j, C_DQ + 128 * (j + 1)), 128, hsrc, Win, RWin, 8)
                rope(pb, 128, 1, [(dq_st[:, j, :], Rdq, 0, 128)])
            for h in range(8):
                pb = proj(slice(96 * h, 96 * (h + 1)), 96, (lambda k: cqn[:, k, :], Rcqn), Wuq, RWuq, 2)
                rope(pb, 96, 0, [(q_st[0:96, h, :], Rq, 0, 96)])
        for sub in range(4):
            pv = 6
            for k in range(8):
                mm(ps[pv][:], hh[:, k, sub * 128:(sub + 1) * 128], Win[:, k, C_DV:C_DV + 512], k == 0, k == 7, [Rhh, RWin], [Rps[pv]], k == 7)
            if sub == 0:
                flush_rope()
            S.op("act", ("activation", A(out=dv_st[:, sub, :], in_=ps[pv][:], func=AF.Identity)), reads=[Rps[pv]], writes=[Rdv])
        S.dma("pool", dvS[t * T:(t + 1) * T, :].rearrange("(s p) c -> p s c", p=128), dv_st[:], reads=[Rdv])
        flush_rope()
        S.dma("pool", dkT[:, :, cols_t].rearrange("j p t -> p j t"), dk_st[:], reads=[Rdk])
        if is_own:
            S.dma("pool", dqT[:, :, cols_q].rearrange("j p t -> p j t"), dq_st[:], reads=[Rdq])
            S.dma("pool", qTs[:, :, cols_q].rearrange("h r t -> r h t"), q_st[:], reads=[Rq])
    S.barrier()

    sb.off = bc_off
    WkK = sb.alloc([128, 8, 64], BF16); WkV = sb.alloc([128, 8, 64], BF16)
    RWk = Reg(True)
    wukv4 = wukv.rearrange("k (h two d) -> k h two d", two=2, d=64)
    S.dma("pool", WkK[:], wukv4[:, :, 0, :], writes=[RWk])
    S.dma("pool", WkV[:], wukv4[:, :, 1, :], writes=[RWk])
    NKT = S_LEN // 128
    c_off = sb.off
    Vh = [sb.alloc([128, NKT, 128], BF16) for _ in range(2)]
    QT = [sb.alloc([128, NO * T], BF16) for _ in range(2)]
    PT = [sb.alloc([128, T], BF16) for _ in range(4)]
    rc = sb.alloc([128, T], F32)
    bcs = sb.alloc([64, T], F32)
    yst = [sb.alloc([64, T], BF16) for _ in range(2)]
    RV = [Reg(), Reg()]; RQ = [Reg(), Reg()]; RPT = [Reg() for _ in range(4)]; Rrc = Reg(); Rbcs = Reg(); Ryst = [Reg(), Reg()]
    RQz = Reg()
    for b_ in range(2):
        S.op("pool", ("memset", A(Vh[b_][:], 1.0)), writes=[RV[b_]])
        S.op("pool", ("memset", A(QT[b_][96:128, :], 0.0)), writes=[RQz])

    def key_tiles(i):
        L = []
        for j in range(i):
            for s_ in range(4):
                L.append((j * 4 + s_, 0, 0))
            for s_ in range(4):
                L.append(((NO + j) * 4 + s_, 0, 0))
        for s_ in range(4):
            L.append((i * 4 + s_, 1, s_))
        for s_ in range(4):
            L.append(((NO + i) * 4 + s_, 2, 0))
        return L

    blocks = []
    for h in range(8):
        for qi in range(NO):
            L = key_tiles(qi)
            for n_, (kt, kind, sub) in enumerate(L):
                blocks.append((h, qi, kt, kind, sub, n_ == 0, n_ == len(L) - 1))
    LA = 2
    pending = []
    fin_i = [0]

    def mla_prologue_steps(h):
        b = h % 2
        steps = []

        def kstep(g):
            mm(ps[7][0:64, :], WkK[:, h, :], kvnT[:, g * T:(g + 1) * T], True, True, [Rkvn, RWk], [Rps[7]], True)
            S.op("dve", ("tensor_copy", A(out=KT2[0:64, b, g * T:(g + 1) * T], in_=ps[7][0:64, :])), reads=[Rps[7]], writes=[Rkt[b]])

        def vstep(g8):
            for j in range(8):
                kt = g8 * 8 + j
                mm(ps[7][:, j * 64:(j + 1) * 64], kvnT[:, kt * 128:(kt + 1) * 128], WkV[:, h, :], True, True, [Rkvn, RWk], [Rps[7]], j == 7)
            S.op("dve", ("tensor_copy", A(out=Vh[b][:, g8 * 8:(g8 + 1) * 8, 0:64], in_=ps[7][:].rearrange("p (j d) -> p j d", d=64))), reads=[Rps[7]], writes=[RV[b]])

        steps.append(lambda: S.dma("sp", QT[b][0:96, :], qTs[h], writes=[RQ[b]]))
        for g in range(S_LEN // T):
            steps.append(lambda g=g: kstep(g))
        for g8 in range(NKT // 8):
            steps.append(lambda g8=g8: vstep(g8))
        return steps

    pro_steps = []

    def mla_head_prologue(h):
        while pro_steps:
            pro_steps.pop(0)()
        if h == 0:
            for st_ in mla_prologue_steps(0):
                st_()
        if h + 1 < 8:
            pro_steps.extend(mla_prologue_steps(h + 1))

    units = []
    for h in range(8):
        for qi in range(NO):
            L = key_tiles(qi)
            n_ = 0
            while n_ < len(L):
                kt, kind, sub = L[n_]
                if kind != 1 and n_ + 1 < len(L) and L[n_ + 1][1] == kind:
                    blks = [L[n_], L[n_ + 1]]
                    n_ += 2
                else:
                    blks = [L[n_]]
                    n_ += 1
                units.append((h, qi, kind, blks, blks[0] is L[0], blks[-1] is L[-1]))
    PT2 = [sb.alloc([128, 2, T], BF16) for _ in range(3)]
    RPT2 = [Reg() for _ in range(3)]

    def mla_S(u):
        h, qi, kind, blks, first, last = units[u]
        if qi == 0 and first:
            mla_head_prologue(h)
        b = h % 2
        a = (u % 2) * 2
        pb = u % 3
        sub = blks[0][2]
        q0 = 128 * sub
        w = T - q0
        nb = len(blks)
        for bi, (kt, _, _) in enumerate(blks):
            mm(ps[a + bi][:, 0:w], KT2[:, b, kt * 128:(kt + 1) * 128], QT[b][:, qi * T + q0:(qi + 1) * T], True, True,
               [Rkt[b], Rkr, Rkz, RQ[b], RQz], [Rps[a + bi]], True)
        kw = dict(func=AF.Exp, scale=SC_M)
        rd = [Rps[a + bi] for bi in range(nb)]
        if kind == 2:
            kw["bias"] = fbias
            rd.append(Rl)
        S.op("act", ("activation", A(out=PT2[pb][:, 0:nb, 0:w], in_=pspair(a)[:, 0:nb, 0:w], **kw)), reads=rd, writes=[RPT2[pb]])
        if kind == 1:
            S.op("dve", ("tensor_tensor", A(out=PT2[pb][:, 0, 0:128], in0=PT2[pb][:, 0, 0:128], in1=cm[:, 0, :], op=ALU.mult)), reads=[RPT2[pb], Rc], writes=[RPT2[pb]])

    def mla_PV(u):
        h, qi, kind, blks, first, last = units[u]
        pb = u % 3
        sub = blks[0][2]
        q0 = 128 * sub
        w = T - q0
        ob = 4 + fin_i[0] % 2
        nb = len(blks)
        for bi, (kt, _, _) in enumerate(blks):
            st = first and bi == 0
            sp_ = last and bi == nb - 1
            mm(ps[ob][:, q0:T], Vh[h % 2][:, kt, :], PT2[pb][:, bi, 0:w], st, sp_, [RV[h % 2], RPT2[pb]], [Rps[ob]], sp_ or bi == nb - 1)
        if last:
            f = fin_i[0] % 2
            fin_i[0] += 1
            S.op("act", ("activation", A(out=rc[64:65, :], in_=ps[ob][64:65, :], func=AF.Ln)), reads=[Rps[ob]], writes=[Rrc])
            S.op("act", ("activation", A(out=rc[64:65, :], in_=rc[64:65, :], func=AF.Exp, scale=-1.0)), reads=[Rrc], writes=[Rrc])

            def fin_pe(ob=ob, f=f, h=h, qi=qi):
                mm(ps[6][0:64, :], ones_f[64:65, 0:64], rc[64:65, :], True, True, [Rrc, Rc], [Rps[6]], True)
                S.op("dve", ("tensor_copy", A(out=bcs[:], in_=ps[6][0:64, :])), reads=[Rps[6]], writes=[Rbcs])
                S.op("dve", ("tensor_tensor", A(out=yst[f][:], in0=ps[ob][0:64, :], in1=bcs[:], op=ALU.mult)), reads=[Rps[ob], Rbcs], writes=[Ryst[f]])
                S.dma("pool", ymT[h * 64:(h + 1) * 64, qi * T:(qi + 1) * T], yst[f][:], reads=[Ryst[f]])
            pending.append([2, fin_pe])

    def run_pending(force=False):
        for p in list(pending):
            p[0] -= 1
            if p[0] <= 0 or force:
                p[1]()
                pending.remove(p)

    n = len(units)
    for u in range(n + 1):
        if u < n:
            mla_S(u)
        if u - 1 >= 0:
            mla_PV(u - 1)
        run_pending()
        if pro_steps and u % 2 == 1:
            pro_steps.pop(0)()
    run_pending(force=True)
    S.barrier()

    sb.off = c_off
    KTd = [sb.alloc([128, S_LEN], BF16) for _ in range(2)]
    Vd = [sb.alloc([128, NKT, 128], BF16) for _ in range(2)]
    QTd = [sb.alloc([128, NO * T], BF16) for _ in range(2)]
    P12 = [sb.alloc([128, 2, T], BF16) for _ in range(3)]
    r1 = sb.alloc([128, T], F32); r2 = sb.alloc([128, T], F32)
    u1 = sb.alloc([128, T], F32); u2 = sb.alloc([128, T], F32)
    dd = sb.alloc([128, T], F32); sqd = sb.alloc([128, T], BF16); rsd = sb.alloc([128, T], F32)
    ysd = [sb.alloc([128, T], BF16) for _ in range(2)]
    RKd = [Reg(), Reg()]; RVd = [Reg(), Reg()]; RQd = [Reg(), Reg()]
    RP12 = [Reg() for _ in range(3)]
    Rr1, Rr2, Ru1, Ru2, Rdd, Rsqd, Rrsd = [Reg() for _ in range(7)]
    Rysd = [Reg(), Reg()]
    dblocks = []
    for g in range(4):
        for qi in range(NO):
            L = key_tiles(qi)
            for n_, (kt, kind, sub) in enumerate(L):
                dblocks.append((g, qi, kt, kind, sub, n_ == 0, n_ == len(L) - 1))
    dfin = [0]

    def diff_prologue(g):
        b = g % 2
        S.dma("sp", KTd[b][:], dkT[g], writes=[RKd[b]])
        S.dma("sp", Vd[b][:], dvS[:, g * 128:(g + 1) * 128].rearrange("(k p) d -> p k d", p=128), writes=[RVd[b]])
        S.dma("sp", QTd[b][:], dqT[g], writes=[RQd[b]])

    def diff_S(i):
        g, qi, kt, kind, sub, first, last = dblocks[i]
        if qi == 0 and first and g == 0:
            diff_prologue(0)
        b = g % 2
        q0 = 128 * sub
        w = T - q0
        a = (i % 2) * 2
        pb = i % 3
        mm(ps[a][:, 0:w], KTd[b][0:64, kt * 128:(kt + 1) * 128], QTd[b][0:64, qi * T + q0:(qi + 1) * T], True, True, [RKd[b], RQd[b]], [Rps[a]], True)
        mm(ps[a + 1][:, 0:w], KTd[b][64:128, kt * 128:(kt + 1) * 128], QTd[b][64:128, qi * T + q0:(qi + 1) * T], True, True, [RKd[b], RQd[b]], [Rps[a + 1]], True)
        kw = dict(func=AF.Exp, scale=SC_D)
        rd = [Rps[a], Rps[a + 1]]
        if kind == 2:
            kw["bias"] = fbias
            rd.append(Rl)
        S.op("act", ("activation", A(out=P12[pb][:, :, 0:w], in_=pspair(a)[:, :, 0:w], **kw)), reads=rd, writes=[RP12[pb]])
        if kind == 1:
            for k_ in range(2):
                S.op("dve", ("tensor_tensor", A(out=P12[pb][:, k_, 0:128], in0=P12[pb][:, k_, 0:128], in1=cm[:, 0, :], op=ALU.mult)), reads=[RP12[pb], Rc], writes=[RP12[pb]])

    def diff_PV(i):
        g, qi, kt, kind, sub, first, last = dblocks[i]
        b = g % 2
        q0 = 128 * sub
        w = T - q0
        pb = i % 3
        st, sp_ = first, last
        mm(ps[4][:, q0:T], Vd[b][:, kt, :], P12[pb][:, 0, 0:w], st, sp_, [RVd[b], RP12[pb]], [Rps[4]], sp_)
        mm(ps[6][:, q0:T], ones_bf[:, :], P12[pb][:, 0, 0:w], st, sp_, [Rc, RP12[pb]], [Rps[6]], sp_)
        mm(ps[5][:, q0:T], Vd[b][:, kt, :], P12[pb][:, 1, 0:w], st, sp_, [RVd[b], RP12[pb]], [Rps[5]], sp_)
        mm(ps[7][:, q0:T], ones_bf[:, :], P12[pb][:, 1, 0:w], st, sp_, [Rc, RP12[pb]], [Rps[7]], True)
        if sp_:
            f = dfin[0] % 2
            dfin[0] += 1
            S.op("dve", ("tensor_copy", A(out=u1[:], in_=ps[4][:])), reads=[Rps[4]], writes=[Ru1])
            S.op("dve", ("tensor_copy", A(out=u2[:], in_=ps[5][:])), reads=[Rps[5]], writes=[Ru2])
            S.op("act", ("activation", A(out=r1[:], in_=ps[6][:], func=AF.Ln)), reads=[Rps[6]], writes=[Rr1])
            S.op("act", ("activation", A(out=r2[:], in_=ps[7][:], func=AF.Ln)), reads=[Rps[7]], writes=[Rr2])
            S.op("act", ("activation", A(out=r1[:], in_=r1[:], func=AF.Exp, scale=-1.0)), reads=[Rr1], writes=[Rr1])
            S.op("act", ("activation", A(out=r2[:], in_=r2[:], func=AF.Exp, scale=-1.0)), reads=[Rr2], writes=[Rr2])
            S.op("dve", ("tensor_tensor", A(out=u1[:], in0=u1[:], in1=r1[:], op=ALU.mult)), reads=[Rr1], writes=[Ru1])
            S.op("dve", ("tensor_tensor", A(out=u2[:], in0=u2[:], in1=r2[:], op=ALU.mult)), reads=[Rr2], writes=[Ru2])
            S.op("dve", ("scalar_tensor_tensor", A(out=dd[:], in0=u2[:], scalar=neglam, in1=u1[:], op0=ALU.mult, op1=ALU.add)), reads=[Ru1, Ru2, Rl], writes=[Rdd])
            S.op("pool", ("tensor_tensor", A(out=sqd[:], in0=dd[:], in1=dd[:], op=ALU.mult)), reads=[Rdd], writes=[Rsqd])

            def fin_pe(f=f, g=g, qi=qi):
                mm(ps[6][:], ones_bf[:, :], sqd[:], True, True, [Rsqd, Rc], [Rps[6]], True)
                S.op("act", ("activation", A(out=rsd[:], in_=ps[6][:], func=AF.Ln, bias=EPS, scale=1.0 / 128.0)), reads=[Rps[6]], writes=[Rrsd])
                S.op("act", ("activation", A(out=rsd[:], in_=rsd[:], func=AF.Exp, scale=-0.5)), reads=[Rrsd], writes=[Rrsd])
                S.op("dve", ("scalar_tensor_tensor", A(out=ysd[f][:], in0=dd[:], scalar=sgain, in1=rsd[:], op0=ALU.mult, op1=ALU.mult)), reads=[Rdd, Rrsd, Rl], writes=[Rysd[f]])
                S.dma("pool", ydT[g * 128:(g + 1) * 128, qi * T:(qi + 1) * T], ysd[f][:], reads=[Rysd[f]])
            fin_pe()

    n = len(dblocks)
    for i in range(n + 1):
        if i < n:
            diff_S(i)
        if i - 1 >= 0:
            diff_PV(i - 1)
        if i < n and dblocks[i][1] == 0 and dblocks[i][5] and dblocks[i][0] + 1 < 4:
            diff_prologue(dblocks[i][0] + 1)
    S.barrier()

    sb.off = base_off
    WgM = sb.alloc([128, 8, D], BF16); WgD = sb.alloc([128, 8, D], BF16)
    Wpm = sb.alloc([128, 4, D], BF16); Wpd = sb.alloc([128, 4, D], BF16); Wo = sb.alloc([128, 8, D], BF16)
    RW1 = Reg(True)
    winr = win.rearrange("(c p) f -> p c f", p=128)
    load_w(WgM, winr[:, :, C_GM:C_GM + D], RW1, 1, D, 3)
    load_w(WgD, winr[:, :, C_GD:C_GD + D], RW1, 1, D, 3)
    load_w(Wpm, wpm.rearrange("(c p) f -> p c f", p=128), RW1, 1, D, 3)
    load_w(Wpd, wpd.rearrange("(c p) f -> p c f", p=128), RW1, 1, D, 3)
    load_w(Wo, wout.rearrange("(c p) f -> p c f", p=128), RW1, 1, D, 3)
    xb = [sb.alloc([128, 8, T], F32) for _ in range(2)]
    ym = [sb.alloc([128, 4, T], BF16) for _ in range(2)]
    yd = [sb.alloc([128, 4, T], BF16) for _ in range(2)]
    h2 = sb.alloc([128, 8, T], BF16)
    rstd = sb.alloc([128, T], F32)
    sg1 = sb.alloc([128, T], F32); sg2 = sb.alloc([128, T], F32)
    m1 = sb.alloc([128, T], F32); m2 = sb.alloc([128, T], F32)
    mg = sb.alloc([128, 8, T], BF16)
    Rx = [Reg(), Reg()]; Rym = [Reg(), Reg()]; Ryd = [Reg(), Reg()]; Rh2 = Reg(); Rr = Reg()
    Rsg1, Rsg2, Rm1, Rm2 = Reg(), Reg(), Reg(), Reg()
    Rmg = [Reg() for _ in range(8)]

    def load_d(qi):
        t = own[qi]
        S.dma("sp", xb[qi % 2][:], x1o[qi].rearrange("(c p) t -> p c t", p=128), writes=[Rx[qi % 2]])
        S.dma("sp", ym[qi % 2][:], ymT[:, qi * T:(qi + 1) * T].rearrange("(c p) t -> p c t", p=128), writes=[Rym[qi % 2]])
        S.dma("sp", yd[qi % 2][:], ydT[:, qi * T:(qi + 1) * T].rearrange("(c p) t -> p c t", p=128), writes=[Ryd[qi % 2]])

    load_d(0)
    h2d = [h2, sb.alloc([128, 8, T], BF16)]
    Rh2d = [Rh2, Reg()]
    sgs = [sg1, sg2, sb.alloc([128, T], F32), sb.alloc([128, T], F32)]
    Rsgs = [Rsg1, Rsg2, Reg(), Reg()]
    mms = [m1, m2, sb.alloc([128, T], F32), sb.alloc([128, T], F32)]
    Rmms = [Rm1, Rm2, Reg(), Reg()]
    bank_i = [0]

    def nbank():
        b_ = 1 + bank_i[0] % 6
        bank_i[0] += 1
        return b_

    def norm_d(qi):
        x_ = xb[qi % 2]
        hh_ = h2d[qi % 2]
        rmsnorm(lambda c: x_[:, c, :], Rx[qi % 2], 8, D, G_MIX, lambda c: hh_[:, c, :], Rh2d[qi % 2], lambda c: hh_[:, c, :], Rh2d[qi % 2], 0, rstd, Rr)

    norm_d(0)
    for qi in range(NO):
        x = xb[qi % 2]; Rxc = Rx[qi % 2]
        hh = h2d[qi % 2]; Rhh = Rh2d[qi % 2]
        if qi + 1 < NO:
            load_d(qi + 1)
        for d in range(8):
            dsl = slice(d * 128, (d + 1) * 128)
            o_ = (d % 2) * 2
            bg1, bp1, bg2, bp2 = (nbank(), nbank(), nbank(), nbank()) if KF[2] == "1" else (1, 3, 2, 4)
            for k in range(8):
                mm(ps[bg1][:], WgM[:, k, dsl], hh[:, k, :], k == 0, k == 7, [RW1, Rhh], [Rps[bg1]], k == 7)
            S.op("act", ("activation", A(out=sgs[o_][:], in_=ps[bg1][:], func=AF.Sigmoid)), reads=[Rps[bg1]], writes=[Rsgs[o_]])
            for k in range(4):
                mm(ps[bp1][:], Wpm[:, k, dsl], ym[qi % 2][:, k, :], k == 0, k == 3, [RW1, Rym[qi % 2]], [Rps[bp1]], k == 3)
            S.op("dve", ("tensor_tensor", A(out=mms[o_][:], in0=ps[bp1][:], in1=sgs[o_][:], op=ALU.mult)), reads=[Rps[bp1], Rsgs[o_]], writes=[Rmms[o_]])
            for k in range(8):
                mm(ps[bg2][:], WgD[:, k, dsl], hh[:, k, :], k == 0, k == 7, [RW1, Rhh], [Rps[bg2]], k == 7)
            S.op("act", ("activation", A(out=sgs[o_ + 1][:], in_=ps[bg2][:], func=AF.Sigmoid)), reads=[Rps[bg2]], writes=[Rsgs[o_ + 1]])
            for k in range(4):
                mm(ps[bp2][:], Wpd[:, k, dsl], yd[qi % 2][:, k, :], k == 0, k == 3, [RW1, Ryd[qi % 2]], [Rps[bp2]], k == 3)
            S.op("dve", ("tensor_tensor", A(out=mms[o_ + 1][:], in0=ps[bp2][:], in1=sgs[o_ + 1][:], op=ALU.mult)), reads=[Rps[bp2], Rsgs[o_ + 1]], writes=[Rmms[o_ + 1]])
            S.op("pool", ("tensor_tensor", A(out=mg[:, d, :], in0=mms[o_][:], in1=mms[o_ + 1][:], op=ALU.add)), reads=[Rmms[o_], Rmms[o_ + 1]], writes=[Rmg[d]])
        if qi + 1 < NO:
            norm_d(qi + 1)
        for d in range(8):
            po = nbank() if KF[2] == "1" else 5 + d % 2
            for k in range(8):
                mm(ps[po][:], Wo[:, k, d * 128:(d + 1) * 128], mg[:, k, :], k == 0, k == 7, [RW1, Rmg[k]], [Rps[po]], k == 7)
            S.op("dve", ("tensor_tensor", A(out=x[:, d, :], in0=ps[po][:], in1=x[:, d, :], op=ALU.add)), reads=[Rps[po]], writes=[Rxc])
        S.dma("pool", x2T[:, qi * T:(qi + 1) * T].rearrange("(c p) t -> p c t", p=128), x[:], reads=[Rxc])
    S.barrier()

    ffn_phase(lambda t: x2T[:, t * T:(t + 1) * T], NO, w2g, w2u, w2d, G_FFN2, lambda t: outT[:, t * T:(t + 1) * T], lambda t: [], True)
    S.emit()
    return nc


def const_tables():
    tri = np.zeros((128, 128), np.float32)
    for p in range(128):
        tri[p, p:] = 1.0
    pmM = np.zeros((128, 128), np.float32)
    for j in range(16):
        pmM[64 + j + 16, 64 + j] = -1.0
        pmM[64 + j, 64 + j + 16] = 1.0
    pmD = np.zeros((128, 128), np.float32)
    for b in (0, 64):
        for j in range(8):
            pmD[b + j + 8, b + j] = -1.0
            pmD[b + j, b + j + 8] = 1.0
    cmat = np.stack([tri, pmM, pmD], axis=1)
    fsc = np.zeros((128, 2), np.float64)
    for j in range(32):
        fsc[64 + j, 0] = THETA ** (-(j % 16) / 16.0) / (2 * math.pi)
    for b in (0, 64):
        for j in range(16):
            fsc[b + j, 1] = THETA ** (-(j % 8) / 8.0) / (2 * math.pi)
    return np.ascontiguousarray(cmat), fsc.astype(np.float32)


def col8(v, n):
    return np.ascontiguousarray(np.asarray(v, np.float32).reshape(n, 128).T)


def prep_shared(inp):
    f = lambda k: np.ascontiguousarray(np.asarray(inp[k], np.float32)[0])
    cmat, fsc = const_tables()
    gv = np.zeros((128, 36), np.float32)
    gv[:, 0:8] = col8(inp["ffn1_norm"][0], 8)
    gv[:, 8:16] = col8(inp["mix_norm"][0], 8)
    gv[:, 16:24] = col8(inp["ffn2_norm"][0], 8)
    gv[:, 24:32] = col8(inp["final_norm"], 8)
    gv[:, 32:34] = col8(inp["mla_q_norm"][0], 2)
    gv[:, 34:35] = col8(inp["mla_kv_norm"][0], 1)
    gv[:, 35:36] = col8(inp["diff_subln"][0], 1)
    lam = np.concatenate([np.asarray(inp[k], np.float32)[0] for k in ("diff_lam_q1", "diff_lam_k1", "diff_lam_q2", "diff_lam_k2")])
    lamr = np.ascontiguousarray(np.broadcast_to(lam[None, :], (128, 256)))
    return {
        "gv": gv, "lamr": lamr, "fsc": fsc, "cmat": cmat,
        "w1g": f("ffn1_w_gate"), "w1u": f("ffn1_w_up"), "w1d": f("ffn1_w_down"),
        "w2g": f("ffn2_w_gate"), "w2u": f("ffn2_w_up"), "w2d": f("ffn2_w_down"),
        "win": f("w_in"), "wuq": f("mla_w_uq"), "wukv": f("mla_w_ukv"),
        "wpm": f("w_proj_mla"), "wpd": f("w_proj_diff"), "wout": f("w_out"),
    }


def core_perm(half, NT):
    NO = NT // 2
    own = [2 * i + half for i in range(NO)]
    oth = [2 * i + 1 - half for i in range(NO)]
    return own, oth


def make_in_maps(inputs, n_cores):
    x = np.asarray(inputs["x"], np.float32)
    pos = np.asarray(inputs["positions"], np.int32)
    B, S_LEN, _ = x.shape
    NT = S_LEN // T
    shared = prep_shared(inputs)
    in_maps, owns = [], []
    for core in range(n_cores):
        b, half = core // 2, core % 2
        own, oth = core_perm(half, NT)
        idx = np.concatenate([np.arange(c * T, (c + 1) * T) for c in own + oth])
        m = dict(shared)
        m["xT"] = np.ascontiguousarray(x[b][idx].T)
        m["posr"] = np.ascontiguousarray(np.broadcast_to(pos[b][idx][None, :], (128, S_LEN)))
        m["flag"] = np.full((128, 1), float(half), np.float32)
        in_maps.append(m)
        owns.append(own)
    return in_maps, owns


def gather_out(results, owns, B, S_LEN):
    out = np.empty((B, S_LEN, D), np.float32)
    for core, r in enumerate(results):
        b = core // 2
        oT = np.asarray(r["outT"])
        for qi, c in enumerate(owns[core]):
            out[b, c * T:(c + 1) * T, :] = oT[:, qi * T:(qi + 1) * T].T
    return out


def kernel(**inputs):
    x = np.asarray(inputs["x"])
    B, S_LEN, _ = x.shape
    in_maps, owns = make_in_maps(inputs, 2 * B)
    nc = build(S_LEN, 2 * B)
    res = run_bass_kernel_spmd(nc, in_maps, core_ids=list(range(2 * B)))
    return gather_out(res.results, owns, B, S_LEN)
```

```python
import math
import os
import numpy as np
import concourse.bass as bass
import concourse.mybir as mybir
from concourse.bass_utils import run_bass_kernel_spmd

F32 = mybir.dt.float32
BF16 = mybir.dt.bfloat16
I32 = mybir.dt.int32
AF = mybir.ActivationFunctionType
ALU = mybir.AluOpType
AX = mybir.AxisListType

D = 1024
FF = 2816
NFF = FF // 128
T = 512
EPS = 1e-6
THETA = 500000.0
N_IN = 4000
C_CQ, C_CKV, C_KR, C_DQ, C_DK, C_DV, C_GM, C_GD = 0, 256, 384, 416, 928, 1440, 1952, 2976
LAM_INIT = 0.8 - 0.6 * math.exp(-0.3 * 0.0)
SC_M = 96.0 ** -0.5
SC_D = 64.0 ** -0.5
MAGIC = 12582912.0
OWN_A = [0, 3, 4, 7, 8, 11, 12, 15]
OWN_B = [1, 2, 5, 6, 9, 10, 13, 14]

ENGS = ("sp", "act", "dve", "pool", "pe")
KF = os.environ.get("KOPT", "1111")


def A(*a, **k):
    return (a, k)


class Reg:
    __slots__ = ("w", "r", "const", "excl")

    def __init__(self, const=False, excl=False):
        self.w = None
        self.r = []
        self.const = const
        self.excl = excl


class Sched:
    def __init__(self, nc, n_dma_sems=12):
        self.nc = nc
        self.q = {e: [] for e in ENGS}
        self.cnt = {e: 0 for e in ENGS}
        self.seen = {e: {} for e in ENGS}
        self.sem = {e: nc.alloc_semaphore("s_" + e) for e in ENGS}
        self.dma_sems = {}
        self.n_dma_sems = n_dma_sems
        self.dma_rr = {}
        self.dma_val = {}

    def _wait(self, eng, tok):
        if tok is None:
            return
        key, val = tok
        if key == eng and val > self.cnt[eng]:
            return
        if self.seen[eng].get(key, 0) >= val:
            return
        self.seen[eng][key] = val
        sem = self.sem[key] if key in self.sem else self.dma_sems[key]
        self.q[eng].append(lambda e, sem=sem, val=val: e.wait_ge(sem, val))

    def _deps(self, reads, writes, deps):
        toks = list(deps)
        for r in reads:
            if r.w is not None:
                toks.append(r.w)
            if r.excl:
                toks.extend(r.r)
        for w in writes:
            if w.w is not None:
                toks.append(w.w)
            toks.extend(w.r)
        return toks

    @staticmethod
    def _merge(toks):
        best = {}
        for t in toks:
            if t is None:
                continue
            if best.get(t[0], 0) < t[1]:
                best[t[0]] = t[1]
        return list(best.items())

    def _record(self, tok, reads, writes):
        for r in reads:
            if r.const:
                continue
            r.r.append(tok)
            if len(r.r) > 16:
                best = {}
                for k, v in r.r:
                    if best.get(k, 0) < v:
                        best[k] = v
                r.r = list(best.items())
        for w in writes:
            w.w = tok
            w.r = []

    def _need(self, eng, tok):
        key, val = tok
        if key == eng and val > self.cnt[eng]:
            return None
        if self.seen[eng].get(key, 0) >= val:
            return None
        self.seen[eng][key] = val
        sem = self.sem[key] if key in self.sem else self.dma_sems[key]
        return (sem, val)

    def op(self, eng, fn, reads=(), writes=(), sig=True, deps=()):
        if isinstance(fn, tuple):
            m_, (a_, k_) = fn
            fn = lambda e, m_=m_, a_=a_, k_=k_: getattr(e, m_)(*a_, **k_)
        needs = []
        for t in self._merge(self._deps(reads, writes, deps)):
            n_ = self._need(eng, t)
            if n_ is not None:
                needs.append(n_)
        for (sem_, val_) in needs[:-1]:
            self.q[eng].append(lambda e, sem_=sem_, val_=val_: e.wait_ge(sem_, val_))
        att = needs[-1] if needs else None
        if sig:
            self.cnt[eng] += 1
            sem = self.sem[eng]
            if att is not None:
                self.q[eng].append(lambda e, fn=fn, sem=sem, att=att: fn(e)._wait_ge(att[0], att[1]).then_inc(sem, 1))
            else:
                self.q[eng].append(lambda e, fn=fn, sem=sem: fn(e).then_inc(sem, 1))
            tok = (eng, self.cnt[eng])
        else:
            if att is not None:
                self.q[eng].append(lambda e, fn=fn, att=att: fn(e)._wait_ge(att[0], att[1]))
            else:
                self.q[eng].append(lambda e, fn=fn: fn(e))
            tok = (eng, self.cnt[eng] + 1)
        self._record(tok, reads, writes)
        return tok

    def dma(self, eng, out, in_, reads=(), writes=(), deps=()):
        toks = self._deps(reads, writes, deps)
        rr = self.dma_rr.get(eng, 0)
        self.dma_rr[eng] = (rr + 1) % (6 if (eng == "pool" and KF[3] == "1") else self.n_dma_sems)
        key = "dma_%s_%d" % (eng, rr)
        if key not in self.dma_sems:
            self.dma_sems[key] = self.nc.alloc_semaphore(key)
            self.dma_val[key] = 0
        if self.dma_val[key] > 0:
            toks.append((key, self.dma_val[key]))
        for t in toks:
            self._wait(eng, t)
        self.dma_val[key] += 16
        sem = self.dma_sems[key]
        self.q[eng].append(lambda e, out=out, in_=in_, sem=sem: e.dma_start(out=out, in_=in_).then_inc(sem, 16))
        tok = (key, self.dma_val[key])
        self._record(tok, reads, writes)
        return tok

    def coll(self, kind, op, groups, in_ap, out_ap, reads=(), writes=()):
        eng = "pool"
        toks = self._deps(reads, writes, ())
        i = self.dma_rr.get("cc", 0)
        self.dma_rr["cc"] = i + 1
        key = "cc_%d" % (i % 8)
        if key not in self.dma_sems:
            self.dma_sems[key] = self.nc.alloc_semaphore(key)
            self.dma_val[key] = 0
        if self.dma_val[key] > 0:
            toks.append((key, self.dma_val[key]))
        for t in self._merge(toks):
            self._wait(eng, t)
        self.dma_val[key] += 1
        sem = self.dma_sems[key]
        self.q[eng].append(lambda e: e.collective_compute(kind, op, replica_groups=groups, ins=[in_ap], outs=[out_ap]).then_inc(sem, 1))
        tok = (key, self.dma_val[key])
        self._record(tok, reads, writes)
        return tok

    def barrier(self):
        toks = [(e, self.cnt[e]) for e in ENGS if self.cnt[e] > 0]
        toks += [(k, v) for k, v in self.dma_val.items() if v > 0]
        for e in ENGS:
            for t in toks:
                self._wait(e, t)

    def emit(self):
        q = self.q
        with self.nc.Block() as block:
            @block.sync
            def _(e):
                for f in q["sp"]:
                    f(e)

            @block.scalar
            def _(e):
                for f in q["act"]:
                    f(e)

            @block.vector
            def _(e):
                for f in q["dve"]:
                    f(e)

            @block.gpsimd
            def _(e):
                for f in q["pool"]:
                    f(e)

            @block.tensor
            def _(e):
                for f in q["pe"]:
                    f(e)


class SB:
    def __init__(self, nc):
        self.nc = nc
        self.off = 16512
        self.top = 229344
        self.n = 0

    def alloc(self, shape, dt):
        n = 1
        for s in shape[1:]:
            n *= s
        nb = n * (2 if dt == BF16 else 4)
        nb = (nb + 31) // 32 * 32
        t = self.nc.alloc_sbuf_tensor_at("t%d" % self.n, list(shape), dt, offset=self.off)
        self.n += 1
        self.off += nb
        assert self.off <= self.top, "SBUF overflow %d" % self.off
        return t


def build(S_LEN, n_cores=8):
    NT = S_LEN // T
    NO = NT // 2
    own = list(range(NO))
    nc = bass.Bass("TRN2", target_bir_lowering=False)

    def din(name, shape, dt=F32):
        return nc.dram_tensor(name, list(shape), dt, kind="ExternalInput").ap()

    def dscr(name, shape, dt):
        return nc.dram_tensor(name, list(shape), dt, kind="Internal").ap()

    xT = din("xT", [D, S_LEN])
    posr = din("posr", [128, S_LEN], I32)
    gv = din("gv", [128, 36])
    lamr = din("lamr", [128, 256])
    fsc = din("fsc", [128, 2])
    cmat = din("cmat", [128, 3, 128])
    flag = din("flag", [128, 1])
    w1g = din("w1g", [D, FF]); w1u = din("w1u", [D, FF]); w1d = din("w1d", [FF, D])
    w2g = din("w2g", [D, FF]); w2u = din("w2u", [D, FF]); w2d = din("w2d", [FF, D])
    win = din("win", [D, N_IN])
    wuq = din("wuq", [256, 768]); wukv = din("wukv", [128, 1024])
    wpm = din("wpm", [512, D]); wpd = din("wpd", [512, D]); wout = din("wout", [D, D])
    outT = nc.dram_tensor("outT", [D, NO * T], F32, kind="ExternalOutput").ap()

    x1o = dscr("x1o", [NO, D, T], F32)
    x1s = dscr("x1s", [NO, D, T], F32)
    groups = [[2 * i, 2 * i + 1] for i in range(n_cores // 2)]
    Rxo = [Reg() for _ in range(NO)]; Rxsum = [Reg() for _ in range(NO)]
    dkT = dscr("dkT", [4, 128, S_LEN], BF16)
    dvS = dscr("dvS", [S_LEN, 512], BF16)
    qTs = dscr("qTs", [8, 96, NO * T], BF16)
    dqT = dscr("dqT", [4, 128, NO * T], BF16)
    ymT = dscr("ymT", [512, NO * T], BF16)
    ydT = dscr("ydT", [512, NO * T], BF16)
    x2T = dscr("x2T", [D, NO * T], F32)

    S = Sched(nc)
    sb = SB(nc)
    psall = nc.alloc_psum_tensor("psall", [128, 8 * T], F32)
    ps = [psall[:, i * T:(i + 1) * T] for i in range(8)]

    def pspair(a):
        return psall[:, a * T:(a + 2) * T].rearrange("p (b t) -> p b t", t=T)
    Rps = [Reg(excl=True) for _ in range(8)]

    ones_bf = sb.alloc([128, 128], BF16)
    ones_f = sb.alloc([128, 64], F32)
    cm = sb.alloc([128, 3, 128], BF16)
    gvt = sb.alloc([128, 36], F32)
    fst = sb.alloc([128, 2], F32)
    lamt = sb.alloc([128, 256], F32)
    lamw = sb.alloc([128, 8], F32)
    flagt = sb.alloc([128, 1], F32)
    Rc = Reg(const=True)
    S.op("pool", ("memset", A(ones_bf[:], 1.0)), writes=[Rc])
    S.op("pool", ("memset", A(ones_f[:], 1.0)), writes=[Rc])
    S.dma("pool", cm[:], cmat, writes=[Rc])
    S.dma("sp", gvt[:], gv, writes=[Rc])
    S.dma("sp", fst[:], fsc, writes=[Rc])
    S.dma("sp", lamt[:], lamr, writes=[Rc])
    S.dma("sp", flagt[:], flag, writes=[Rc])
    Rl = Reg()
    S.op("dve", ("tensor_tensor", A(out=lamt[:, 0:64], in0=lamt[:, 0:64], in1=lamt[:, 64:128], op=ALU.mult)), reads=[Rc], writes=[Rl])
    S.op("dve", ("tensor_tensor", A(out=lamt[:, 128:192], in0=lamt[:, 128:192], in1=lamt[:, 192:256], op=ALU.mult)), reads=[Rc], writes=[Rl])
    S.op("dve", ("reduce_sum", A(out=lamw[:, 0:1], in_=lamt[:, 0:64], axis=AX.X)), reads=[Rl], writes=[Rl])
    S.op("dve", ("reduce_sum", A(out=lamw[:, 1:2], in_=lamt[:, 128:192], axis=AX.X)), reads=[Rl], writes=[Rl])
    S.op("act", ("activation", A(out=lamw[:, 2:4], in_=lamw[:, 0:2], func=AF.Exp)), reads=[Rl], writes=[Rl])
    S.op("dve", ("tensor_tensor", A(out=lamw[:, 4:5], in0=lamw[:, 3:4], in1=lamw[:, 2:3], op=ALU.subtract)), reads=[Rl], writes=[Rl])
    S.op("dve", ("tensor_scalar", A(out=lamw[:, 4:5], in0=lamw[:, 4:5], scalar1=-LAM_INIT, scalar2=None, op0=ALU.add)), reads=[Rl], writes=[Rl])
    S.op("dve", ("tensor_scalar", A(out=lamw[:, 5:6], in0=gvt[:, 35:36], scalar1=1.0 - LAM_INIT, scalar2=None, op0=ALU.mult)), reads=[Rl, Rc], writes=[Rl])
    S.op("dve", ("tensor_scalar", A(out=lamw[:, 6:7], in0=flagt[:, 0:1], scalar1=-1.0, scalar2=30000.0, op0=ALU.add, op1=ALU.mult)), reads=[Rc], writes=[Rl])
    S.barrier()
    G_FFN1, G_MIX, G_FFN2, G_FIN, G_Q, G_KV = 0, 8, 16, 24, 32, 34
    neglam = lamw[:, 4:5]
    sgain = lamw[:, 5:6]
    fbias = lamw[:, 6:7]
    base_off = sb.off

    def mm(out, lhsT, rhs, start, stop, reads, writes, sig):
        return S.op("pe", ("matmul", A(out, lhsT, rhs, start=start, stop=stop)), reads=reads, writes=writes, sig=sig)

    def rmsnorm(src, Rsrc, C, nfeat, gcol, sqbuf, Rsq, out, Rout, pbank, rstd, Rrstd):
        snap = ([Rsq.w] if Rsq.w is not None else []) + list(Rsq.r)
        Rsqc = [Reg() for _ in range(C)]
        for c in range(C):
            if c % 2 == 0:
                S.op("act", ("activation", A(out=sqbuf(c), in_=src(c), func=AF.Square)), reads=[Rsrc], writes=[Rsqc[c]], deps=snap)
            else:
                S.op("dve", ("tensor_tensor", A(out=sqbuf(c), in0=src(c), in1=src(c), op=ALU.mult)), reads=[Rsrc], writes=[Rsqc[c]], deps=snap)
        for c in range(C):
            mm(ps[pbank][:], ones_bf[:, :], sqbuf(c), c == 0, c == C - 1, [Rsqc[c], Rsq, Rc], [Rps[pbank]], c == C - 1)
        S.op("act", ("activation", A(out=rstd[:], in_=ps[pbank][:], func=AF.Ln, bias=EPS, scale=1.0 / nfeat)), reads=[Rps[pbank]], writes=[Rrstd])
        S.op("act", ("activation", A(out=rstd[:], in_=rstd[:], func=AF.Exp, scale=-0.5)), reads=[Rrstd], writes=[Rrstd])
        for c in range(C):
            S.op("dve", ("scalar_tensor_tensor", A(out=out(c), in0=src(c), scalar=gvt[:, gcol + c:gcol + c + 1], in1=rstd[:], op0=ALU.mult, op1=ALU.mult)),
                 reads=[Rsrc, Rrstd, Rc], writes=[Rout])

    def load_w(dst, src_ap, R, pieces, axis_len, dim):
        step = axis_len // pieces
        for i in range(pieces):
            sl = slice(i * step, (i + 1) * step)
            if dim == 3:
                S.dma("pool", dst[:, :, sl], src_ap[:, :, sl], writes=[R])
            else:
                S.dma("pool", dst[:, sl], src_ap[:, sl], writes=[R])

    def ffn_phase(x_in, n_tiles, wg, wu, wd, gpre, x_out, out_tiles, final_norm, after_store=None):
        sb.off = base_off
        Wg = sb.alloc([128, 8, FF], BF16); Wu = sb.alloc([128, 8, FF], BF16); Wd = sb.alloc([128, NFF, D], BF16)
        xb = [sb.alloc([128, 8, T], F32) for _ in range(2)]
        hT = sb.alloc([128, 8, T], BF16)
        aT = sb.alloc([128, NFF, T], BF16)
        rstd = sb.alloc([128, T], F32)
        sil = [sb.alloc([128, T], BF16) for _ in range(2)]
        Rx = [Reg(), Reg()]; Rh = Reg(); Ra = [Reg() for _ in range(NFF)]; Rr = Reg(); Rs = [Reg(), Reg()]

        def load_x(t):
            S.dma("sp", xb[t % 2][:], x_in(t).rearrange("(c p) t -> p c t", p=128), writes=[Rx[t % 2]])

        load_x(0)
        NPC = 4
        FPC = 6
        RWg = [Reg(True) for _ in range(NPC)]; RWu = [Reg(True) for _ in range(NPC)]; RWd = [Reg(True) for _ in range(2)]
        wg3 = wg.rearrange("(c p) f -> p c f", p=128); wu3 = wu.rearrange("(c p) f -> p c f", p=128); wd3 = wd.rearrange("(c p) f -> p c f", p=128)
        for pc in range(NPC):
            sl = slice(pc * FPC * 128, min((pc + 1) * FPC * 128, FF))
            S.dma("pool", Wg[:, :, sl], wg3[:, :, sl], writes=[RWg[pc]])
            S.dma("pool", Wu[:, :, sl], wu3[:, :, sl], writes=[RWu[pc]])
        for d2 in range(2):
            sl = slice(d2 * 512, (d2 + 1) * 512)
            S.dma("pool", Wd[:, :, sl], wd3[:, :, sl], writes=[RWd[d2]])

        def norm_pre(t):
            x = xb[t % 2]
            rmsnorm(lambda c: x[:, c, :], Rx[t % 2], 8, D, gpre, lambda c: hT[:, c, :], Rh, lambda c: hT[:, c, :], Rh, 0, rstd, Rr)

        if KF[0] == "1":
            norm_pre(0)
        for t in range(n_tiles):
            x = xb[t % 2]; Rxc = Rx[t % 2]
            if t + 1 < n_tiles:
                load_x(t + 1)
            if KF[0] != "1":
                norm_pre(t)
            for f in range(NFF):
                pg, pu = 1 + f % 2, 3 + f % 2
                for k in range(8):
                    mm(ps[pg][:], Wg[:, k, f * 128:(f + 1) * 128], hT[:, k, :], k == 0, k == 7, [RWg[f // FPC], Rh], [Rps[pg]], k == 7)
                for k in range(8):
                    mm(ps[pu][:], Wu[:, k, f * 128:(f + 1) * 128], hT[:, k, :], k == 0, k == 7, [RWu[f // FPC], Rh], [Rps[pu]], k == 7)
                S.op("act", ("activation", A(out=sil[f % 2][:], in_=ps[pg][:], func=AF.Silu)), reads=[Rps[pg]], writes=[Rs[f % 2]])
                S.op("dve", ("tensor_tensor", A(out=aT[:, f, :], in0=ps[pu][:], in1=sil[f % 2][:], op=ALU.mult)), reads=[Rps[pu], Rs[f % 2]], writes=[Ra[f]])
            for d in range(8):
                po = 5 + d % 2
                for f in range(NFF):
                    mm(ps[po][:], Wd[:, f, d * 128:(d + 1) * 128], aT[:, f, :], f == 0, f == NFF - 1, [RWd[d // 4], Ra[f]], [Rps[po]], f == NFF - 1)
                S.op("dve", ("scalar_tensor_tensor", A(out=x[:, d, :], in0=ps[po][:], scalar=0.5, in1=x[:, d, :], op0=ALU.mult, op1=ALU.add)),
                     reads=[Rps[po]], writes=[Rxc])
                if KF[0] == "1" and d == 1 and t + 1 < n_tiles:
                    norm_pre(t + 1)
            if final_norm:
                rmsnorm(lambda c: x[:, c, :], Rxc, 8, D, G_FIN, lambda c: aT[:, c, :], Ra[0], lambda c: x[:, c, :], Rxc, 0, rstd, Rr)
            S.dma("pool", x_out(t).rearrange("(c p) t -> p c t", p=128), x[:], reads=[Rxc], writes=out_tiles(t))
            if after_store is not None:
                after_store(t)
        S.barrier()


    def exchange(t):
        S.coll("AllReduce", ALU.add, groups, x1o[t], x1s[t], reads=[Rxo[t]], writes=[Rxsum[t]])

    ffn_phase(lambda t: xT[:, t * T:(t + 1) * T], NO, w1g, w1u, w1d, G_FFN1, lambda t: x1o[t], lambda t: [Rxo[t]], False, after_store=exchange)

    sb.off = base_off
    kvnT = sb.alloc([128, S_LEN], BF16)
    KT2 = sb.alloc([128, 2, S_LEN], BF16)
    bc_off = sb.off
    Rkvn = Reg(); Rkt = [Reg(), Reg()]; Rkr = Reg(); Rkz = Reg()
    S.op("pool", ("memset", A(KT2[96:128, :, :], 0.0)), writes=[Rkz])
    Win = sb.alloc([128, 8, C_GM], BF16)
    Wuq = sb.alloc([128, 2, 768], BF16)
    RWin, RWuq = Reg(True), Reg(True)
    load_w(Win, win.rearrange("(c p) f -> p c f", p=128)[:, :, 0:C_GM], RWin, 2, C_GM, 3)
    load_w(Wuq, wuq.rearrange("(c p) f -> p c f", p=128), RWuq, 1, 768, 3)
    xb = [sb.alloc([128, 8, T], F32) for _ in range(2)]
    posi = [sb.alloc([128, T], I32) for _ in range(2)]
    h2 = sb.alloc([128, 8, T], BF16)
    rstd = sb.alloc([128, T], F32)
    posf = sb.alloc([128, T], F32)
    trgb = [sb.alloc([128, 4, T], F32) for _ in range(2)]
    ta = sb.alloc([128, T], F32); tb = sb.alloc([128, T], F32)
    ckv = sb.alloc([128, 2, T], F32)
    sqb = sb.alloc([128, 2, T], BF16)
    cqn = sb.alloc([128, 2, T], BF16)
    xs = [sb.alloc([128, T], BF16) for _ in range(2)]
    t1 = [sb.alloc([128, T], F32) for _ in range(2)]
    t2 = [sb.alloc([128, T], F32) for _ in range(2)]
    dk_st = sb.alloc([128, 4, T], BF16)
    stg_off = sb.off
    xo_full = sb.alloc([128, 8, T], F32)
    q_st = nc.alloc_sbuf_tensor_at("q_st_alias", [96, 8, T], BF16, offset=stg_off)
    dq_st = nc.alloc_sbuf_tensor_at("dq_st_alias", [128, 4, T], BF16, offset=stg_off + 8 * T * 2)
    dv_st = sb.alloc([128, 4, 512], BF16)
    Rx = [Reg(), Reg()]; Rpi = [Reg(), Reg()]; Rh2 = Reg(); Rr = Reg(); Rpf = Reg(); Rtrgb = [Reg(), Reg()]; Rta = Reg(); Rtb = Reg()
    Rckv = Reg(); Rsqb = Reg(); Rcqn = Reg(); Rxs = [Reg(), Reg()]; Rt1 = [Reg(), Reg()]; Rt2 = [Reg(), Reg()]
    Rdk = Reg(); Rdq = Reg(); Rq = Reg(); Rdv = Reg()
    rope_i = [0]
    cur = {}
    h2b = [h2, sb.alloc([128, 8, T], BF16)]
    Rh2b = [Rh2, Reg()]
    pend_rope = []

    def flush_rope():
        while pend_rope:
            pend_rope.pop(0)()

    def rope(pbank, R, tab, outs):
        i = rope_i[0] % 2
        rope_i[0] += 1
        pw = 4 + i
        S.op("act", ("activation", A(out=xs[i][0:R, :], in_=ps[pbank][0:R, :], func=AF.Identity)), reads=[Rps[pbank]], writes=[Rxs[i]])
        r0 = min(o[2] for o in outs); r1 = max(o[3] for o in outs)
        S.op("dve", ("tensor_tensor", A(out=t1[i][r0:r1, :], in0=ps[pbank][r0:r1, :], in1=cur['trg'][r0:r1, 2 * tab, :], op=ALU.mult)), reads=[Rps[pbank], cur['Rtrg']], writes=[Rt1[i]])

        trg_, Rtrg_ = cur['trg'], cur['Rtrg']

        def stage_b():
            mm(ps[pw][0:R, :], cm[0:R, 1 + tab, 0:R], xs[i][0:R, :], True, True, [Rxs[i], Rc], [Rps[pw]], True)
            S.op("dve", ("tensor_tensor", A(out=t2[i][r0:r1, :], in0=ps[pw][r0:r1, :], in1=trg_[r0:r1, 2 * tab + 1, :], op=ALU.mult)), reads=[Rps[pw], Rtrg_], writes=[Rt2[i]])
            for (apf, Ro, a, b) in outs:
                S.op("pool", ("tensor_tensor", A(out=apf, in0=t1[i][a:b, :], in1=t2[i][a:b, :], op=ALU.add)), reads=[Rt1[i], Rt2[i]], writes=[Ro])
        if KF[1] == "1":
            pend_rope.append(stage_b)
        else:
            stage_b()

    Rxoh = Reg()
    pend_sub = []

    def load_b(t):
        if t < NO:
            S.dma("sp", xb[t % 2][:], x1o[t].rearrange("(c p) t -> p c t", p=128), writes=[Rx[t % 2]])
        else:
            j = t - NO
            S.dma("sp", xb[t % 2][:], x1s[j].rearrange("(c p) t -> p c t", p=128), reads=[Rxsum[j]], writes=[Rx[t % 2]])
            S.dma("sp", xo_full[:], x1o[j].rearrange("(c p) t -> p c t", p=128), writes=[Rxoh, Rq, Rdq])

            def sub(t=t):
                S.op("dve", ("tensor_tensor", A(out=xb[t % 2][:], in0=xb[t % 2][:], in1=xo_full[:], op=ALU.subtract)), reads=[Rxoh], writes=[Rx[t % 2]])
            pend_sub.append(sub)
        S.dma("sp", posi[t % 2][:], posr[:, t * T:(t + 1) * T], writes=[Rpi[t % 2]])

    load_b(0)
    pp_i = [0]

    def proj(cols, M, srcs, W, RW, nk):
        pb = 1 + pp_i[0] % 3
        pp_i[0] += 1
        for k in range(nk):
            mm(ps[pb][0:M, :], W[:, k, cols], srcs[0](k), k == 0, k == nk - 1, [RW, srcs[1]], [Rps[pb]], k == nk - 1)
        flush_rope()
        return pb

    def norm_b(t):
        x = xb[t % 2]
        hh = h2b[t % 2]
        rmsnorm(lambda c: x[:, c, :], Rx[t % 2], 8, D, G_MIX, lambda c: hh[:, c, :], Rh2b[t % 2], lambda c: hh[:, c, :], Rh2b[t % 2], 0, rstd, Rr)

    def trig_b(t):
        trg, Rtrg = trgb[t % 2], Rtrgb[t % 2]
        S.op("dve", ("tensor_copy", A(out=posf[:], in_=posi[t % 2][:])), reads=[Rpi[t % 2]], writes=[Rpf])
        for tab in range(2):
            for cs in range(2):
                S.op("dve", ("tensor_scalar", A(out=ta[:], in0=posf[:], scalar1=fst[:, tab:tab + 1], scalar2=(0.25 if cs == 0 else 0.0), op0=ALU.mult, op1=ALU.add)),
                     reads=[Rpf, Rc], writes=[Rta])
                S.op("dve", ("tensor_scalar", A(out=tb[:], in0=ta[:], scalar1=MAGIC, scalar2=None, op0=ALU.add)), reads=[Rta], writes=[Rtb])
                S.op("dve", ("scalar_tensor_tensor", A(out=ta[:], in0=tb[:], scalar=MAGIC, in1=ta[:], op0=ALU.subtract, op1=ALU.subtract)), reads=[Rtb, Rta], writes=[Rta])
                S.op("act", ("activation", A(out=trg[:, 2 * tab + cs, :], in_=ta[:], func=AF.Sin, scale=-2.0 * math.pi * 0.999999)), reads=[Rta], writes=[Rtrg])

    norm_b(0)
    trig_b(0)
    for t in range(NT):
        if t + 1 < NT:
            load_b(t + 1)
        cols_t = slice(t * T, (t + 1) * T)
        hh = h2b[t % 2]; Rhh = Rh2b[t % 2]
        cur['trg'] = trgb[t % 2]; cur['Rtrg'] = Rtrgb[t % 2]
        hsrc = (lambda k: hh[:, k, :], Rhh)
        is_own = t in own
        pb = proj(slice(C_CKV, C_CKV + 128), 128, hsrc, Win, RWin, 8)
        S.op("act", ("activation", A(out=ckv[:, 0, :], in_=ps[pb][:], func=AF.Identity)), reads=[Rps[pb]], writes=[Rckv])
        rmsnorm(lambda c: ckv[:, c, :], Rckv, 1, 128, G_KV, lambda c: sqb[:, c, :], Rsqb, lambda c: kvnT[:, cols_t], Rkvn, 7, rstd, Rr)
        if is_own:
            qi = own.index(t)
            cols_q = slice(qi * T, (qi + 1) * T)
            for c in range(2):
                pb = proj(slice(C_CQ + 128 * c, C_CQ + 128 * (c + 1)), 128, hsrc, Win, RWin, 8)
                S.op("act", ("activation", A(out=ckv[:, c, :], in_=ps[pb][:], func=AF.Identity)), reads=[Rps[pb]], writes=[Rckv])
            rmsnorm(lambda c: ckv[:, c, :], Rckv, 2, 256, G_Q, lambda c: sqb[:, c, :], Rsqb, lambda c: cqn[:, c, :], Rcqn, 7, rstd, Rr)
        pb = proj(slice(C_KR - 64, C_KR + 32), 96, hsrc, Win, RWin, 8)
        rope(pb, 96, 0, [(KT2[64:96, 0, cols_t], Rkr, 64, 96), (KT2[64:96, 1, cols_t], Rkr, 64, 96)])
        for j in range(4):
            pb = proj(slice(C_DK + 128 * j, C_DK + 128 * (j + 1)), 128, hsrc, Win, RWin, 8)
            rope(pb, 128, 1, [(dk_st[:, j, :], Rdk, 0, 128)])
        if t + 1 < NT:
            while pend_sub:
                pend_sub.pop(0)()
            norm_b(t + 1)
            trig_b(t + 1)
        if is_own:
            for j in range(4):
                pb = proj(slice(C_DQ + 128 * j, C_DQ + 128 * (j + 1)), 128, hsrc, Win, RWin, 8)
                rope(pb, 128, 1, [(dq_st[:, j, :], Rdq, 0, 128)])
            for h in range(8):
                pb = proj(slice(96 * h, 96 * (h + 1)), 96, (lambda k: cqn[:, k, :], Rcqn), Wuq, RWuq, 2)
                rope(pb, 96, 0, [(q_st[0:96, h, :], Rq, 0, 96)])
        for sub in range(4):
            pv = 6
            for k in range(8):
                mm(ps[pv][:], hh[:, k, sub * 128:(sub + 1) * 128], Win[:, k, C_DV:C_DV + 512], k == 0, k == 7, [Rhh, RWin], [Rps[pv]], k == 7)
            if sub == 0:
                flush_rope()
            S.op("act", ("activation", A(out=dv_st[:, sub, :], in_=ps[pv][:], func=AF.Identity)), reads=[Rps[pv]], writes=[Rdv])
        S.dma("pool", dvS[t * T:(t + 1) * T, :].rearrange("(s p) c -> p s c", p=128), dv_st[:], reads=[Rdv])
        flush_rope()
        S.dma("pool", dkT[:, :, cols_t].rearrange("j p t -> p j t"), dk_st[:], reads=[Rdk])
        if is_own:
            S.dma("pool", dqT[:, :, cols_q].rearrange("j p t -> p j t"), dq_st[:], reads=[Rdq])
            S.dma("pool", qTs[:, :, cols_q].rearrange("h r t -> r h t"), q_st[:], reads=[Rq])
    S.barrier()

    sb.off = bc_off
    WkK = sb.alloc([128, 8, 64], BF16); WkV = sb.alloc([128, 8, 64], BF16)
    RWk = Reg(True)
    wukv4 = wukv.rearrange("k (h two d) -> k h two d", two=2, d=64)
    S.dma("pool", WkK[:], wukv4[:, :, 0, :], writes=[RWk])
    S.dma("pool", WkV[:], wukv4[:, :, 1, :], writes=[RWk])
    NKT = S_LEN // 128
    c_off = sb.off
    Vh = [sb.alloc([128, NKT, 128], BF16) for _ in range(2)]
    QT = [sb.alloc([128, NO * T], BF16) for _ in range(2)]
    PT = [sb.alloc([128, T], BF16) for _ in range(4)]
    rc = sb.alloc([128, T], F32)
    bcs = sb.alloc([64, T], F32)
    yst = [sb.alloc([64, T], BF16) for _ in range(2)]
    RV = [Reg(), Reg()]; RQ = [Reg(), Reg()]; RPT = [Reg() for _ in range(4)]; Rrc = Reg(); Rbcs = Reg(); Ryst = [Reg(), Reg()]
    RQz = Reg()
    for b_ in range(2):
        S.op("pool", ("memset", A(Vh[b_][:], 1.0)), writes=[RV[b_]])
        S.op("pool", ("memset", A(QT[b_][96:128, :], 0.0)), writes=[RQz])

    def key_tiles(i):
        L = []
        for j in range(i):
            for s_ in range(4):
                L.append((j * 4 + s_, 0, 0))
            for s_ in range(4):
                L.append(((NO + j) * 4 + s_, 0, 0))
        for s_ in range(4):
            L.append((i * 4 + s_, 1, s_))
        for s_ in range(4):
            L.append(((NO + i) * 4 + s_, 2, 0))
        return L

    blocks = []
    for h in range(8):
        for qi in range(NO):
            L = key_tiles(qi)
            for n_, (kt, kind, sub) in enumerate(L):
                blocks.append((h, qi, kt, kind, sub, n_ == 0, n_ == len(L) - 1))
    LA = 2
    pending = []
    fin_i = [0]

    def mla_prologue_steps(h):
        b = h % 2
        steps = []

        def kstep(g):
            mm(ps[7][0:64, :], WkK[:, h, :], kvnT[:, g * T:(g + 1) * T], True, True, [Rkvn, RWk], [Rps[7]], True)
            S.op("dve", ("tensor_copy", A(out=KT2[0:64, b, g * T:(g + 1) * T], in_=ps[7][0:64, :])), reads=[Rps[7]], writes=[Rkt[b]])

        def vstep(g8):
            for j in range(8):
                kt = g8 * 8 + j
                mm(ps[7][:, j * 64:(j + 1) * 64], kvnT[:, kt * 128:(kt + 1) * 128], WkV[:, h, :], True, True, [Rkvn, RWk], [Rps[7]], j == 7)
            S.op("dve", ("tensor_copy", A(out=Vh[b][:, g8 * 8:(g8 + 1) * 8, 0:64], in_=ps[7][:].rearrange("p (j d) -> p j d", d=64))), reads=[Rps[7]], writes=[RV[b]])

        steps.append(lambda: S.dma("sp", QT[b][0:96, :], qTs[h], writes=[RQ[b]]))
        for g in range(S_LEN // T):
            steps.append(lambda g=g: kstep(g))
        for g8 in range(NKT // 8):
            steps.append(lambda g8=g8: vstep(g8))
        return steps

    pro_steps = []

    def mla_head_prologue(h):
        while pro_steps:
            pro_steps.pop(0)()
        if h == 0:
            for st_ in mla_prologue_steps(0):
                st_()
        if h + 1 < 8:
            pro_steps.extend(mla_prologue_steps(h + 1))

    units = []
    for h in range(8):
        for qi in range(NO):
            L = key_tiles(qi)
            n_ = 0
            while n_ < len(L):
                kt, kind, sub = L[n_]
                if kind != 1 and n_ + 1 < len(L) and L[n_ + 1][1] == kind:
                    blks = [L[n_], L[n_ + 1]]
                    n_ += 2
                else:
                    blks = [L[n_]]
                    n_ += 1
                units.append((h, qi, kind, blks, blks[0] is L[0], blks[-1] is L[-1]))
    PT2 = [sb.alloc([128, 2, T], BF16) for _ in range(3)]
    RPT2 = [Reg() for _ in range(3)]

    def mla_S(u):
        h, qi, kind, blks, first, last = units[u]
        if qi == 0 and first:
            mla_head_prologue(h)
        b = h % 2
        a = (u % 2) * 2
        pb = u % 3
        sub = blks[0][2]
        q0 = 128 * sub
        w = T - q0
        nb = len(blks)
        for bi, (kt, _, _) in enumerate(blks):
            mm(ps[a + bi][:, 0:w], KT2[:, b, kt * 128:(kt + 1) * 128], QT[b][:, qi * T + q0:(qi + 1) * T], True, True,
               [Rkt[b], Rkr, Rkz, RQ[b], RQz], [Rps[a + bi]], bi == nb - 1)
        kw = dict(func=AF.Exp, scale=SC_M)
        rd = [Rps[a + bi] for bi in range(nb)]
        if kind == 2:
            kw["bias"] = fbias
            rd.append(Rl)
        S.op("act", ("activation", A(out=PT2[pb][:, 0:nb, 0:w], in_=pspair(a)[:, 0:nb, 0:w], **kw)), reads=rd, writes=[RPT2[pb]])
        if kind == 1:
            S.op("dve", ("tensor_tensor", A(out=PT2[pb][:, 0, 0:128], in0=PT2[pb][:, 0, 0:128], in1=cm[:, 0, :], op=ALU.mult)), reads=[RPT2[pb], Rc], writes=[RPT2[pb]])

    def mla_PV(u):
        h, qi, kind, blks, first, last = units[u]
        pb = u % 3
        sub = blks[0][2]
        q0 = 128 * sub
        w = T - q0
        ob = 4 + fin_i[0] % 2
        nb = len(blks)
        for bi, (kt, _, _) in enumerate(blks):
            st = first and bi == 0
            sp_ = last and bi == nb - 1
            mm(ps[ob][:, q0:T], Vh[h % 2][:, kt, :], PT2[pb][:, bi, 0:w], st, sp_, [RV[h % 2], RPT2[pb]], [Rps[ob]], sp_ or bi == nb - 1)
        if last:
            f = fin_i[0] % 2
            fin_i[0] += 1
            S.op("act", ("activation", A(out=rc[64:65, :], in_=ps[ob][64:65, :], func=AF.Ln)), reads=[Rps[ob]], writes=[Rrc])
            S.op("act", ("activation", A(out=rc[64:65, :], in_=rc[64:65, :], func=AF.Exp, scale=-1.0)), reads=[Rrc], writes=[Rrc])

            def fin_pe(ob=ob, f=f, h=h, qi=qi):
                mm(ps[6][0:64, :], ones_f[64:65, 0:64], rc[64:65, :], True, True, [Rrc, Rc], [Rps[6]], True)
                S.op("dve", ("tensor_copy", A(out=bcs[:], in_=ps[6][0:64, :])), reads=[Rps[6]], writes=[Rbcs])
                S.op("dve", ("tensor_tensor", A(out=yst[f][:], in0=ps[ob][0:64, :], in1=bcs[:], op=ALU.mult)), reads=[Rps[ob], Rbcs], writes=[Ryst[f]])
                S.dma("pool", ymT[h * 64:(h + 1) * 64, qi * T:(qi + 1) * T], yst[f][:], reads=[Ryst[f]])
            pending.append([2, fin_pe])

    def run_pending(force=False):
        for p in list(pending):
            p[0] -= 1
            if p[0] <= 0 or force:
                p[1]()
                pending.remove(p)

    n = len(units)
    for u in range(n + 1):
        if u < n:
            mla_S(u)
        if u - 1 >= 0:
            mla_PV(u - 1)
        run_pending()
        if pro_steps and u % 2 == 1:
            pro_steps.pop(0)()
    run_pending(force=True)
    S.barrier()

    sb.off = c_off
    KTd = [sb.alloc([128, S_LEN], BF16) for _ in range(2)]
    Vd = [sb.alloc([128, NKT, 128], BF16) for _ in range(2)]
    QTd = [sb.alloc([128, NO * T], BF16) for _ in range(2)]
    P12 = [sb.alloc([128, 2, T], BF16) for _ in range(3)]
    r1 = sb.alloc([128, T], F32); r2 = sb.alloc([128, T], F32)
    u1 = sb.alloc([128, T], F32); u2 = sb.alloc([128, T], F32)
    dd = sb.alloc([128, T], F32); sqd = sb.alloc([128, T], BF16); rsd = sb.alloc([128, T], F32)
    ysd = [sb.alloc([128, T], BF16) for _ in range(2)]
    RKd = [Reg(), Reg()]; RVd = [Reg(), Reg()]; RQd = [Reg(), Reg()]
    RP12 = [Reg() for _ in range(3)]
    Rr1, Rr2, Ru1, Ru2, Rdd, Rsqd, Rrsd = [Reg() for _ in range(7)]
    Rysd = [Reg(), Reg()]
    dblocks = []
    for g in range(4):
        for qi in range(NO):
            L = key_tiles(qi)
            for n_, (kt, kind, sub) in enumerate(L):
                dblocks.append((g, qi, kt, kind, sub, n_ == 0, n_ == len(L) - 1))
    dfin = [0]

    def diff_prologue(g):
        b = g % 2
        S.dma("sp", KTd[b][:], dkT[g], writes=[RKd[b]])
        S.dma("sp", Vd[b][:], dvS[:, g * 128:(g + 1) * 128].rearrange("(k p) d -> p k d", p=128), writes=[RVd[b]])
        S.dma("sp", QTd[b][:], dqT[g], writes=[RQd[b]])

    def diff_S(i):
        g, qi, kt, kind, sub, first, last = dblocks[i]
        if qi == 0 and first and g == 0:
            diff_prologue(0)
        b = g % 2
        q0 = 128 * sub
        w = T - q0
        a = (i % 2) * 2
        pb = i % 3
        mm(ps[a][:, 0:w], KTd[b][0:64, kt * 128:(kt + 1) * 128], QTd[b][0:64, qi * T + q0:(qi + 1) * T], True, True, [RKd[b], RQd[b]], [Rps[a]], False)
        mm(ps[a + 1][:, 0:w], KTd[b][64:128, kt * 128:(kt + 1) * 128], QTd[b][64:128, qi * T + q0:(qi + 1) * T], True, True, [RKd[b], RQd[b]], [Rps[a + 1]], True)
        kw = dict(func=AF.Exp, scale=SC_D)
        rd = [Rps[a], Rps[a + 1]]
        if kind == 2:
            kw["bias"] = fbias
            rd.append(Rl)
        S.op("act", ("activation", A(out=P12[pb][:, :, 0:w], in_=pspair(a)[:, :, 0:w], **kw)), reads=rd, writes=[RP12[pb]])
        if kind == 1:
            for k_ in range(2):
                S.op("dve", ("tensor_tensor", A(out=P12[pb][:, k_, 0:128], in0=P12[pb][:, k_, 0:128], in1=cm[:, 0, :], op=ALU.mult)), reads=[RP12[pb], Rc], writes=[RP12[pb]])

    def diff_PV(i):
        g, qi, kt, kind, sub, first, last = dblocks[i]
        b = g % 2
        q0 = 128 * sub
        w = T - q0
        pb = i % 3
        st, sp_ = first, last
        mm(ps[4][:, q0:T], Vd[b][:, kt, :], P12[pb][:, 0, 0:w], st, sp_, [RVd[b], RP12[pb]], [Rps[4]], sp_)
        mm(ps[6][:, q0:T], ones_bf[:, :], P12[pb][:, 0, 0:w], st, sp_, [Rc, RP12[pb]], [Rps[6]], sp_)
        mm(ps[5][:, q0:T], Vd[b][:, kt, :], P12[pb][:, 1, 0:w], st, sp_, [RVd[b], RP12[pb]], [Rps[5]], sp_)
        mm(ps[7][:, q0:T], ones_bf[:, :], P12[pb][:, 1, 0:w], st, sp_, [Rc, RP12[pb]], [Rps[7]], True)
        if sp_:
            f = dfin[0] % 2
            dfin[0] += 1
            S.op("dve", ("tensor_copy", A(out=u1[:], in_=ps[4][:])), reads=[Rps[4]], writes=[Ru1])
            S.op("dve", ("tensor_copy", A(out=u2[:], in_=ps[5][:])), reads=[Rps[5]], writes=[Ru2])
            S.op("act", ("activation", A(out=r1[:], in_=ps[6][:], func=AF.Ln)), reads=[Rps[6]], writes=[Rr1])
            S.op("act", ("activation", A(out=r2[:], in_=ps[7][:], func=AF.Ln)), reads=[Rps[7]], writes=[Rr2])
            S.op("act", ("activation", A(out=r1[:], in_=r1[:], func=AF.Exp, scale=-1.0)), reads=[Rr1], writes=[Rr1])
            S.op("act", ("activation", A(out=r2[:], in_=r2[:], func=AF.Exp, scale=-1.0)), reads=[Rr2], writes=[Rr2])
            S.op("dve", ("tensor_tensor", A(out=u1[:], in0=u1[:], in1=r1[:], op=ALU.mult)), reads=[Rr1], writes=[Ru1])
            S.op("dve", ("tensor_tensor", A(out=u2[:], in0=u2[:], in1=r2[:], op=ALU.mult)), reads=[Rr2], writes=[Ru2])
            S.op("dve", ("scalar_tensor_tensor", A(out=dd[:], in0=u2[:], scalar=neglam, in1=u1[:], op0=ALU.mult, op1=ALU.add)), reads=[Ru1, Ru2, Rl], writes=[Rdd])
            S.op("pool", ("tensor_tensor", A(out=sqd[:], in0=dd[:], in1=dd[:], op=ALU.mult)), reads=[Rdd], writes=[Rsqd])

            def fin_pe(f=f, g=g, qi=qi):
                mm(ps[6][:], ones_bf[:, :], sqd[:], True, True, [Rsqd, Rc], [Rps[6]], True)
                S.op("act", ("activation", A(out=rsd[:], in_=ps[6][:], func=AF.Ln, bias=EPS, scale=1.0 / 128.0)), reads=[Rps[6]], writes=[Rrsd])
                S.op("act", ("activation", A(out=rsd[:], in_=rsd[:], func=AF.Exp, scale=-0.5)), reads=[Rrsd], writes=[Rrsd])
                S.op("dve", ("scalar_tensor_tensor", A(out=ysd[f][:], in0=dd[:], scalar=sgain, in1=rsd[:], op0=ALU.mult, op1=ALU.mult)), reads=[Rdd, Rrsd, Rl], writes=[Rysd[f]])
                S.dma("pool", ydT[g * 128:(g + 1) * 128, qi * T:(qi + 1) * T], ysd[f][:], reads=[Rysd[f]])
            fin_pe()

    n = len(dblocks)
    for i in range(n + 1):
        if i < n:
            diff_S(i)
        if i - 1 >= 0:
            diff_PV(i - 1)
        if i < n and dblocks[i][1] == 0 and dblocks[i][5] and dblocks[i][0] + 1 < 4:
            diff_prologue(dblocks[i][0] + 1)
    S.barrier()

    sb.off = base_off
    WgM = sb.alloc([128, 8, D], BF16); WgD = sb.alloc([128, 8, D], BF16)
    Wpm = sb.alloc([128, 4, D], BF16); Wpd = sb.alloc([128, 4, D], BF16); Wo = sb.alloc([128, 8, D], BF16)
    RW1 = Reg(True)
    winr = win.rearrange("(c p) f -> p c f", p=128)
    load_w(WgM, winr[:, :, C_GM:C_GM + D], RW1, 1, D, 3)
    load_w(WgD, winr[:, :, C_GD:C_GD + D], RW1, 1, D, 3)
    load_w(Wpm, wpm.rearrange("(c p) f -> p c f", p=128), RW1, 1, D, 3)
    load_w(Wpd, wpd.rearrange("(c p) f -> p c f", p=128), RW1, 1, D, 3)
    load_w(Wo, wout.rearrange("(c p) f -> p c f", p=128), RW1, 1, D, 3)
    xb = [sb.alloc([128, 8, T], F32) for _ in range(2)]
    ym = [sb.alloc([128, 4, T], BF16) for _ in range(2)]
    yd = [sb.alloc([128, 4, T], BF16) for _ in range(2)]
    h2 = sb.alloc([128, 8, T], BF16)
    rstd = sb.alloc([128, T], F32)
    sg1 = sb.alloc([128, T], F32); sg2 = sb.alloc([128, T], F32)
    m1 = sb.alloc([128, T], F32); m2 = sb.alloc([128, T], F32)
    mg = sb.alloc([128, 8, T], BF16)
    Rx = [Reg(), Reg()]; Rym = [Reg(), Reg()]; Ryd = [Reg(), Reg()]; Rh2 = Reg(); Rr = Reg()
    Rsg1, Rsg2, Rm1, Rm2 = Reg(), Reg(), Reg(), Reg()
    Rmg = [Reg() for _ in range(8)]

    def load_d(qi):
        t = own[qi]
        S.dma("sp", xb[qi % 2][:], x1o[qi].rearrange("(c p) t -> p c t", p=128), writes=[Rx[qi % 2]])
        S.dma("sp", ym[qi % 2][:], ymT[:, qi * T:(qi + 1) * T].rearrange("(c p) t -> p c t", p=128), writes=[Rym[qi % 2]])
        S.dma("sp", yd[qi % 2][:], ydT[:, qi * T:(qi + 1) * T].rearrange("(c p) t -> p c t", p=128), writes=[Ryd[qi % 2]])

    load_d(0)
    h2d = [h2, sb.alloc([128, 8, T], BF16)]
    Rh2d = [Rh2, Reg()]
    sgs = [sg1, sg2, sb.alloc([128, T], F32), sb.alloc([128, T], F32)]
    Rsgs = [Rsg1, Rsg2, Reg(), Reg()]
    mms = [m1, m2, sb.alloc([128, T], F32), sb.alloc([128, T], F32)]
    Rmms = [Rm1, Rm2, Reg(), Reg()]
    bank_i = [0]

    def nbank():
        b_ = 1 + bank_i[0] % 6
        bank_i[0] += 1
        return b_

    def norm_d(qi):
        x_ = xb[qi % 2]
        hh_ = h2d[qi % 2]
        rmsnorm(lambda c: x_[:, c, :], Rx[qi % 2], 8, D, G_MIX, lambda c: hh_[:, c, :], Rh2d[qi % 2], lambda c: hh_[:, c, :], Rh2d[qi % 2], 0, rstd, Rr)

    norm_d(0)
    for qi in range(NO):
        x = xb[qi % 2]; Rxc = Rx[qi % 2]
        hh = h2d[qi % 2]; Rhh = Rh2d[qi % 2]
        if qi + 1 < NO:
            load_d(qi + 1)
        for d in range(8):
            dsl = slice(d * 128, (d + 1) * 128)
            o_ = (d % 2) * 2
            bg1, bp1, bg2, bp2 = (nbank(), nbank(), nbank(), nbank()) if KF[2] == "1" else (1, 3, 2, 4)
            for k in range(8):
                mm(ps[bg1][:], WgM[:, k, dsl], hh[:, k, :], k == 0, k == 7, [RW1, Rhh], [Rps[bg1]], k == 7)
            S.op("act", ("activation", A(out=sgs[o_][:], in_=ps[bg1][:], func=AF.Sigmoid)), reads=[Rps[bg1]], writes=[Rsgs[o_]])
            for k in range(4):
                mm(ps[bp1][:], Wpm[:, k, dsl], ym[qi % 2][:, k, :], k == 0, k == 3, [RW1, Rym[qi % 2]], [Rps[bp1]], k == 3)
            S.op("dve", ("tensor_tensor", A(out=mms[o_][:], in0=ps[bp1][:], in1=sgs[o_][:], op=ALU.mult)), reads=[Rps[bp1], Rsgs[o_]], writes=[Rmms[o_]])
            for k in range(8):
                mm(ps[bg2][:], WgD[:, k, dsl], hh[:, k, :], k == 0, k == 7, [RW1, Rhh], [Rps[bg2]], k == 7)
            S.op("act", ("activation", A(out=sgs[o_ + 1][:], in_=ps[bg2][:], func=AF.Sigmoid)), reads=[Rps[bg2]], writes=[Rsgs[o_ + 1]])
            for k in range(4):
                mm(ps[bp2][:], Wpd[:, k, dsl], yd[qi % 2][:, k, :], k == 0, k == 3, [RW1, Ryd[qi % 2]], [Rps[bp2]], k == 3)
            S.op("dve", ("tensor_tensor", A(out=mms[o_ + 1][:], in0=ps[bp2][:], in1=sgs[o_ + 1][:], op=ALU.mult)), reads=[Rps[bp2], Rsgs[o_ + 1]], writes=[Rmms[o_ + 1]])
            S.op("pool", ("tensor_tensor", A(out=mg[:, d, :], in0=mms[o_][:], in1=mms[o_ + 1][:], op=ALU.add)), reads=[Rmms[o_], Rmms[o_ + 1]], writes=[Rmg[d]])
        if qi + 1 < NO:
            norm_d(qi + 1)
        for d in range(8):
            po = nbank() if KF[2] == "1" else 5 + d % 2
            for k in range(8):
                mm(ps[po][:], Wo[:, k, d * 128:(d + 1) * 128], mg[:, k, :], k == 0, k == 7, [RW1, Rmg[k]], [Rps[po]], k == 7)
            S.op("dve", ("tensor_tensor", A(out=x[:, d, :], in0=ps[po][:], in1=x[:, d, :], op=ALU.add)), reads=[Rps[po]], writes=[Rxc])
        S.dma("pool", x2T[:, qi * T:(qi + 1) * T].rearrange("(c p) t -> p c t", p=128), x[:], reads=[Rxc])
    S.barrier()

    ffn_phase(lambda t: x2T[:, t * T:(t + 1) * T], NO, w2g, w2u, w2d, G_FFN2, lambda t: outT[:, t * T:(t + 1) * T], lambda t: [], True)
    S.emit()
    return nc


def const_tables():
    tri = np.zeros((128, 128), np.float32)
    for p in range(128):
        tri[p, p:] = 1.0
    pmM = np.zeros((128, 128), np.float32)
    for j in range(16):
        pmM[64 + j + 16, 64 + j] = -1.0
        pmM[64 + j, 64 + j + 16] = 1.0
    pmD = np.zeros((128, 128), np.float32)
    for b in (0, 64):
        for j in range(8):
            pmD[b + j + 8, b + j] = -1.0
            pmD[b + j, b + j + 8] = 1.0
    cmat = np.stack([tri, pmM, pmD], axis=1)
    fsc = np.zeros((128, 2), np.float64)
    for j in range(32):
        fsc[64 + j, 0] = THETA ** (-(j % 16) / 16.0) / (2 * math.pi)
    for b in (0, 64):
        for j in range(16):
            fsc[b + j, 1] = THETA ** (-(j % 8) / 8.0) / (2 * math.pi)
    return np.ascontiguousarray(cmat), fsc.astype(np.float32)


def col8(v, n):
    return np.ascontiguousarray(np.asarray(v, np.float32).reshape(n, 128).T)


def prep_shared(inp):
    f = lambda k: np.ascontiguousarray(np.asarray(inp[k], np.float32)[0])
    cmat, fsc = const_tables()
    gv = np.zeros((128, 36), np.float32)
    gv[:, 0:8] = col8(inp["ffn1_norm"][0], 8)
    gv[:, 8:16] = col8(inp["mix_norm"][0], 8)
    gv[:, 16:24] = col8(inp["ffn2_norm"][0], 8)
    gv[:, 24:32] = col8(inp["final_norm"], 8)
    gv[:, 32:34] = col8(inp["mla_q_norm"][0], 2)
    gv[:, 34:35] = col8(inp["mla_kv_norm"][0], 1)
    gv[:, 35:36] = col8(inp["diff_subln"][0], 1)
    lam = np.concatenate([np.asarray(inp[k], np.float32)[0] for k in ("diff_lam_q1", "diff_lam_k1", "diff_lam_q2", "diff_lam_k2")])
    lamr = np.ascontiguousarray(np.broadcast_to(lam[None, :], (128, 256)))
    return {
        "gv": gv, "lamr": lamr, "fsc": fsc, "cmat": cmat,
        "w1g": f("ffn1_w_gate"), "w1u": f("ffn1_w_up"), "w1d": f("ffn1_w_down"),
        "w2g": f("ffn2_w_gate"), "w2u": f("ffn2_w_up"), "w2d": f("ffn2_w_down"),
        "win": f("w_in"), "wuq": f("mla_w_uq"), "wukv": f("mla_w_ukv"),
        "wpm": f("w_proj_mla"), "wpd": f("w_proj_diff"), "wout": f("w_out"),
    }


def core_perm(half, NT):
    NO = NT // 2
    own = [2 * i + half for i in range(NO)]
    oth = [2 * i + 1 - half for i in range(NO)]
    return own, oth


def make_in_maps(inputs, n_cores):
    x = np.asarray(inputs["x"], np.float32)
    pos = np.asarray(inputs["positions"], np.int32)
    B, S_LEN, _ = x.shape
    NT = S_LEN // T
    shared = prep_shared(inputs)
    in_maps, owns = [], []
    for core in range(n_cores):
        b, half = core // 2, core % 2
        own, oth = core_perm(half, NT)
        idx = np.concatenate([np.arange(c * T, (c + 1) * T) for c in own + oth])
        m = dict(shared)
        m["xT"] = np.ascontiguousarray(x[b][idx].T)
        m["posr"] = np.ascontiguousarray(np.broadcast_to(pos[b][idx][None, :], (128, S_LEN)))
        m["flag"] = np.full((128, 1), float(half), np.float32)
        in_maps.append(m)
        owns.append(own)
    return in_maps, owns


def gather_out(results, owns, B, S_LEN):
    out = np.empty((B, S_LEN, D), np.float32)
    for core, r in enumerate(results):
        b = core // 2
        oT = np.asarray(r["outT"])
        for qi, c in enumerate(owns[core]):
            out[b, c * T:(c + 1) * T, :] = oT[:, qi * T:(qi + 1) * T].T
    return out


def kernel(**inputs):
    x = np.asarray(inputs["x"])
    B, S_LEN, _ = x.shape
    in_maps, owns = make_in_maps(inputs, 2 * B)
    nc = build(S_LEN, 2 * B)
    res = run_bass_kernel_spmd(nc, in_maps, core_ids=list(range(2 * B)))
    return gather_out(res.results, owns, B, S_LEN)
```
